# Optimizing a Trainium2 kernel written in Bass

```python
import jax, jax.numpy as jnp
from jax import lax
import numpy as np

D_MODEL = 1024
BATCH = 8
SEQ = 2048
DEPTH = 2
DEC_BATCH = 128
DEC_SEQ = 1
PAST_LEN = 16384
PAGE_SIZE = 128

N_MIXERS = 2
N_GDN_LAYERS = (DEPTH + 1) // 2
N_POOL_LAYERS = DEPTH // 2
GDN_NK = 8
GDN_NV = 16
GDN_DK = 128
GDN_DV = 128
KEY_DIM = GDN_NK * GDN_DK
VALUE_DIM = GDN_NV * GDN_DV
CONV_W = 4
CONV_DIM = 2 * KEY_DIM + VALUE_DIM
IN_DIM = CONV_DIM + VALUE_DIM + 2 * GDN_NV
CHUNK = 64
POOL_WINDOWS = (2, 4, 8, 16)
POOL_GROUPS = len(POOL_WINDOWS)
POOL_G = D_MODEL // POOL_GROUPS
POOL_BUF = max(POOL_WINDOWS) - 1
D_FF = ((8 * D_MODEL + 3 * 256 - 1) // (3 * 256)) * 256
DN_ALPHA = (2 * DEPTH) ** 0.25
DN_BETA = (8 * DEPTH) ** -0.25
LN_EPS = 1e-5
NORM_EPS = 1e-6

kernel_name = "hybrid_gdn_pool_adaln_deepnorm_step"

F32 = jnp.float32


def layer_norm(x, g, b):
    xf = x.astype(F32)
    mu = jnp.mean(xf, -1, keepdims=True)
    var = jnp.mean(jnp.square(xf - mu), -1, keepdims=True)
    return ((xf - mu) * lax.rsqrt(var + LN_EPS) * g.astype(F32) + b.astype(F32)).astype(x.dtype)


def l2norm(x):
    return x * lax.rsqrt(jnp.sum(jnp.square(x), -1, keepdims=True) + NORM_EPS)


def causal_conv(qkv, prev, w):
    L = qkv.shape[1]
    ext = jnp.concatenate([prev.astype(qkv.dtype), qkv], axis=1)
    y = sum(ext[:, j:j + L] * w[j] for j in range(CONV_W))
    return jax.nn.silu(y), ext[:, L:]


def gated_delta_rule(q, k, v, g, beta, S0):
    B, H, L, DK = q.shape
    DV = v.shape[-1]
    C = min(CHUNK, L)
    pad = (-L) % C
    if pad:
        pw = lambda t: jnp.pad(t, [(0, 0), (0, 0), (0, pad)] + [(0, 0)] * (t.ndim - 3))
        q, k, v, g, beta = pw(q), pw(k), pw(v), pw(g), pw(beta)
    N = (L + pad) // C
    q = q.reshape(B, H, N, C, DK)
    k = k.reshape(B, H, N, C, DK)
    v = v.reshape(B, H, N, C, DV)
    beta = beta.reshape(B, H, N, C)
    g = jnp.cumsum(g.reshape(B, H, N, C), axis=-1)
    tri = jnp.tril(jnp.ones((C, C), bool))
    strict = jnp.tril(jnp.ones((C, C), bool), -1)
    decay = jnp.exp(jnp.where(tri, g[..., :, None] - g[..., None, :], -jnp.inf))
    kb = k * beta[..., None]
    A = jnp.where(strict, jnp.einsum('bhncd,bhnsd->bhncs', kb, k) * decay, 0.0)
    lhs = A + jnp.eye(C, dtype=F32)
    rhs = jnp.concatenate([v * beta[..., None], kb * jnp.exp(g)[..., None]], axis=-1)
    sol = lax.linalg.triangular_solve(lhs, rhs, left_side=True, lower=True)
    value, k_cum = sol[..., :DV], sol[..., DV:]
    attn_local = jnp.where(tri, jnp.einsum('bhncd,bhnsd->bhncs', q, k) * decay, 0.0)
    q_dec = q * jnp.exp(g)[..., None]
    k_dec = k * jnp.exp(g[..., -1:] - g)[..., None]
    g_last = jnp.exp(g[..., -1])

    def step(S, inp):
        qd, kc, val, al, kd, gl = inp
        v_new = val - jnp.einsum('bhcd,bhde->bhce', kc, S)
        o = jnp.einsum('bhcd,bhde->bhce', qd, S) + jnp.einsum('bhcs,bhse->bhce', al, v_new)
        S = S * gl[..., None, None] + jnp.einsum('bhcd,bhce->bhde', kd, v_new)
        return S, o

    mv = lambda t: jnp.moveaxis(t, 2, 0)
    S, o = lax.scan(step, S0, (mv(q_dec), mv(k_cum), mv(value), mv(attn_local), mv(k_dec), mv(g_last)))
    o = jnp.moveaxis(o, 0, 2).reshape(B, H, N * C, DV)[:, :, :L]
    return o, S


def gdn_mixer(u, S0, conv0, w_in, conv_w, A_log, dt_bias, norm_w, w_out):
    B, L, _ = u.shape
    proj = jnp.einsum('bld,de->ble', u, w_in)
    qkv = proj[..., :CONV_DIM]
    z = proj[..., CONV_DIM:CONV_DIM + VALUE_DIM]
    b = proj[..., CONV_DIM + VALUE_DIM:CONV_DIM + VALUE_DIM + GDN_NV]
    a = proj[..., CONV_DIM + VALUE_DIM + GDN_NV:]
    y, conv_new = causal_conv(qkv, conv0, conv_w)
    y = y.astype(F32)
    rep = GDN_NV // GDN_NK
    q = l2norm(y[..., :KEY_DIM].reshape(B, L, GDN_NK, GDN_DK))
    k = l2norm(y[..., KEY_DIM:2 * KEY_DIM].reshape(B, L, GDN_NK, GDN_DK))
    q = jnp.repeat(q, rep, axis=2) * (GDN_DK ** -0.5)
    k = jnp.repeat(k, rep, axis=2)
    v = y[..., 2 * KEY_DIM:].reshape(B, L, GDN_NV, GDN_DV)
    beta = jax.nn.sigmoid(b.astype(F32))
    g = -jnp.exp(A_log.astype(F32)) * jax.nn.softplus(a.astype(F32) + dt_bias.astype(F32))
    th = lambda t: jnp.swapaxes(t, 1, 2)
    o, S = gated_delta_rule(th(q), th(k), th(v), th(g), th(beta), S0.astype(F32))
    o = jnp.swapaxes(o, 1, 2)
    o = o * lax.rsqrt(jnp.mean(jnp.square(o), -1, keepdims=True) + NORM_EPS) * norm_w.astype(F32)
    o = o * jax.nn.silu(z.reshape(B, L, GDN_NV, GDN_DV).astype(F32))
    out = jnp.einsum('blv,vd->bld', o.reshape(B, L, VALUE_DIM).astype(u.dtype), w_out)
    return out, S.astype(S0.dtype), conv_new.astype(conv0.dtype)


def pool_mixer(u, prev, pos0, w_pool, pool_scale):
    B, L, D = u.shape
    uf = u.astype(F32)
    ext = jnp.concatenate([prev.astype(F32), uf], axis=1)
    cs = jnp.concatenate([jnp.zeros((B, 1, D), F32), jnp.cumsum(ext, axis=1)], axis=1)
    end = cs[:, POOL_BUF + 1:]
    pos = pos0 + jnp.arange(L)
    means = []
    for gi, w in enumerate(POOL_WINDOWS):
        sl = slice(gi * POOL_G, (gi + 1) * POOL_G)
        start = cs[:, POOL_BUF + 1 - w:POOL_BUF + 1 - w + L, sl]
        cnt = jnp.minimum(pos + 1, w).astype(F32)[None, :, None]
        means.append((end[..., sl] - start) / cnt)
    pooled = jnp.concatenate(means, axis=-1) - uf
    h = jnp.einsum('blgc,gce->blge', pooled.reshape(B, L, POOL_GROUPS, POOL_G), w_pool.astype(F32))
    h = h.reshape(B, L, D) * pool_scale.astype(F32)
    return h.astype(u.dtype), ext[:, L:].astype(prev.dtype)


def swiglu(u, w_up, w_down):
    h = jnp.einsum('bld,df->blf', u, w_up)
    gt, up = h[..., :D_FF], h[..., D_FF:]
    return jnp.einsum('blf,fd->bld', jax.nn.silu(gt) * up, w_down)


def run_group(x, c, gdn_S, gdn_conv, pool_buf, pos0, ada_w, ada_b, ln_g, ln_b,
              gdn_w_in, gdn_conv_w, gdn_A_log, gdn_dt_bias, gdn_norm_w, gdn_w_out,
              pool_w, pool_scale, ffn_w_up, ffn_w_down):
    mod = jnp.einsum('bd,lde->lbe', jax.nn.silu(c), ada_w) + ada_b[:, None, :]
    y = x
    new_S, new_conv, new_pool = [], [], []
    for i in range(DEPTH):
        sh_m, sc_m, ga_m, sh_f, sc_f, ga_f = jnp.split(mod[i][:, None, :], 6, axis=-1)
        u = y * (1 + sc_m) + sh_m
        j = i // N_MIXERS
        if i % N_MIXERS == 0:
            h, S, cv = gdn_mixer(u, gdn_S[j], gdn_conv[j], gdn_w_in[j], gdn_conv_w[j], gdn_A_log[j],
                                 gdn_dt_bias[j], gdn_norm_w[j], gdn_w_out[j])
            new_S.append(S)
            new_conv.append(cv)
        else:
            h, pb = pool_mixer(u, pool_buf[j], pos0, pool_w[j], pool_scale[j])
            new_pool.append(pb)
        y = layer_norm(DN_ALPHA * y + ga_m * h, ln_g[i, 0], ln_b[i, 0])
        u = y * (1 + sc_f) + sh_f
        y = layer_norm(DN_ALPHA * y + ga_f * swiglu(u, ffn_w_up[i], ffn_w_down[i]), ln_g[i, 1], ln_b[i, 1])
    return y, jnp.stack(new_S), jnp.stack(new_conv), jnp.stack(new_pool)


def setup_inputs(seed: int = 0) -> dict:
    key = jax.random.key(seed)
    ks = jax.random.split(key, 24)
    nrm = lambda k, s, sc: jax.random.normal(k, s, F32) * sc
    D = D_MODEL
    ada_b = jnp.concatenate([
        nrm(ks[5], (DEPTH, 2 * D), 0.02),
        1.0 + nrm(ks[6], (DEPTH, D), 0.02),
        nrm(ks[7], (DEPTH, 2 * D), 0.02),
        1.0 + nrm(ks[8], (DEPTH, D), 0.02)], axis=-1)
    w_in = nrm(ks[9], (N_GDN_LAYERS, D, IN_DIM), D ** -0.5)
    v_cols = (jnp.arange(IN_DIM) >= 2 * KEY_DIM) & (jnp.arange(IN_DIM) < CONV_DIM)
    w_in = w_in * jnp.where(v_cols, DN_BETA, 1.0)
    return {
        "x_prompt": nrm(ks[0], (BATCH, SEQ, D), 1.0),
        "x_sample": nrm(ks[1], (DEC_BATCH, DEC_SEQ, D), 1.0),
        "c_prompt": nrm(ks[2], (BATCH, D), 1.0),
        "c_sample": nrm(ks[3], (DEC_BATCH, D), 1.0),
        "state_gdn_S": nrm(ks[4], (N_GDN_LAYERS, DEC_BATCH, GDN_NV, GDN_DK, GDN_DV), 0.1),
        "state_gdn_conv": nrm(ks[10], (N_GDN_LAYERS, DEC_BATCH, CONV_W - 1, CONV_DIM), 1.0),
        "state_pool": nrm(ks[11], (N_POOL_LAYERS, DEC_BATCH, POOL_BUF, D), 1.0),
        "ada_w": nrm(ks[12], (DEPTH, D, 6 * D), 0.1 * D ** -0.5),
        "ada_b": ada_b,
        "ln_g": 1.0 + nrm(ks[13], (DEPTH, 2, D), 0.02),
        "ln_b": nrm(ks[14], (DEPTH, 2, D), 0.02),
        "gdn_w_in": w_in,
        "gdn_conv_w": nrm(ks[15], (N_GDN_LAYERS, CONV_W, CONV_DIM), CONV_W ** -0.5),
        "gdn_A_log": jnp.log(jax.random.uniform(ks[16], (N_GDN_LAYERS, GDN_NV), F32, 1.0, 16.0)),
        "gdn_dt_bias": nrm(ks[17], (N_GDN_LAYERS, GDN_NV), 0.1),
        "gdn_norm_w": 1.0 + nrm(ks[18], (N_GDN_LAYERS, GDN_DV), 0.02),
        "gdn_w_out": nrm(ks[19], (N_GDN_LAYERS, VALUE_DIM, D), DN_BETA * VALUE_DIM ** -0.5),
        "pool_w": nrm(ks[20], (N_POOL_LAYERS, POOL_GROUPS, POOL_G, POOL_G), DN_BETA * POOL_G ** -0.5),
        "pool_scale": 1.0 + nrm(ks[21], (N_POOL_LAYERS, D), 0.02),
        "ffn_w_up": nrm(ks[22], (DEPTH, D, 2 * D_FF), D ** -0.5),
        "ffn_w_down": nrm(ks[23], (DEPTH, D_FF, D), DN_BETA * D_FF ** -0.5),
    }


def reference(x_prompt, x_sample, c_prompt, c_sample, state_gdn_S, state_gdn_conv, state_pool,
              ada_w, ada_b, ln_g, ln_b, gdn_w_in, gdn_conv_w, gdn_A_log, gdn_dt_bias, gdn_norm_w,
              gdn_w_out, pool_w, pool_scale, ffn_w_up, ffn_w_down):
    B = x_prompt.shape[0]
    dt = x_prompt.dtype
    zS = jnp.zeros((N_GDN_LAYERS, B, GDN_NV, GDN_DK, GDN_DV), dt)
    zC = jnp.zeros((N_GDN_LAYERS, B, CONV_W - 1, CONV_DIM), dt)
    zP = jnp.zeros((N_POOL_LAYERS, B, POOL_BUF, D_MODEL), dt)
    y_prompt, p_S, p_conv, p_pool = run_group(
        x_prompt, c_prompt, zS, zC, zP, 0, ada_w, ada_b, ln_g, ln_b, gdn_w_in, gdn_conv_w,
        gdn_A_log, gdn_dt_bias, gdn_norm_w, gdn_w_out, pool_w, pool_scale, ffn_w_up, ffn_w_down)
    y_sample, s_S, s_conv, s_pool = run_group(
        x_sample, c_sample, state_gdn_S, state_gdn_conv, state_pool, PAST_LEN, ada_w, ada_b, ln_g, ln_b,
        gdn_w_in, gdn_conv_w, gdn_A_log, gdn_dt_bias, gdn_norm_w, gdn_w_out, pool_w, pool_scale,
        ffn_w_up, ffn_w_down)
    return (y_prompt, y_sample, p_S, p_conv, p_pool, s_S, s_conv, s_pool)
```

```python
import numpy as np
from contextlib import ExitStack
import concourse.bass as bass
import concourse.mybir as mybir
from concourse.bass_utils import run_bass_kernel_spmd

F32 = mybir.dt.float32
BF16 = mybir.dt.bfloat16
AF = mybir.ActivationFunctionType
ALU = mybir.AluOpType

D = 1024
SEQ = 2048
TB = 512
NBLK = SEQ // TB
NS = 16
KC = 8
DFF = 2816
FC = 22
NH = 16
IN_DIM = 6176
ALPHA = float(4.0 ** 0.25)
LN_EPS = 1e-5
NORM_EPS = 1e-6
NCORES = 8
C_ID, C_TRI, C_SU, C_ONE, C_RC = 0, 128, 256, 384, 512
NCONST = 576
BANKS_RC = (0, 1, 2)
BANKS_PD = (3, 4, 5)
BANKS_PJ = (6, 7)


class Prog:
    ENGS = ("pe", "act", "dve", "pool", "sp")
    WINDOW = 700
    SEM_LAT = 0.12

    def __init__(self, nc):
        self.nc = nc
        self.ops = []
        self.last_w = {}
        self.readers = {}
        self.schedule = True
        self.tag = ""

    def add(self, eng, fn, reads=(), writes=(), dma=None, cost=0.3, tbl=None, xfer=0.0):
        i = len(self.ops)
        deps = set()
        why = {}
        for r in reads:
            j = self.last_w.get(r)
            if j is not None:
                deps.add(j)
                why.setdefault(j, ("RAW", r))
        for w in writes:
            j = self.last_w.get(w)
            if j is not None:
                deps.add(j)
                why.setdefault(j, ("WAW", w))
            for j in self.readers.get(w, ()):
                deps.add(j)
                why.setdefault(j, ("WAR", w))
        self.why = getattr(self, "why", {})
        self.why[i] = why
        for r in reads:
            self.readers.setdefault(r, []).append(i)
        for w in writes:
            self.last_w[w] = i
            self.readers[w] = []
        deps.discard(i)
        self.ops.append(dict(eng=eng, fn=fn, deps=deps, dma=dma, sig=False, cost=cost, tbl=tbl, xfer=xfer, tag=getattr(self, "tag", "")))
        return i

    def _list_schedule(self):
        import heapq
        ops = self.ops
        n = len(ops)
        succ = [[] for _ in range(n)]
        indeg = [0] * n
        for i, o in enumerate(ops):
            indeg[i] = len(o["deps"])
            for j in o["deps"]:
                succ[j].append(i)
        prio = [0.0] * n
        for i in range(n - 1, -1, -1):
            m = 0.0
            for s in succ[i]:
                if prio[s] > m:
                    m = prio[s]
            prio[i] = m + ops[i]["cost"] + ops[i]["xfer"]
        fin = [0.0] * n
        rt = [0.0] * n
        free = {e: 0.0 for e in self.ENGS}
        ready = {e: [] for e in self.ENGS}
        pending = {e: [] for e in self.ENGS}
        scheduled = [False] * n
        order = {e: [] for e in self.ENGS}
        cur_tbl = [None]
        dma_free = [0.0]
        low = 0
        for i in range(n):
            if indeg[i] == 0:
                heapq.heappush(pending[ops[i]["eng"]], i)
        done = 0
        while done < n:
            while low < n and scheduled[low]:
                low += 1
            lim = low + self.WINDOW
            for e in self.ENGS:
                pe_ = pending[e]
                while pe_ and pe_[0] < lim:
                    ready[e].append(heapq.heappop(pe_))
            best = None
            for e in self.ENGS:
                rl = ready[e]
                if not rl:
                    continue
                tf = free[e]
                bi, bstart, bscore = None, None, None
                for i in rl:
                    stt_ = rt[i] if rt[i] > tf else tf
                    pen = 0.0
                    if e == "act" and ops[i]["tbl"] is not None and ops[i]["tbl"] != cur_tbl[0]:
                        pen = 1.3
                    key = (stt_ + pen, -prio[i], i)
                    if bscore is None or key < bscore:
                        bi, bstart, bscore = i, stt_ + pen, key
                if best is None or (bstart, bi) < (best[0], best[2]):
                    best = (bstart, e, bi)
            if best is None:
                raise RuntimeError("scheduler stuck")
            start, e, i = best
            o = ops[i]
            ready[e].remove(i)
            if e == "act" and o["tbl"] is not None:
                cur_tbl[0] = o["tbl"]
            end = start + o["cost"]
            free[e] = end
            if o["dma"] is not None:
                ds = end if end > dma_free[0] else dma_free[0]
                dma_free[0] = ds + o["xfer"]
                fin[i] = dma_free[0] + 1.5
            else:
                fin[i] = end
            scheduled[i] = True
            order[e].append(i)
            done += 1
            for s in succ[i]:
                lat = 0.0 if (ops[s]["eng"] == "pe" and e == "pe" and o["dma"] is None) else self.SEM_LAT
                t = fin[i] + lat
                if t > rt[s]:
                    rt[s] = t
                indeg[s] -= 1
                if indeg[s] == 0:
                    heapq.heappush(pending[ops[s]["eng"]], s)
        self.sim_end = max(free.values())
        self.fin = fin
        return order

    def emit(self, sems, dma_sems):
        ops = self.ops
        if self.schedule:
            per_eng = self._list_schedule()
        else:
            per_eng = {e: [] for e in self.ENGS}
            for i, o in enumerate(ops):
                per_eng[o["eng"]].append(i)

        def is_pe(o):
            return o["eng"] == "pe" and o["dma"] is None
        for o in ops:
            o["wdeps"] = {j for j in o["deps"] if not (is_pe(o) and is_pe(ops[j]))}
            for j in o["wdeps"]:
                ops[j]["sig"] = True
        cnt = {}
        for e in self.ENGS:
            for i in per_eng[e]:
                o = ops[i]
                if o["dma"] is not None:
                    key = ("dma", o["dma"])
                    cnt[key] = cnt.get(key, 0) + 16
                    o["ticket"] = (key, cnt[key])
                elif o["sig"]:
                    key = ("eng", o["eng"])
                    cnt[key] = cnt.get(key, 0) + 1
                    o["ticket"] = (key, cnt[key])

        def semof(key):
            return dma_sems[key[1]] if key[0] == "dma" else sems[key[1]]

        nc = self.nc
        with nc.Block() as block:
            def body(eng_name):
                def f(eng):
                    known = {}
                    for i in per_eng[eng_name]:
                        o = ops[i]
                        need = {}
                        for j in o["wdeps"]:
                            key, val = ops[j]["ticket"]
                            if val > need.get(key, 0):
                                need[key] = val
                        for key, val in need.items():
                            if known.get(key, 0) >= val:
                                continue
                            eng.wait_ge(semof(key), val)
                            known[key] = val
                        ins = o["fn"](eng)
                        if o["dma"] is not None:
                            ins.then_inc(dma_sems[o["dma"]], 16)
                        elif o["sig"]:
                            ins.then_inc(sems[eng_name], 1)
                    if eng_name == "sp":
                        for key, val in cnt.items():
                            if key[0] == "dma":
                                eng.wait_ge(semof(key), val)
                return f

            block.tensor(body("pe"))
            block.scalar(body("act"))
            block.vector(body("dve"))
            block.gpsimd(body("pool"))
            block.sync(body("sp"))


def build_program():
    nc = bass.Bass("TRN2", target_bir_lowering=False)
    es = ExitStack()

    def din(name, shape, dt=F32):
        return nc.dram_tensor(name, list(shape), dt, kind="ExternalInput").ap()

    def dout(name, shape, dt=F32):
        return nc.dram_tensor(name, list(shape), dt, kind="ExternalOutput").ap()

    xT = din("xT", [D, SEQ])
    xsT = din("xsT", [D, NS])
    cT = din("cT", [D, 1 + NS])
    S_in = din("S_in", [NS, NH, 128, 128])
    sconv_nat = din("sconv_nat", [NS, 3, 4096])
    sconvT = din("sconvT", [128, 32 * 3 * NS])
    spool_nat = din("spool_nat", [NS, 15, D])
    spoolT = din("spoolT", [128, KC * NS * 15])
    ada_w = din("ada_w", [2, D, 6 * D])
    ada_bT = din("ada_bT", [128, 2 * 48])
    ln_gT = din("ln_gT", [128, 32])
    ln_bT = din("ln_bT", [128, 32])
    w_in = din("w_in", [D, IN_DIM])
    convwT = din("convwT", [128, 32 * 4])
    alog_b = din("alog_b", [128, NH])
    dtb_b = din("dtb_b", [128, NH])
    normw_c = din("normw_c", [128, 1])
    w_out = din("w_out", [2048, D])
    pool_w = din("pool_w", [4, 256, 256])
    pscaleT = din("pscaleT", [128, KC])
    w_up = din("w_up", [2, D, 2 * DFF])
    w_down = din("w_down", [2, DFF, D])
    consts_d = din("consts", [128, NCONST])

    yT_o = dout("yT", [D, SEQ])
    ysT_o = dout("ysT", [D, NS])
    pS_o = dout("pS", [NH, 128, 128])
    pconv_o = dout("pconvT", [128, 96])
    ppool_o = dout("ppoolT", [128, KC * 15])
    sS_o = dout("sS", [NS, NH, 128, 128])
    sconv_shift_o = dout("sconv_shift", [NS, 2, 4096])
    sconv_new_o = dout("sconv_newT", [128, 32 * NS])
    spool_shift_o = dout("spool_shift", [NS, 14, D])
    spool_new_o = dout("spool_newT", [128, KC * NS])

    def sb(name, shape, dt=F32):
        return es.enter_context(nc.sbuf_tensor(name, list(shape), dt))

    def psum(name, shape, dt=F32):
        return es.enter_context(nc.psum_tensor(name, list(shape), dt))

    y = sb("y", [128, KC, SEQ])
    ys = sb("ys", [128, KC, NS])
    cst = sb("cst", [128, NCONST])
    id_bf = sb("id_bf", [128, 128], BF16)
    one_bf = sb("one_bf", [128, 128], BF16)
    mod = sb("mod", [128, 2, 6, KC, 1 + NS])
    csil = sb("csil", [128, KC, 1 + NS], BF16)
    lng = sb("lng", [128, 2, 2, KC])
    lnb = sb("lnb", [128, 2, 2, KC])
    lngA = sb("lngA", [128, 2, 2, KC])
    lnbA = sb("lnbA", [128, 2, 2, KC])
    epst = sb("epst", [128, 2])
    convw = sb("convw", [128, 32, 4])
    ccarry = sb("ccarry", [128, 32, 3])
    pcarry = sb("pcarry", [128, KC, 15])
    wba = sb("wba", [128, KC, 32], BF16)
    alog = sb("alog", [128, NH])
    dtb = sb("dtb", [128, NH])
    nexpA = sb("nexpA", [128, NH])
    normw = sb("normw", [128, 1])
    pscale = sb("pscale", [128, KC])
    cgate = sb("cgate", [128, KC, 1 + NS])
    S = sb("S", [128, NH, 128])
    Sb = sb("Sb", [128, NH, 128], BF16)
    ws = [sb(f"ws{i}", [128, 4096], BF16) for i in range(2)]
    u = sb("u", [128, KC, TB], BF16)
    us = sb("us", [128, KC, NS], BF16)
    uf = sb("uf", [128, KC, TB], BF16)
    big = sb("big", [128, FC, TB], BF16)
    bigs = sb("bigs", [128, FC, NS], BF16)
    Qt = sb("Qt", [128, 2, TB], BF16)
    Kt = sb("Kt", [128, 2, TB], BF16)
    Vt = sb("Vt", [128, 4, TB], BF16)
    sz = sb("sz", [128, 4, TB], BF16)
    Ktok = sb("Ktok", [128, 4, 2, 128], BF16)
    Vtok = sb("Vtok", [128, 4, 4, 128], BF16)
    beta = sb("beta", [128, 4, NH])
    nbeta = sb("nbeta", [128, 4, NH])
    gtok = sb("gtok", [128, 4, NH])
    ctmp = sb("ctmp", [128, 8, NH])
    FW = 528
    F_all = sb("F_all", [128, 9, FW])
    H_all = sb("H_all", [128, 23, 512], BF16)
    Fflat = F_all[:].rearrange("p a b -> p (a b)")
    Hf = H_all[:].rearrange("p a b -> p (a b)").bitcast(F32)

    class _T:
        def __init__(self, ap):
            self.ap = ap
        def __getitem__(self, k):
            return self.ap[k]

    Fs = [_T(F_all[:, i, 0:512]) for i in range(9)]
    Fw = [_T(F_all[:, i, :]) for i in range(9)]
    Hs = [_T(H_all[:, i, :]) for i in range(23)]
    pre = Fw[4]
    craw = F_all[:, 7, 0:KC * (1 + NS)].rearrange("p (a b) -> p a b", a=KC)
    adab = F_all[:, 8, 0:96].rearrange("p (a b) -> p a b", a=2)
    u1s = F_all[:, 5, 0:KC * NS].rearrange("p (a b) -> p a b", a=KC)
    pls = F_all[:, 6, 0:KC * NS].rearrange("p (a b) -> p a b", a=KC)
    u1 = Fw[3]
    pres = sb("pres", [128, 48, NS])
    sconv = Fflat[:, 0:1536].rearrange("p (a b c) -> p a b c", a=32, b=3)
    spool = Fflat[:, 0:1920].rearrange("p (a b c) -> p a b c", a=KC, b=NS)
    vtoks = Fflat[0:NS, 0:2048].rearrange("p (a b) -> p a b", a=NH)
    vmask = Fflat[0:NS, 4 * FW:4 * FW + 2048].rearrange("p (a b) -> p a b", a=NH)
    ktoks = Hf[0:NS, 0:1024].rearrange("p (a b) -> p a b", a=8)
    sexp = Hf[0:NS, 1024:1536].rearrange("p (q a b) -> p q a b", q=2, a=NS)
    kqs = Hf[:, 1536:2048].rearrange("p (b h t) -> p b h t", b=NS, h=NH)
    qkvs = Hf[:, 2048:2560].rearrange("p (a b) -> p a b", a=32)
    szs = Hf[:, 2560:2816].rearrange("p (a b) -> p a b", a=16)
    bas = Hf[0:NS, 2816:2848]
    KF03 = tuple(("F", i) for i in range(4))
    KF47 = tuple(("F", i) for i in range(4, 8))
    K_sconv = tuple(("F", i) for i in range(3))
    K_ktoks = tuple(("H", i) for i in range(4))
    K_sexp = (("H", 4), ("H", 5))
    K_kqs = (("H", 6), ("H", 7))
    K_qkvs = (("H", 8), ("H", 9))
    K_szs = (("H", 10),)
    K_bas = (("H", 11),)
    sm = u[:].rearrange("p a b -> p (a b)").bitcast(F32).rearrange("p (a b) -> p a b", a=8)

    PS = [psum(f"ps{i}", [128, 512]) for i in range(8)]
    PSB = [_T(PS[i][:].bitcast(BF16)) for i in range(8)]

    sems = {e: es.enter_context(nc.semaphore(f"s_{e}")) for e in Prog.ENGS}
    dsem = {}

    def stream(name):
        if name not in dsem:
            dsem[name] = es.enter_context(nc.semaphore(f"d_{name}"))
        return name

    P = Prog(nc)
    st = dict(ps=0, pb=0, wsl=0, c=0)

    def _next_bank():
        bs = st.get("bankset")
        if bs == "proj":
            i = BANKS_PJ[st.setdefault("pj", 0) % len(BANKS_PJ)]
            st["pj"] += 1
        elif bs == "sample":
            i = (0, 1, 2)[st.setdefault("sm_", 0) % 3]
            st["sm_"] += 1
        elif bs == "main5":
            i = (3, 4, 5, 6, 7)[st.setdefault("m5", 0) % 5]
            st["m5"] += 1
        else:
            i = st["ps"] % 8
            st["ps"] += 1
        return i

    def bank():
        i = _next_bank()
        return PS[i], ("ps", i)

    def bbank():
        i = _next_bank()
        return PSB[i], ("ps", i)

    def _fs(ap):
        n = 1
        for d in ap.shape[1:]:
            n *= d
        return n

    _TBL = {AF.Silu: "silu", AF.Exp: "explog", AF.Ln: "explog"}

    def act(out, in_, func, r, w, bias=0.0, scale=1.0):
        P.add("act", lambda e: e.activation(out=out, in_=in_, func=func, bias=bias, scale=scale), r, w,
              cost=0.12 + _fs(in_) * 0.00112, tbl=_TBL.get(func))

    def tt(eng, out, in0, in1, op, r, w):
        c = (0.12 + _fs(in0) * 0.00115) if eng == "dve" else (0.2 + _fs(in0) * 0.0021)
        P.add(eng, lambda e: e.tensor_tensor(out=out, in0=in0, in1=in1, op=op), r, w, cost=c)

    def tsc(eng, out, in0, s1, s2, op0, op1, r, w):
        c = (0.12 + _fs(in0) * 0.00115) if eng == "dve" else (0.2 + _fs(in0) * 0.004)
        P.add(eng, lambda e: e.tensor_scalar(out=out, in0=in0, scalar1=s1, scalar2=s2, op0=op0, op1=op1), r, w, cost=c)

    def stt(out, in0, scalar, in1, op0, op1, r, w):
        P.add("dve", lambda e: e.scalar_tensor_tensor(out=out, in0=in0, scalar=scalar, in1=in1, op0=op0, op1=op1), r, w,
              cost=0.12 + _fs(in0) * 0.00125)

    def cp(eng, out, in_, r, w):
        if eng == "act":
            P.add("act", lambda e: e.activation(out=out, in_=in_, func=AF.Copy), r, w, cost=0.12 + _fs(in_) * 0.00112)
        else:
            c = (0.12 + _fs(in_) * 0.00115) if eng == "dve" else (0.2 + _fs(in_) * 0.0021)
            P.add(eng, lambda e: e.tensor_copy(out=out, in_=in_), r, w, cost=c)

    def mm(out, lhsT, rhs, r, w, start=True, stop=True):
        nfree = max(_fs(rhs), 64)
        c = 0.03 + nfree * (4 if rhs.dtype == F32 else 1) / 2400.0
        P.add("pe", lambda e: e.matmul(out, lhsT, rhs, start=start, stop=stop), r, w, cost=c)

    def tr(out, in_, ident, r, w):
        P.add("pe", lambda e: e.transpose(out, in_, ident), r, w, cost=0.1)

    def dma(eng, out, in_, strm, r, w):
        nbytes = 128 * _fs(in_) * 4 if in_.shape[0] == 128 else in_.shape[0] * _fs(in_) * 4
        P.add(eng, lambda e: e.dma_start(out=out, in_=in_), r, w, dma=stream(strm),
              cost=(1.7 if eng == "pool" else 0.15), xfer=nbytes / 330e3)

    def setup_dma(out, in_, w):
        st["c"] += 1
        dma("sp", out, in_, f"c{st['c']}", (), w)

    def memset(eng, ap, val, w):
        P.add(eng, lambda e: e.memset(ap, val), (), w)

    IDf = cst[:, C_ID:C_ID + 128]
    TRI = cst[:, C_TRI:C_TRI + 128]
    SU = cst[:, C_SU:C_SU + 128]
    ONEf = cst[:, C_ONE:C_ONE + 128]
    KCST = ("cst",)

    def b4(ap2d):
        return ap2d.unsqueeze(1).broadcast_to([128, 4, 128])

    def v4(t):
        return t[:].rearrange("p (a b) -> p a b", a=4)

    def hsl_(hh):
        return slice(hh * 128, (hh + 1) * 128)

    def wload(parts):
        s = st["wsl"] % 2
        st["wsl"] += 1
        t = ws[s]
        keys = []
        for pi, (dst_fn, src) in enumerate(parts):
            key = ("ws", s, pi)
            keys.append(key)
            dma("pool", dst_fn(t), src, f"ws{s}_{pi}", (), (key,))
        for pi in range(len(parts), 2):
            keys.append(("ws", s, pi))
        return t, tuple(keys)

    def wv(t, kc, cols, off=0):
        return t[:, off:off + kc * cols].rearrange("p (k c) -> p k c", k=kc)

    setup_dma(cst[:], consts_d, (KCST,))
    setup_dma(craw[:], cT.rearrange("(k p) n -> p k n", p=128), (("F", 7),))
    setup_dma(adab[:].rearrange("p a b -> p (a b)"), ada_bT, (("F", 8),))
    setup_dma(lng[:].rearrange("p a b c -> p (a b c)"), ln_gT, (("lng",),))
    setup_dma(lnb[:].rearrange("p a b c -> p (a b c)"), ln_bT, (("lnb",),))
    setup_dma(convw[:].rearrange("p a b -> p (a b)"), convwT, (("convw",),))
    setup_dma(alog[:], alog_b, (("alog",),))
    setup_dma(dtb[:], dtb_b, (("dtb",),))
    setup_dma(normw[:], normw_c, (("normw",),))
    setup_dma(pscale[:], pscaleT, (("pscale",),))
    def load_x_block(b):
        for k in range(KC):
            dma("sp", y[:, k, b * TB:(b + 1) * TB], xT[k * 128:(k + 1) * 128, b * TB:(b + 1) * TB], f"x{k}", (), (("y", k, b),))
        for k in range(KC):
            tsc("pool", y[:, k, b * TB:(b + 1) * TB], y[:, k, b * TB:(b + 1) * TB], ALPHA, 1.0, ALU.mult, ALU.mult, (("y", k, b),), (("y", k, b),))
    load_x_block(0)
    setup_dma(ys[:], xsT.rearrange("(k p) n -> p k n", p=128), tuple(("ys", k) for k in range(KC)))
    dma("pool", wba[:], w_in.rearrange("(k p) n -> p k n", p=128)[:, :, 6144:6176], "misc", (), (("wba",),))
    dma("sp", sconv_shift_o, sconv_nat[:, 1:3, :], "out", (), ())
    dma("sp", spool_shift_o, spool_nat[:, 1:15, :], "out", (), ())

    tsc("dve", lngA[:], lng[:], ALPHA, None, ALU.mult, ALU.bypass, (("lng",),), (("lngA",),))
    tsc("dve", lnbA[:], lnb[:], ALPHA, None, ALU.mult, ALU.bypass, (("lnb",),), (("lnbA",),))
    memset("dve", epst[:, 0:1], LN_EPS, (("epst",),))
    memset("dve", epst[:, 1:2], NORM_EPS, (("epst",),))
    tsc("pool", ys[:], ys[:], ALPHA, 1.0, ALU.mult, ALU.mult, tuple(("ys", k) for k in range(KC)), tuple(("ys", k) for k in range(KC)))
    for l_ in range(2):
        for kind_ in (1, 4):
            tsc("dve", adab[:, l_, kind_ * 8:(kind_ + 1) * 8], adab[:, l_, kind_ * 8:(kind_ + 1) * 8], 1.0, 1.0 / ALPHA, ALU.add, ALU.mult, (("F", 8),), (("F", 8),))
    cp("dve", id_bf[:], IDf, (KCST,), (("id_bf",),))
    cp("dve", one_bf[:], ONEf, (KCST,), (("one_bf",),))
    memset("dve", ccarry[:], 0.0, tuple(("ccarry", c) for c in range(32)))
    memset("dve", pcarry[:], 0.0, tuple(("pcarry", k) for k in range(KC)))
    memset("dve", S[:], 0.0, tuple(("S", h) for h in range(NH)))
    memset("dve", Sb[:], 0.0, tuple(("Sb", h) for h in range(NH)))
    act(nexpA[:], alog[:], AF.Exp, (("alog",),), (("nexpA",),))
    tsc("dve", nexpA[:], nexpA[:], -1.0, None, ALU.mult, ALU.bypass, (("nexpA",),), (("nexpA",),))
    act(csil[:], craw[:], AF.Silu, (("F", 7),), (("csil",),))

    P.tag = "mod"

    for l_ in range(2):
        for kind_ in range(6):
            cp("dve", mod[:, l_, kind_, :, :], adab[:, l_, kind_ * 8:(kind_ + 1) * 8].unsqueeze(2).broadcast_to([128, 8, 1 + NS]),
               (("F", 8),), tuple(("mod", l_, kind_, ec) for ec in range(8)))

    def mod_piece(l, pc):
        tag0 = P.tag
        P.tag = "mod"
        awl = ada_w[l].rearrange("(k p) n -> p k n", p=128)
        t, wk = wload([(lambda t: wv(t, KC, 512), awl[:, :, pc * 512:(pc + 1) * 512])])
        wvv = wv(t, KC, 512)
        kind = pc // 2
        bs_ = st.get("bankset")
        st["bankset"] = "all"
        pt, pk = bank()
        st["bankset"] = bs_
        for e4 in range(4):
            for k in range(KC):
                mm(pt[:, e4 * 17:(e4 + 1) * 17], wvv[:, k, e4 * 128:(e4 + 1) * 128], csil[:, k, :],
                   wk + (("csil",),), (pk,), start=(k == 0), stop=(k == KC - 1))
        s_ = (1.0 / ALPHA) if kind in (1, 4) else 1.0
        for e4 in range(4):
            ec = (pc % 2) * 4 + e4
            stt(mod[:, l, kind, ec, :], pt[:, e4 * 17:(e4 + 1) * 17], s_, mod[:, l, kind, ec, :], ALU.mult, ALU.add,
                (pk, ("mod", l, kind, ec)), (("mod", l, kind, ec),))
        P.tag = tag0

    mod_queue = [(0, pc) for pc in range(4, 12)] + [(1, pc) for pc in range(12)]
    for pc in range(4):
        mod_piece(0, pc)

    def mod_emit(n):
        for _ in range(n):
            if mod_queue:
                mod_piece(*mod_queue.pop(0))

    def modkeys(l):
        return tuple(("mod", l, kind, ec) for kind in range(6) for ec in range(8))

    def mod_ap(l, kind, k, samp):
        if samp:
            return mod[:, l, kind, k, 1:1 + NS]
        return mod[:, l, kind, k, 0:1]

    def ykey(k, blk):
        return ("ys", k) if blk == "s" else ("y", k, blk)

    def yap(k, blk):
        return ys[:, k, :] if blk == "s" else y[:, k, blk * TB:(blk + 1) * TB]

    def modulate(l, sub, blk, out_fn, out_key_fn):
        ksc, ksh = (1, 0) if sub == 0 else (4, 3)
        samp = blk == "s"
        for k in range(KC):
            mk = (("mod", l, ksc, k), ("mod", l, ksh, k))
            if samp:
                tt("dve", Fs[7][:, 0:NS], yap(k, blk), mod_ap(l, ksc, k, True), ALU.mult, (ykey(k, blk),) + mk, (("F", 7),))
                tt("dve", out_fn(k), Fs[7][:, 0:NS], mod_ap(l, ksh, k, True), ALU.add, (("F", 7),) + mk, (out_key_fn(k),))
            else:
                act(out_fn(k), yap(k, blk), AF.Identity, (ykey(k, blk),) + mk, (out_key_fn(k),),
                    bias=mod_ap(l, ksh, k, False), scale=mod_ap(l, ksc, k, False))

    def residual(l, sub, blk, k, pt, pk, pool=False):
        kg = 2 if sub == 0 else 5
        samp = blk == "s"
        N = NS if samp else TB
        if pool:
            gate = cgate[:, k, 1:1 + NS] if samp else cgate[:, k, 0:1]
            gk = ("cgate",)
        else:
            gate = mod_ap(l, kg, k, samp)
            gk = ("mod", l, kg, k)
        if samp:
            tt("dve", Fs[7][:, 0:NS], pt[:, 0:NS], gate, ALU.mult, (pk, gk), (("F", 7),))
            tt("dve", yap(k, blk), yap(k, blk), Fs[7][:, 0:NS], ALU.add, (ykey(k, blk), ("F", 7)), (ykey(k, blk),))
        else:
            stt(yap(k, blk), pt[:, 0:N], gate, yap(k, blk), ALU.mult, ALU.add, (pk, gk, ykey(k, blk)), (ykey(k, blk),))

    def layer_norm(l, i, blk):
        P.tag = "ln"
        samp = blk == "s"
        N = NS if samp else TB
        psum_t, psk = bank()
        psq_t, pqk = bank()
        for k in range(KC):
            mm(psum_t[:, 0:N], ONEf, yap(k, blk), (KCST, ykey(k, blk)), (psk,), start=(k == 0), stop=(k == KC - 1))
        for k in range(KC):
            h = Hs[6 + (k % 2)]
            hk = ("H", 6 + (k % 2))
            act(h[:, 0:N], yap(k, blk), AF.Square, (ykey(k, blk),), (hk,))
            mm(psq_t[:, 0:N], one_bf[:], h[:, 0:N], (("one_bf",), hk), (pqk,), start=(k == 0), stop=(k == KC - 1))
        m, msq, var, mr = Fs[0], Fs[1], Fs[2], Fs[3]
        act(m[:, 0:N], psum_t[:, 0:N], AF.Copy, (psk,), (("F", 0),), scale=1.0 / D)
        tt("dve", msq[:, 0:N], m[:, 0:N], m[:, 0:N], ALU.mult, (("F", 0),), (("F", 1),))
        stt(var[:, 0:N], psq_t[:, 0:N], 1.0 / D, msq[:, 0:N], ALU.mult, ALU.subtract, (pqk, ("F", 1)), (("F", 2),))
        act(var[:, 0:N], var[:, 0:N], AF.Ln, (("F", 2), ("epst",)), (("F", 2),), bias=epst[:, 0:1])
        act(var[:, 0:N], var[:, 0:N], AF.Exp, (("F", 2),), (("F", 2),), scale=-0.5)
        tt("dve", mr[:, 0:N], m[:, 0:N], var[:, 0:N], ALU.mult, (("F", 0), ("F", 2)), (("F", 3),))
        for k in range(KC):
            f = Fs[4 + (k % 2)]
            fk = ("F", 4 + (k % 2))
            tt("dve", f[:, 0:N], yap(k, blk), var[:, 0:N], ALU.mult, (ykey(k, blk), ("F", 2)), (fk,))
            tt("dve", f[:, 0:N], f[:, 0:N], mr[:, 0:N], ALU.subtract, (fk, ("F", 3)), (fk,))
            fin = (l == 1 and i == 1)
            gt, bt = (lng, lnb) if fin else (lngA, lnbA)
            act(yap(k, blk), f[:, 0:N], AF.Identity, (fk, ("lng",), ("lnb",), ("lngA",), ("lnbA",)), (ykey(k, blk),),
                bias=bt[:, l, i, k:k + 1], scale=gt[:, l, i, k:k + 1])

    def blkinfo(blk, ffn_in=False):
        samp = blk == "s"
        N = NS if samp else TB
        ub = us if samp else (uf if ffn_in else u)
        uk = (lambda k: ("us", k)) if samp else ((lambda k: ("uf", k)) if ffn_in else (lambda k: ("u", k)))
        bg = bigs if samp else big
        bk = (lambda c: ("bigs", c)) if samp else (lambda c: ("big", c))
        return N, ub, uk, bg, bk

    def open_acc(blks, ndc):
        acc = {}
        for blk in blks:
            if blk == "s":
                acc[blk] = [bank()]
            else:
                acc[blk] = [bank() for _ in range(ndc)]
        return acc

    def acc_mm(acc, blk, dl, lhsT, rhs, r, first, last, st_first):
        if blk == "s":
            pt, pk = acc[blk][0]
            out = pt[:, dl * NS:(dl + 1) * NS]
            P.add("pe", lambda e: e.matmul(out, lhsT, rhs, start=st_first, stop=last, skip_group_check=True), r, (pk,), cost=0.06)
        else:
            pt, pk = acc[blk][dl]
            mm(pt[:, 0:TB], lhsT, rhs, r, (pk,), start=first, stop=last)

    def acc_out(acc, blk, dl):
        if blk == "s":
            pt, pk = acc[blk][0]
            return _T(pt[:, dl * NS:(dl + 1) * NS]), pk
        pt, pk = acc[blk][dl]
        return pt, pk

    def ffn(l, blks):
        P.tag = "ffn"
        wup = w_up[l].rearrange("(k p) n -> p k n", p=128)
        wdn = w_down[l].rearrange("(k p) n -> p k n", p=128)
        for blk in blks:
            N, ub, uk, bg, bk = blkinfo(blk, True)
            modulate(l, 1, blk, lambda k, ub=ub, N=N: ub[:, k, 0:N], uk)
        for pc in range(6):
            nf = 4 if pc < 5 else 2
            cols = nf * 128
            for half in range(2):
                t, wk = wload([(lambda t, cols=cols: wv(t, KC, cols), wup[:, :, half * DFF + pc * 512:half * DFF + pc * 512 + cols])])
                wvv = wv(t, KC, cols)
                for blk in blks:
                    N, ub, uk, bg, bk = blkinfo(blk, True)
                    for f in range(nf):
                        fc = pc * 4 + f
                        pg, pgk = bank()
                        for k in range(KC):
                            mm(pg[:, 0:N], wvv[:, k, f * 128:(f + 1) * 128], ub[:, k, 0:N], wk + (uk(k),), (pgk,), start=(k == 0), stop=(k == KC - 1))
                        if blk == "s":
                            tmp, tk = Fs[4][:, f * NS:(f + 1) * NS], ("F", 4)
                        else:
                            tmp, tk = Fs[f][:, 0:N], ("F", f)
                        if half == 0:
                            act(tmp, pg[:, 0:N], AF.Silu, (pgk,), (tk,))
                        else:
                            tt("dve", bg[:, fc, 0:N], tmp, pg[:, 0:N], ALU.mult, (tk, pgk), (bk(fc),))
        fgs = [(0, 8), (8, 16), (16, 22)]
        for ch in range(2):
            acc = open_acc(blks, 4)
            first_s = True
            for gi, (f0, f1) in enumerate(fgs):
                nfc = f1 - f0
                t, wk = wload([(lambda t, nfc=nfc: wv(t, nfc, 512), wdn[:, f0:f1, ch * 512:(ch + 1) * 512])])
                wvv = wv(t, nfc, 512)
                for blk in blks:
                    N, ub, uk, bg, bk = blkinfo(blk)
                    for dl in range(4):
                        for fc in range(f0, f1):
                            acc_mm(acc, blk, dl, wvv[:, fc - f0, dl * 128:(dl + 1) * 128], bg[:, fc, 0:N], wk + (bk(fc),),
                                   first=(fc == 0), last=(fc == FC - 1), st_first=(blk == "s" and first_s))
                            if blk == "s":
                                first_s = False
            for blk in blks:
                for dl in range(4):
                    pt, pk = acc_out(acc, blk, dl)
                    residual(l, 1, blk, ch * 4 + dl, pt, pk)
        for blk in blks:
            layer_norm(l, 1, blk)

    win = w_in.rearrange("(k p) n -> p k n", p=128)

    CONV_SETS = [((Fw[0], ("F", 0)), (Fs[1], ("F", 1)), (Fs[2], ("F", 2))),
                 ((Fw[3], ("F", 3)), (Fs[4], ("F", 4)), (Fs[5], ("F", 5)))]

    CONV_SET_C = ((Fw[6], ("F", 6)), (Fs[8], ("F", 8)), None)

    def conv_set(v=False):
        if v:
            i = st.setdefault("cvv", 0) % 3
            st["cvv"] += 1
            return [CONV_SET_C, CONV_SETS[0], CONV_SETS[1]][i]
        i = st.setdefault("cv", 0) % 2
        st["cv"] += 1
        return CONV_SETS[i]

    def conv_silu(c, out_ap, out_key, pt, pk, cset):
        (pre_, prek), (acc, acck), _ = cset
        cp("act", pre_[:, 3:3 + TB], pt[:, 0:TB], (pk,), (prek,))
        cp("pool", pre_[:, 0:3], ccarry[:, c, :], (("ccarry", c),), (prek,))
        cp("pool", ccarry[:, c, :], pre_[:, TB:TB + 3], (prek,), (("ccarry", c),))
        tsc("dve", acc[:, 0:TB], pre_[:, 0:TB], convw[:, c, 0:1], None, ALU.mult, ALU.bypass, (prek, ("convw",)), (acck,))
        for j in range(1, 4):
            stt(acc[:, 0:TB], pre_[:, j:j + TB], convw[:, c, j:j + 1], acc[:, 0:TB], ALU.mult, ALU.add,
                (prek, ("convw",), acck), (acck,))
        if out_ap is None:
            act(acc[:, 0:TB], acc[:, 0:TB], AF.Silu, (acck,), (acck,))
        else:
            act(out_ap, acc[:, 0:TB], AF.Silu, (acck,), (out_key,))

    def l2norm_set(cset, out_ap, out_key, scale):
        (pre_, prek), (acc, acck), (sq, sqk) = cset
        act(sq[:, 0:TB], acc[:, 0:TB], AF.Square, (acck,), (sqk,))
        pt, pk = bank()
        mm(pt[:, 0:TB], ONEf, sq[:, 0:TB], (KCST, sqk), (pk,))
        act(pre_[:, 0:TB], pt[:, 0:TB], AF.Ln, (pk, ("epst",)), (prek,), bias=epst[:, 1:2])
        act(pre_[:, 0:TB], pre_[:, 0:TB], AF.Exp, (prek,), (prek,), scale=-0.5)
        stt(out_ap, acc[:, 0:TB], scale, pre_[:, 0:TB], ALU.mult, ALU.mult, (acck, prek), (out_key,))

    def gchunk_id(g, ci):
        if ci < 2:
            return g * 2 + ci
        if ci < 4:
            return 8 + g * 2 + (ci - 2)
        return 16 + g * 4 + (ci - 4)

    def gdn_prompt_block(blk):
        P.tag = "gdn_proj"
        st["bankset"] = "proj"
        modulate(0, 0, blk, lambda k: u[:, k, :], lambda k: ("u", k))
        for n in range(4):
            pt, pk = bank()
            for k in range(KC):
                mm(pt[:, 0:32], u[:, k, n * 128:(n + 1) * 128], wba[:, k, :], (("u", k), ("wba",)), (pk,), start=(k == 0), stop=(k == KC - 1))
            act(ctmp[:, 0, :], pt[:, 0:16], AF.Exp, (pk,), (("ctmp", 0),), scale=-1.0)
            tsc("dve", ctmp[:, 0, :], ctmp[:, 0, :], 1.0, None, ALU.add, ALU.bypass, (("ctmp", 0),), (("ctmp", 0),))
            P.add("dve", lambda e, n=n: e.reciprocal(out=beta[:, n, :], in_=ctmp[:, 0, :]), (("ctmp", 0),), (("beta", n),))
            tsc("dve", nbeta[:, n, :], beta[:, n, :], -1.0, None, ALU.mult, ALU.bypass, (("beta", n),), (("nbeta", n),))
            tt("dve", ctmp[:, 1, :], pt[:, 16:32], dtb[:], ALU.add, (pk, ("dtb",), ("ctmp", 0)), (("ctmp", 1),))
            act(ctmp[:, 1, :], ctmp[:, 1, :], AF.Exp, (("ctmp", 1),), (("ctmp", 1),))
            tsc("dve", ctmp[:, 1, :], ctmp[:, 1, :], 1.0, None, ALU.add, ALU.bypass, (("ctmp", 1),), (("ctmp", 1),))
            act(ctmp[:, 1, :], ctmp[:, 1, :], AF.Ln, (("ctmp", 1),), (("ctmp", 1),))
            tt("dve", gtok[:, n, :], ctmp[:, 1, :], nexpA[:], ALU.mult, (("ctmp", 1), ("nexpA",)), (("gtok", n),))

        for g in range(4):
            tA, wkA = wload([
                (lambda t: wv(t, KC, 512)[:, :, 0:256], win[:, :, g * 256:(g + 1) * 256]),
                (lambda t: wv(t, KC, 512)[:, :, 256:512], win[:, :, 1024 + g * 256:1024 + (g + 1) * 256]),
            ])
            wA = wv(tA, KC, 512)

            def proj(wvv, wk, cl, rhs_fn, rkey_fn, N):
                pt, pk = bank()
                for k in range(KC):
                    mm(pt[:, 0:N], wvv[:, k, cl * 128:(cl + 1) * 128], rhs_fn(k), wk + (rkey_fn(k),), (pk,), start=(k == 0), stop=(k == KC - 1))
                return pt, pk

            pr_u = (lambda k: u[:, k, :], lambda k: ("u", k), TB)
            pr_s = (lambda k: us[:, k, :], lambda k: ("us", k), NS)
            for ci in range(4):
                pt, pk = proj(wA, wkA, ci, *pr_u)
                cset = conv_set()
                conv_silu(gchunk_id(g, ci), None, None, pt, pk, cset)
                if ci < 2:
                    l2norm_set(cset, Qt[:, ci, :], ("Qt", ci), 128.0 ** -0.5)
                else:
                    l2norm_set(cset, Kt[:, ci - 2, :], ("Kt", ci - 2), 1.0)
            if blk == 0:
                for ci in range(4):
                    gc = gchunk_id(g, ci)
                    pt, pk = proj(wA, wkA, ci, *pr_s)
                    cp("act", pres[:, gc, :], pt[:, 0:NS], (pk,), (("pres", gc),))
            tB, wkB = wload([(lambda t: wv(t, KC, 512), win[:, :, 2048 + g * 512:2048 + (g + 1) * 512])])
            wB = wv(tB, KC, 512)
            for ci in range(4):
                pt, pk = proj(wB, wkB, ci, *pr_u)
                conv_silu(gchunk_id(g, 4 + ci), Vt[:, ci, :], ("Vt", ci), pt, pk, conv_set(True))
            if blk == 0:
                for ci in range(4):
                    gc = gchunk_id(g, 4 + ci)
                    pt, pk = proj(wB, wkB, ci, *pr_s)
                    cp("act", pres[:, gc, :], pt[:, 0:NS], (pk,), (("pres", gc),))
            tC, wkC = wload([(lambda t: wv(t, KC, 512), win[:, :, 4096 + g * 512:4096 + (g + 1) * 512])])
            wC = wv(tC, KC, 512)
            for ci in range(4):
                pt, pk = proj(wC, wkC, ci, *pr_u)
                act(sz[:, ci, :], pt[:, 0:TB], AF.Silu, (pk,), (("sz", ci),))
            if blk == 0:
                for ci in range(4):
                    gc = 32 + g * 4 + ci
                    pt, pk = proj(wC, wkC, ci, *pr_s)
                    cp("act", pres[:, gc, :], pt[:, 0:NS], (pk,), (("pres", gc),))
            for n in range(4):
                pb, pbk = bbank()
                for kh in range(2):
                    tr(pb[:, kh * 128:(kh + 1) * 128], Kt[:, kh, n * 128:(n + 1) * 128], id_bf[:], (("Kt", kh), ("id_bf",)), (pbk,))
                for vh in range(4):
                    tr(pb[:, 256 + vh * 128:256 + (vh + 1) * 128], Vt[:, vh, n * 128:(n + 1) * 128], id_bf[:], (("Vt", vh), ("id_bf",)), (pbk,))
                cp("act", Ktok[:, n, :, :].rearrange("p a b -> p (a b)"), pb[:, 0:256], (pbk,), (("Ktok", n),))
                cp("act", Vtok[:, n, :, :].rearrange("p a b -> p (a b)"), pb[:, 256:768], (pbk,), (("Vtok", n),))
            gdn_group_chunks(g, blk)
            P.tag = "gdn_proj"
        st["bankset"] = "all"
        if blk < NBLK - 1:
            load_x_block(blk + 1)
        if blk == 0:
            mod_emit(2)
            P.tag = "sample_gdn"
            st["bankset"] = "sample"
            sample_gdn()
            st["bankset"] = "main5"
        P.tag = "wout"
        wo = w_out.rearrange("(k p) n -> p k n", p=128)
        blks2 = [blk, "s"] if blk == 1 else [blk]
        for ch in range(2):
            acc = open_acc(blks2, 4)
            first_s = True
            for vh in range(2):
                t, wk = wload([(lambda t: wv(t, 8, 512), wo[:, vh * 8:(vh + 1) * 8, ch * 512:(ch + 1) * 512])])
                wvv = wv(t, 8, 512)
                for blk2 in blks2:
                    N, ub, uk, bg, bk = blkinfo(blk2)
                    for dl in range(4):
                        for vc in range(vh * 8, vh * 8 + 8):
                            acc_mm(acc, blk2, dl, wvv[:, vc - vh * 8, dl * 128:(dl + 1) * 128], bg[:, vc, 0:N], wk + (bk(vc),),
                                   first=(vc == 0), last=(vc == 15), st_first=(blk2 == "s" and first_s))
                            if blk2 == "s":
                                first_s = False
            for blk2 in blks2:
                for dl in range(4):
                    pt, pk = acc_out(acc, blk2, dl)
                    residual(0, 0, blk2, ch * 4 + dl, pt, pk)
        if blk == 0:
            mod_emit(6)
        layer_norm(0, 0, blk)
        if blk == 1:
            layer_norm(0, 0, "s")

    def bank_pd():
        i = BANKS_PD[st.setdefault("pd", 0) % len(BANKS_PD)]
        st["pd"] += 1
        return PS[i], ("ps", i)

    def bank_rc():
        i = BANKS_RC[st.setdefault("rc", 0) % len(BANKS_RC)]
        st["rc"] += 1
        return PS[i], ("ps", i)

    def gdn_tiles(blk):
        if blk < NBLK - 1:
            return [(_T(y[:, k, (blk + 1) * TB:(blk + 2) * TB]), ("y", k, blk + 1)) for k in range(8)]
        return [(Fs[i], ("F", i)) for i in range(6)] + [(Fs[6], ("F", 6)), (Fs[8], ("F", 8))]

    def sethk(s):
        b = 9 + 4 * s
        return [(Hs[b + j], ("H", b + j)) for j in range(4)]

    def pd_stages(g, n, s, blk):
        cs = slice(n * 128, (n + 1) * 128)
        hs = slice(4 * g, 4 * g + 4)
        gcols = gtok[:, n, hs]
        gk = ("gtok", n)
        GT = gdn_tiles(blk)
        (gSL, k0), (gTRI, k1), (G1, k2), (DTm, k3), (Dm, k4), (eGr, k5) = GT[0:6]
        Vb, Vbk = GT[6 + s]
        (Tfin, Tfk), (attnT, atk), (Qd, Qdk), (kdec, kdk) = sethk(s)
        hb = 0 if s == 0 else 17
        A0, U0, P1, Q1, Ta, Tb_ = (Hs[hb + j] for j in range(6))
        hk_ = lambda j: ("H", hb + j)
        cN = ("ctmp", 3, s)
        cG = ("ctmp", 4, s)
        stages = []

        def prep():
            gb = gcols.unsqueeze(2).broadcast_to([128, 4, 128])
            tt("pool", v4(gSL), b4(SU), gb, ALU.mult, (KCST, gk), (k0,))
            tt("pool", v4(gTRI), b4(TRI), gb, ALU.mult, (KCST, gk), (k1,))
            tt("pool", v4(G1), b4(ONEf), gb, ALU.mult, (KCST, gk), (k2,))
            pA, pAk = bank_pd()
            pB, pBk = bank_pd()
            pC, pCk = bank_pd()
            for hh in range(4):
                hsl = hsl_(hh)
                mm(pA[:, hsl], gSL[:, hsl], TRI, (k0, KCST), (pAk,))
                mm(pB[:, hsl], gTRI[:, hsl], SU, (k1, KCST), (pBk,))
                mm(pC[:, hsl], G1[:, hsl], TRI, (k2, KCST), (pCk,))
            act(DTm[:], pA[:], AF.Exp, (pAk,), (k3,))
            tt("pool", v4(DTm), v4(DTm), b4(TRI), ALU.mult, (k3, KCST), (k3,))
            act(Dm[:], pB[:], AF.Exp, (pBk,), (k4,))
            tt("pool", v4(Dm), v4(Dm), b4(SU), ALU.mult, (k4, KCST), (k4,))
            act(eGr[:], pC[:], AF.Exp, (pCk,), (k5,))
            cp("pool", ctmp[:, 4, 4 * s:4 * s + 4], v4(eGr)[:, :, 127], (k5,), (cG,))
            pD, pDk = bank_pd()
            mm(pD[:, 0:4], TRI, gcols, (KCST, gk), (pDk,))
            mm(pD[:, 4:8], SU, gcols, (KCST, gk), (pDk,))
            act(ctmp[:, 2, 0:8], pD[:, 0:8], AF.Exp, (pDk,), (("ctmp", 2),))
            tt("dve", ctmp[:, 3, 4 * s:4 * s + 4], ctmp[:, 2, 0:4], nbeta[:, n, hs], ALU.mult, (("ctmp", 2), ("nbeta", n)), (cN,))
            pK, pKk = bank_pd()
            for kh in range(2):
                mm(pK[:, kh * 128:(kh + 1) * 128], Kt[:, kh, cs], Kt[:, kh, cs], (("Kt", kh),), (pKk,))
                mm(pK[:, 256 + kh * 128:256 + (kh + 1) * 128], Kt[:, kh, cs], Qt[:, kh, cs], (("Kt", kh), ("Qt", kh)), (pKk,))
            for hh in range(4):
                kh = hh // 2
                hsl = hsl_(hh)
                stt(A0[:, hsl], pK[:, kh * 128:(kh + 1) * 128], nbeta[:, n, 4 * g + hh:4 * g + hh + 1], Dm[:, hsl], ALU.mult, ALU.mult,
                    (pKk, ("nbeta", n), k4), (hk_(0),))
                tt("dve", attnT[:, hsl], pK[:, 256 + kh * 128:256 + (kh + 1) * 128], DTm[:, hsl], ALU.mult, (pKk, k3), (atk,))
                tt("pool", Qd[:, hsl], Qt[:, kh, cs], eGr[:, hsl], ALU.mult, (("Qt", kh), k5), (Qdk,))
                act(Vb[:, hsl], Vtok[:, n, hh, :], AF.Identity, (("Vtok", n), ("beta", n)), (Vbk,), scale=beta[:, n, 4 * g + hh:4 * g + hh + 1])
                act(kdec[:, hsl], Ktok[:, n, kh, :], AF.Identity, (("Ktok", n), ("ctmp", 2)), (kdk,), scale=ctmp[:, 2, 4 + hh:5 + hh])
            pb, pbk = bbank()
            for hh in range(4):
                tr(pb[:, hsl_(hh)], A0[:, hsl_(hh)], id_bf[:], (hk_(0), ("id_bf",)), (pbk,))
            cp("act", U0[:], pb[:, 0:512], (pbk,), (hk_(1),))
            tt("dve", v4(Ta), v4(U0), b4(id_bf[:]), ALU.add, (hk_(1), ("id_bf",)), (hk_(4),))
        stages.append(prep)

        state = dict(Pc=U0, Qc=A0, Pk=hk_(1), Qk=hk_(0), Pn=P1, Qn=Q1, Pnk=hk_(2), Qnk=hk_(3),
                     Tc=Ta, Tck=hk_(4), Tn=Tb_, Tnk=hk_(5))

        def level(lev):
            def f():
                z = state
                last = lev == 5
                if last:
                    z["Tn"], z["Tnk"] = Tfin, Tfk
                pq, pqk = bank_pd()
                for hh in range(4):
                    mm(pq[:, hsl_(hh)], z["Pc"][:, hsl_(hh)], z["Qc"][:, hsl_(hh)], (z["Pk"], z["Qk"]), (pqk,))
                cp("act", z["Qn"][:], pq[:], (pqk,), (z["Qnk"],))
                if not last:
                    pp, ppk = bank_pd()
                    for hh in range(4):
                        mm(pp[:, hsl_(hh)], z["Qc"][:, hsl_(hh)], z["Pc"][:, hsl_(hh)], (z["Pk"], z["Qk"]), (ppk,))
                    cp("dve", z["Pn"][:], pp[:], (ppk,), (z["Pnk"],))
                ptt, ptk = bank_pd()
                for hh in range(4):
                    mm(ptt[:, hsl_(hh)], z["Qn"][:, hsl_(hh)], z["Tc"][:, hsl_(hh)], (z["Qnk"], z["Tck"]), (ptk,))
                tt("dve", z["Tn"][:], z["Tc"][:], ptt[:], ALU.add, (z["Tck"], ptk), (z["Tnk"],))
                (z["Pc"], z["Qc"], z["Pk"], z["Qk"], z["Pn"], z["Qn"], z["Pnk"], z["Qnk"]) = (
                    z["Pn"], z["Qn"], z["Pnk"], z["Qnk"], z["Pc"], z["Qc"], z["Pk"], z["Qk"])
                z["Tc"], z["Tck"], z["Tn"], z["Tnk"] = z["Tn"], z["Tnk"], z["Tc"], z["Tck"]
            return f
        for lev in range(6):
            stages.append(level(lev))
        return stages

    def rc_stages(g, n, s, blk):
        cs = slice(n * 128, (n + 1) * 128)
        Vb, Vbk = gdn_tiles(blk)[6 + s]
        (Tfin, Tfk), (attnT, atk), (Qd, Qdk), (kdec, kdk) = sethk(s)
        W, vnew, osq, Ft = Hs[6], Hs[7], Hs[8], Fs[7]
        cN = ("ctmp", 3, s)
        cG = ("ctmp", 4, s)
        hold = {}

        def r1():
            pks, pksk = bank_rc()
            for hh in range(4):
                h = 4 * g + hh
                mm(pks[:, hsl_(hh)], Kt[:, hh // 2, cs], Sb[:, h, :], (("Kt", hh // 2), ("Sb", h)), (pksk,))
            for hh in range(4):
                stt(W[:, hsl_(hh)], pks[:, hsl_(hh)], ctmp[:, 3, 4 * s + hh:4 * s + hh + 1], Vb[:, hsl_(hh)], ALU.mult, ALU.add,
                    (pksk, cN, Vbk), (("H", 6),))

        def r2():
            pvn, pvnk = bank_rc()
            for hh in range(4):
                mm(pvn[:, hsl_(hh)], Tfin[:, hsl_(hh)], W[:, hsl_(hh)], (Tfk, ("H", 6)), (pvnk,))
            cp("act", vnew[:], pvn[:], (pvnk,), (("H", 7),))

        def r3():
            po, pok = bank_rc()
            pds, pdsk = bank_rc()
            hold["po"] = (po, pok)
            for hh in range(4):
                h = 4 * g + hh
                mm(po[:, hsl_(hh)], Sb[:, h, :], Qd[:, hsl_(hh)], (("Sb", h), Qdk), (pok,), start=True, stop=False)
                mm(po[:, hsl_(hh)], vnew[:, hsl_(hh)], attnT[:, hsl_(hh)], (("H", 7), atk), (pok,), start=False, stop=True)
                mm(pds[:, hsl_(hh)], kdec[:, hsl_(hh)], vnew[:, hsl_(hh)], (kdk, ("H", 7)), (pdsk,))
            for hh in range(4):
                h = 4 * g + hh
                stt(S[:, h, :], S[:, h, :], ctmp[:, 4, 4 * s + hh:4 * s + hh + 1], pds[:, hsl_(hh)], ALU.mult, ALU.add,
                    (("S", h), cG, pdsk), (("S", h),))
                cp("pool", Sb[:, h, :], S[:, h, :], (("S", h),), (("Sb", h),))

        def r4a():
            po, pok = hold["po"]
            act(osq[:], po[:], AF.Square, (pok,), (("H", 8),))
            pss, pssk = bank_rc()
            hold["pss"] = (pss, pssk)
            mm(pss[:], one_bf[:], osq[:], (("one_bf",), ("H", 8)), (pssk,))
            tsc("dve", Ft[:], pss[:], 1.0 / 128.0, NORM_EPS, ALU.mult, ALU.add, (pssk,), (("F", 7),))
            act(Ft[:], Ft[:], AF.Ln, (("F", 7),), (("F", 7),))
            act(Ft[:], Ft[:], AF.Exp, (("F", 7),), (("F", 7),), scale=-0.5)

        def r4b():
            po, pok = hold["po"]
            tt("dve", Ft[:], po[:], Ft[:], ALU.mult, (pok, ("F", 7)), (("F", 7),))
            stt(big[:, 4 * g:4 * g + 4, cs], v4(Ft), normw[:, 0:1], sz[:, :, cs], ALU.mult, ALU.mult,
                (("F", 7), ("normw",)) + tuple(("sz", i) for i in range(4)), tuple(("big", 4 * g + i) for i in range(4)))

        return [r1, r2, r3, r4a, r4b]

    def gdn_group_chunks(g, blk):
        P.tag = "gdn_chunk"
        for f in pd_stages(g, 0, 0, blk):
            f()
        for n in range(4):
            rc = rc_stages(g, n, n % 2, blk)
            pd = pd_stages(g, n + 1, (n + 1) % 2, blk) if n < 3 else []
            order = []
            ri, pi = 0, 0
            while ri < len(rc) or pi < len(pd):
                if pi < len(pd):
                    order.append(pd[pi]); pi += 1
                if ri < len(rc):
                    order.append(rc[ri]); ri += 1
            for f in order:
                f()

    def sample_gdn():
        NSL = 8
        Sst = [_T(y[:, i, 3 * TB:4 * TB].rearrange("p (a b) -> p a b", a=4)) for i in range(NSL)]
        sstk = lambda sl: ("y", sl, 3)
        preskeys = tuple(("pres", c) for c in range(32))
        dma("sp", sconv[:].rearrange("p a b c -> p (a b c)"), sconvT, "sc", (), K_sconv)
        pt, pk = bank()
        for k in range(KC):
            mm(pt[0:NS, 0:32], us[:, k, :], wba[:, k, :], (("us", k), ("wba",)), (pk,), start=(k == 0), stop=(k == KC - 1))
        act(bas[:, 0:16], pt[0:NS, 0:16], AF.Exp, (pk,), K_bas, scale=-1.0)
        tsc("dve", bas[:, 0:16], bas[:, 0:16], 1.0, None, ALU.add, ALU.bypass, K_bas, K_bas)
        P.add("dve", lambda e: e.reciprocal(out=bas[:, 0:16], in_=bas[:, 0:16]), K_bas, K_bas)
        tt("dve", bas[:, 16:32], pt[0:NS, 16:32], dtb[0:NS, :], ALU.add, (pk, ("dtb",)) + K_bas, K_bas)
        act(bas[:, 16:32], bas[:, 16:32], AF.Exp, K_bas, K_bas)
        tsc("dve", bas[:, 16:32], bas[:, 16:32], 1.0, None, ALU.add, ALU.bypass, K_bas, K_bas)
        act(bas[:, 16:32], bas[:, 16:32], AF.Ln, K_bas, K_bas)
        tt("dve", bas[:, 16:32], bas[:, 16:32], nexpA[0:NS, :], ALU.mult, K_bas + (("nexpA",),), K_bas)
        act(bas[:, 16:32], bas[:, 16:32], AF.Exp, K_bas, K_bas)
        I16 = cst[0:NS, C_ID:C_ID + NS].unsqueeze(2).broadcast_to([NS, NS, NH])
        for qi in range(2):
            tt("dve", sexp[:, qi, :, :], bas[:, qi * 16:qi * 16 + 16].unsqueeze(1).broadcast_to([NS, NS, NH]), I16, ALU.mult, K_bas + (KCST,), K_sexp)
        prow, prk = bank()
        for qi in range(2):
            mm(prow[:, qi * 256:(qi + 1) * 256], cst[0:NS, C_ONE:C_ONE + 128], sexp[:, qi, :, :].rearrange("p a b -> p (a b)"),
               (KCST,) + K_sexp, (prk,))
        cp("act", sm[:, 0, :], prow[:, 0:256], (prk,), (("u", 0),))
        cp("act", sm[:, 1, :], prow[:, 256:512], (prk,), (("u", 1),))

        def cw(j):
            return convw[:, :, j].unsqueeze(2).broadcast_to([128, 32, NS])
        ctmp2 = sm[:, 2:4, :].rearrange("p a b -> p (a b)").rearrange("p (c n) -> p c n", c=32)
        ck = (("u", 2), ("u", 3))
        tt("dve", qkvs[:], pres[:, 0:32, :], cw(3), ALU.mult, preskeys + (("convw",),), K_qkvs)
        for j in range(3):
            tt("dve", ctmp2, sconv[:, :, j, :], cw(j), ALU.mult, K_sconv + (("convw",),), ck)
            tt("dve", qkvs[:], qkvs[:], ctmp2, ALU.add, K_qkvs + ck, K_qkvs)
        act(qkvs[:], qkvs[:], AF.Silu, K_qkvs, K_qkvs)
        act(szs[:], pres[:, 32:48, :], AF.Silu, tuple(("pres", c) for c in range(32, 48)), K_szs)
        dma("sp", sconv_new_o, pres[:, 0:32, :].rearrange("p a b -> p (a b)"), "out", preskeys, ())
        qk2 = qkvs[:, 0:16, :].rearrange("p a b -> p (a b)")
        act(sm[:, 2, :], qk2, AF.Square, K_qkvs, (("u", 2),))
        pt2, pk2 = bank()
        mm(pt2[:, 0:256], ONEf, sm[:, 2, :], (KCST, ("u", 2)), (pk2,))
        tsc("dve", sm[:, 3, :], pt2[:, 0:256], NORM_EPS, None, ALU.add, ALU.bypass, (pk2,), (("u", 3),))
        act(sm[:, 3, :], sm[:, 3, :], AF.Ln, (("u", 3),), (("u", 3),))
        act(sm[:, 3, :], sm[:, 3, :], AF.Exp, (("u", 3),), (("u", 3),), scale=-0.5)
        tt("dve", qk2, qk2, sm[:, 3, :], ALU.mult, K_qkvs + (("u", 3),), K_qkvs)
        tsc("dve", qkvs[:, 0:8, :], qkvs[:, 0:8, :], 128.0 ** -0.5, None, ALU.mult, ALU.bypass, K_qkvs, K_qkvs)
        for r in range(2):
            src_k = qkvs[:, 8:16, :].rearrange("p k b -> p b k")
            src_q = qkvs[:, 0:8, :].rearrange("p k b -> p b k")
            dstk = kqs[:, :, :, 0].rearrange("p b (k r) -> p b k r", r=2)[:, :, :, r]
            dstq = kqs[:, :, :, 1].rearrange("p b (k r) -> p b k r", r=2)[:, :, :, r]
            cp("dve", dstk, src_k, K_qkvs, K_kqs)
            cp("dve", dstq, src_q, K_qkvs, K_kqs)
        tt("dve", sm[:, 4, 0:128], qkvs[:, 0:8, :].rearrange("p a b -> p (a b)"), qkvs[:, 8:16, :].rearrange("p a b -> p (a b)"), ALU.mult,
           K_qkvs, (("u", 4),))
        pqk_t, pqk_k = bank()
        mm(pqk_t[:, 0:128], ONEf, sm[:, 4, 0:128], (KCST, ("u", 4)), (pqk_k,))
        cp("act", sm[:, 5, 0:128], pqk_t[:, 0:128], (pqk_k,), (("u", 5),))
        ptk, ptkk = bank()
        ptk2, ptkk2 = bank()
        for kh in range(8):
            dst = (ptk if kh < 4 else ptk2)
            dk_ = (ptkk if kh < 4 else ptkk2)
            P.add("pe", lambda e, dst=dst, kh=kh: e.transpose(dst[0:NS, (kh % 4) * 128:(kh % 4 + 1) * 128], qkvs[:, 8 + kh, :], IDf),
                  K_qkvs + (KCST,), (dk_,))
        cp("act", ktoks[:, 0:4, :].rearrange("p a b -> p (a b)"), ptk[0:NS, :], (ptkk,), K_ktoks)
        cp("act", ktoks[:, 4:8, :].rearrange("p a b -> p (a b)"), ptk2[0:NS, :], (ptkk2,), K_ktoks)
        pS_t, pS_k = bank()
        it = 0
        for b in range(NS):
            for q4 in range(4):
                sl = it % NSL
                it += 1
                dma("sp", Sst[sl][:], S_in[b, q4 * 4:(q4 + 1) * 4].rearrange("h k v -> k h v"), f"sin{sl}", (), (sstk(sl),))
                for hh in range(4):
                    h = q4 * 4 + hh
                    mm(pS_t[:, (b * NH + h) * 2:(b * NH + h) * 2 + 2], Sst[sl][:, hh, :], kqs[:, b, h, :], (sstk(sl),) + K_kqs, (pS_k,))
        pS4 = pS_t[:].rearrange("p (b h t) -> p b h t", b=NS, h=NH)
        Sk = pS4[:, :, :, 0]
        Sq = pS4[:, :, :, 1]

        def bh(t2d):
            return t2d.rearrange("p (b h) -> p b h", b=NS)
        vS = qkvs[:, 16:32, :].rearrange("p h b -> p b h")
        tt("dve", bh(sm[:, 6, :]), Sk, bh(sm[:, 1, :]), ALU.mult, (pS_k, ("u", 1)), (("u", 6),))
        tt("dve", bh(sm[:, 6, :]), vS, bh(sm[:, 6, :]), ALU.subtract, K_qkvs + (("u", 6),), (("u", 6),))
        tt("dve", bh(sm[:, 6, :]), bh(sm[:, 6, :]), bh(sm[:, 0, :]), ALU.mult, (("u", 6), ("u", 0)), (("u", 6),))
        tt("dve", bh(sm[:, 7, :]), Sq, bh(sm[:, 1, :]), ALU.mult, (pS_k, ("u", 1)), (("u", 7),))
        qkb = sm[:, 5, 0:128].rearrange("p (k b) -> p b k", k=8)
        for r in range(2):
            vn_r = bh(sm[:, 6, :]).rearrange("p b (k r) -> p b k r", r=2)[:, :, :, r]
            o8_r = bh(sm[:, 2, :]).rearrange("p b (k r) -> p b k r", r=2)[:, :, :, r]
            tt("dve", o8_r, vn_r, qkb, ALU.mult, (("u", 6), ("u", 5)), (("u", 2),))
        tt("dve", sm[:, 7, :], sm[:, 7, :], sm[:, 2, :], ALU.add, (("u", 7), ("u", 2)), (("u", 7),))
        act(sm[:, 3, :], sm[:, 7, :], AF.Square, (("u", 7),), (("u", 3),))
        pn_t, pn_k = bank()
        mm(pn_t[:, 0:256], ONEf, sm[:, 3, :], (KCST, ("u", 3)), (pn_k,))
        tsc("dve", sm[:, 4, :], pn_t[:, 0:256], 1.0 / 128.0, NORM_EPS, ALU.mult, ALU.add, (pn_k,), (("u", 4),))
        act(sm[:, 4, :], sm[:, 4, :], AF.Ln, (("u", 4),), (("u", 4),))
        act(sm[:, 4, :], sm[:, 4, :], AF.Exp, (("u", 4),), (("u", 4),), scale=-0.5)
        tt("dve", sm[:, 7, :], sm[:, 7, :], sm[:, 4, :], ALU.mult, (("u", 7), ("u", 4)), (("u", 7),))
        stt(bigs[:, 0:16, :].rearrange("p h b -> p b h"), bh(sm[:, 7, :]), normw[:, 0:1], szs[:].rearrange("p h b -> p b h"), ALU.mult, ALU.mult,
            (("u", 7), ("normw",)) + K_szs, tuple(("bigs", c) for c in range(16)))
        vt_t = [_T(y[0:NS, i, 2 * TB:3 * TB].rearrange("p (a b) -> p a b", a=4)) for i in range(4)]
        vm_t = [_T(y[0:NS, 4 + i, 2 * TB:3 * TB].rearrange("p (a b) -> p a b", a=4)) for i in range(4)]
        vtk = lambda i: ("y", i, 2)
        vmk = lambda i: ("y", 4 + i, 2)
        for q4 in range(4):
            pv, pvk = bank()
            for hh in range(4):
                h = q4 * 4 + hh
                P.add("pe", lambda e, pv=pv, hh=hh, h=h: e.transpose(pv[0:NS, hh * 128:(hh + 1) * 128], bh(sm[:, 6, :])[:, :, h], IDf),
                      (("u", 6), KCST), (pvk,))
            cp("act", vt_t[q4][:].rearrange("p a b -> p (a b)"), pv[0:NS, :], (pvk,), (vtk(q4),))
        for b in range(NS):
            for q4_ in range(4):
                tsc("dve", vm_t[q4_][:].rearrange("p a b -> p (a b)"), vt_t[q4_][:].rearrange("p a b -> p (a b)"), cst[0:NS, C_ID + b:C_ID + b + 1], None,
                    ALU.mult, ALU.bypass, (vtk(q4_), KCST), (vmk(q4_),))
            for q4 in range(4):
                sl = it % NSL
                it += 1
                dma("sp", Sst[sl][:], S_in[b, q4 * 4:(q4 + 1) * 4].rearrange("h k v -> k h v"), f"sin{sl}", (), (sstk(sl),))
                pu_t, pu_k = bank()
                for hh in range(4):
                    h = q4 * 4 + hh
                    mm(pu_t[:, hsl_(hh)], ktoks[:, h // 2, :], vm_t[q4][:, hh, :], K_ktoks + (vmk(q4),), (pu_k,))
                for hh in range(4):
                    h = q4 * 4 + hh
                    stt(Sst[sl][:, hh, :], Sst[sl][:, hh, :], sm[:, 1, b * NH + h:b * NH + h + 1], pu_t[:, hsl_(hh)], ALU.mult, ALU.add,
                        (sstk(sl), ("u", 1), pu_k), (sstk(sl),))
                dma("sp", sS_o[b, q4 * 4:(q4 + 1) * 4].rearrange("h k v -> k h v"), Sst[sl][:], f"sout{sl}", (sstk(sl),), ())

    def pool_weights():
        t, wk = wload([(lambda t: t[:, 0:2048].rearrange("p (g k c) -> p g k c", g=4, k=2), pool_w.rearrange("g (k p) c -> p g k c", p=128))])
        return t[:, 0:2048].rearrange("p (g k c) -> p g k c", g=4, k=2), wk

    def pool_prompt_block(blk, pw, pwk):
        P.tag = "pool"
        L = 15 + TB
        PSETS = [((Fw[3 * i], ("F", 3 * i)), [(Fw[3 * i + 1], ("F", 3 * i + 1)), (Fw[3 * i + 2], ("F", 3 * i + 2))]) for i in range(3)]
        for k in range(KC):
            (u1_, u1k), bufs = PSETS[k % 3]
            gi = k // 2
            w = 2 << gi
            mk = (("mod", 1, 1, k), ("mod", 1, 0, k))
            cp("pool", u1_[:, 0:15], pcarry[:, k, :], (("pcarry", k),), (u1k,))
            act(u1_[:, 15:15 + TB], yap(k, blk), AF.Identity, (ykey(k, blk),) + mk, (u1k,),
                bias=mod_ap(1, 0, k, False), scale=mod_ap(1, 1, k, False))
            cp("pool", pcarry[:, k, :], u1_[:, TB:TB + 15], (u1k,), (("pcarry", k),))
            src_, srck, off, m, bi = u1_, u1k, 0, 1, 0
            while m < w:
                dst, dstk = bufs[bi % 2]
                n_el = L - off - m
                tt("dve", dst[:, 0:n_el], src_[:, m:m + n_el], src_[:, 0:n_el], ALU.add, (srck,), (dstk,))
                src_, srck = dst, dstk
                off += m
                m *= 2
                bi += 1
            s0 = 15 - off
            mean, meank = bufs[bi % 2]
            tsc("dve", mean[:, 0:TB], src_[:, s0:s0 + TB], 1.0 / w, None, ALU.mult, ALU.bypass, (srck,), (meank,))
            if blk == 0:
                rc = cst[:, C_RC + gi * 16:C_RC + gi * 16 + 16]
                tt("dve", mean[:, 0:16], src_[:, s0:s0 + 16], rc, ALU.mult, (srck, KCST, meank), (meank,))
            tt("dve", u[:, k, :], mean[:, 0:TB], u1_[:, 15:15 + TB], ALU.subtract, (meank, u1k), (("u", k),))
        for dc in range(KC):
            gi = dc // 2
            el = dc % 2
            pt, pk = bank()
            for k2 in range(2):
                mm(pt[:, 0:TB], pw[:, gi, k2, el * 128:(el + 1) * 128], u[:, gi * 2 + k2, :], pwk + (("u", gi * 2 + k2),), (pk,), start=(k2 == 0), stop=(k2 == 1))
            residual(1, 0, blk, dc, pt, pk, pool=True)
        layer_norm(1, 0, blk)

    def pool_sample(pw, pwk):
        dma("sp", spool[:].rearrange("p a b c -> p (a b c)"), spoolT, "sp_", (), KF03)
        modulate(1, 0, "s", lambda k: u1s[:, k, :], lambda k: ("F", 5))
        u1keys = (("F", 5),)
        dma("sp", spool_new_o, u1s[:].rearrange("p a b -> p (a b)"), "out", u1keys, ())
        for gi in range(4):
            w = 2 << gi
            ks = slice(gi * 2, gi * 2 + 2)
            P.add("dve", lambda e, ks=ks, w=w: e.reduce_sum(out=pls[:, ks, :], in_=spool[:, ks, :, 15 - (w - 1):15], axis=mybir.AxisListType.X),
                  KF03, (("F", 6),))
            tt("dve", pls[:, ks, :], pls[:, ks, :], u1s[:, ks, :], ALU.add, (("F", 6),) + u1keys, (("F", 6),))
            stt(us[:, ks, :], pls[:, ks, :], 1.0 / w, u1s[:, ks, :], ALU.mult, ALU.subtract, (("F", 6),) + u1keys, (("us", gi * 2), ("us", gi * 2 + 1)))
        for dc in range(KC):
            gi = dc // 2
            el = dc % 2
            pt, pk = bank()
            for k2 in range(2):
                mm(pt[:, 0:NS], pw[:, gi, k2, el * 128:(el + 1) * 128], us[:, gi * 2 + k2, :], pwk + (("us", gi * 2 + k2),), (pk,), start=(k2 == 0), stop=(k2 == 1))
            residual(1, 0, "s", dc, pt, pk, pool=True)
        layer_norm(1, 0, "s")

    modulate(0, 0, "s", lambda k: us[:, k, :], lambda k: ("us", k))
    for blk in range(NBLK):
        gdn_prompt_block(blk)
        ffn(0, [blk, "s"] if blk == 2 else [blk])
        if blk == 0:
            mod_emit(12)
    for k in range(KC):
        tsc("dve", cgate[:, k, :], mod[:, 1, 2, k, :], pscale[:, k:k + 1], None, ALU.mult, ALU.bypass, (("mod", 1, 2, k), ("pscale",)), (("cgate",),))
    for blk in range(NBLK):
        pw, pwk = pool_weights()
        pool_prompt_block(blk, pw, pwk)
        if blk == 0:
            pool_sample(pw, pwk)
        ffn(1, [blk, "s"] if blk == 1 else [blk])
    for k in range(KC):
        dma("sp", yT_o[k * 128:(k + 1) * 128, :], y[:, k, :], "out", tuple(("y", k, b) for b in range(NBLK)), ())
    dma("sp", ysT_o.rearrange("(k p) n -> p k n", p=128), ys[:], "out", tuple(("ys", k) for k in range(KC)), ())
    dma("sp", pS_o.rearrange("h k v -> k h v"), S[:], "out", tuple(("S", h) for h in range(NH)), ())
    dma("sp", pconv_o, ccarry[:].rearrange("p a b -> p (a b)"), "out", tuple(("ccarry", c) for c in range(32)), ())
    dma("sp", ppool_o, pcarry[:].rearrange("p a b -> p (a b)"), "out", tuple(("pcarry", k) for k in range(KC)), ())

    P.emit(sems, dsem)
    return nc, es, P


_CACHE = {}


def _consts():
    c = np.zeros((128, NCONST), np.float32)
    i = np.arange(128)
    c[:, C_ID:C_ID + 128] = np.eye(128, dtype=np.float32)
    c[:, C_TRI:C_TRI + 128] = (i[:, None] <= i[None, :]).astype(np.float32)
    c[:, C_SU:C_SU + 128] = (i[:, None] > i[None, :]).astype(np.float32)
    c[:, C_ONE:C_ONE + 128] = 1.0
    for gi in range(4):
        w = 2 << gi
        c[:, C_RC + gi * 16:C_RC + gi * 16 + 16] = (1.0 / np.minimum(np.arange(16) + 1, w)).astype(np.float32)[None, :]
    return c


def _pm(a):
    n = a.shape[0] // 128
    b = a.reshape((n, 128) + a.shape[1:])
    b = np.moveaxis(b, 1, 0)
    return np.ascontiguousarray(b.reshape(128, -1))


def kernel(x_prompt, x_sample, c_prompt, c_sample, state_gdn_S, state_gdn_conv, state_pool,
           ada_w, ada_b, ln_g, ln_b, gdn_w_in, gdn_conv_w, gdn_A_log, gdn_dt_bias, gdn_norm_w,
           gdn_w_out, pool_w, pool_scale, ffn_w_up, ffn_w_down):
    f = lambda a: np.ascontiguousarray(np.asarray(a, dtype=np.float32))
    x_prompt, x_sample, c_prompt, c_sample = f(x_prompt), f(x_sample), f(c_prompt), f(c_sample)
    state_gdn_S, state_gdn_conv, state_pool = f(state_gdn_S), f(state_gdn_conv), f(state_pool)
    if "nc" not in _CACHE:
        _CACHE["nc"] = build_program()
    nc = _CACHE["nc"][0]
    shared = {
        "ada_w": f(ada_w),
        "ada_bT": np.ascontiguousarray(f(ada_b).reshape(2, 48, 128).transpose(2, 0, 1).reshape(128, 96)),
        "ln_gT": np.ascontiguousarray(f(ln_g).reshape(2, 2, 8, 128).transpose(3, 0, 1, 2).reshape(128, 32)),
        "ln_bT": np.ascontiguousarray(f(ln_b).reshape(2, 2, 8, 128).transpose(3, 0, 1, 2).reshape(128, 32)),
        "w_in": f(gdn_w_in)[0],
        "convwT": np.ascontiguousarray(f(gdn_conv_w)[0].reshape(4, 32, 128).transpose(2, 1, 0).reshape(128, 128)),
        "alog_b": np.ascontiguousarray(np.broadcast_to(f(gdn_A_log)[0][None, :], (128, NH))),
        "dtb_b": np.ascontiguousarray(np.broadcast_to(f(gdn_dt_bias)[0][None, :], (128, NH))),
        "normw_c": np.ascontiguousarray(f(gdn_norm_w)[0].reshape(128, 1)),
        "w_out": f(gdn_w_out)[0],
        "pool_w": f(pool_w)[0],
        "pscaleT": np.ascontiguousarray(f(pool_scale)[0].reshape(8, 128).T),
        "w_up": f(ffn_w_up),
        "w_down": f(ffn_w_down),
        "consts": _consts(),
    }
    in_maps = []
    for i in range(NCORES):
        sl = slice(i * NS, (i + 1) * NS)
        cT = np.concatenate([c_prompt[i][:, None], c_sample[sl].T], axis=1)
        sc = state_gdn_conv[0, sl]
        spl = state_pool[0, sl]
        m = dict(shared)
        m.update({
            "xT": np.ascontiguousarray(x_prompt[i].T),
            "xsT": np.ascontiguousarray(x_sample[sl, 0, :].T),
            "cT": np.ascontiguousarray(cT),
            "S_in": np.ascontiguousarray(state_gdn_S[0, sl]),
            "sconv_nat": np.ascontiguousarray(sc),
            "sconvT": np.ascontiguousarray(sc.reshape(NS, 3, 32, 128).transpose(3, 2, 1, 0).reshape(128, 32 * 3 * NS)),
            "spool_nat": np.ascontiguousarray(spl),
            "spoolT": np.ascontiguousarray(spl.reshape(NS, 15, 8, 128).transpose(3, 2, 0, 1).reshape(128, 8 * NS * 15)),
        })
        in_maps.append(m)
    res = run_bass_kernel_spmd(nc, in_maps, core_ids=list(range(NCORES)))
    R = res.results
    B, DEC = NCORES, NCORES * NS
    y_prompt = np.zeros((B, SEQ, D), np.float32)
    y_sample = np.zeros((DEC, 1, D), np.float32)
    p_S = np.zeros((1, B, NH, 128, 128), np.float32)
    p_conv = np.zeros((1, B, 3, 4096), np.float32)
    p_pool = np.zeros((1, B, 15, D), np.float32)
    s_S = np.zeros((1, DEC, NH, 128, 128), np.float32)
    s_conv = np.zeros((1, DEC, 3, 4096), np.float32)
    s_pool = np.zeros((1, DEC, 15, D), np.float32)
    for i in range(NCORES):
        r = R[i]
        sl = slice(i * NS, (i + 1) * NS)
        g = lambda k: np.asarray(r[k], dtype=np.float32)
        y_prompt[i] = g("yT").T
        y_sample[sl, 0, :] = g("ysT").T
        p_S[0, i] = g("pS")
        p_conv[0, i] = g("pconvT").reshape(128, 32, 3).transpose(2, 1, 0).reshape(3, 4096)
        p_pool[0, i] = g("ppoolT").reshape(128, 8, 15).transpose(2, 1, 0).reshape(15, D)
        s_S[0, sl] = g("sS")
        s_conv[0, sl, 0:2] = g("sconv_shift")
        s_conv[0, sl, 2] = g("sconv_newT").reshape(128, 32, NS).transpose(2, 1, 0).reshape(NS, 4096)
        s_pool[0, sl, 0:14] = g("spool_shift")
        s_pool[0, sl, 14] = g("spool_newT").reshape(128, 8, NS).transpose(2, 1, 0).reshape(NS, D)
    return (y_prompt, y_sample, p_S, p_conv, p_pool, s_S, s_conv, s_pool)
```

```python
import numpy as np
from contextlib import ExitStack
import concourse.bass as bass
import concourse.mybir as mybir
from concourse.bass_utils import run_bass_kernel_spmd

F32 = mybir.dt.float32
BF16 = mybir.dt.bfloat16
AF = mybir.ActivationFunctionType
ALU = mybir.AluOpType

D = 1024
SEQ = 2048
TB = 512
NBLK = SEQ // TB
NS = 16
KC = 8
DFF = 2816
FC = 22
NH = 16
IN_DIM = 6176
ALPHA = float(4.0 ** 0.25)
LN_EPS = 1e-5
NORM_EPS = 1e-6
NCORES = 8
C_ID, C_TRI, C_SU, C_ONE, C_RC = 0, 128, 256, 384, 512
NCONST = 576
BANKS_RC = (0, 1, 2)
BANKS_PD = (3, 4, 5)
BANKS_PJ = (6, 7)


class Prog:
    ENGS = ("pe", "act", "dve", "pool", "sp")
    WINDOW = 700
    SEM_LAT = 0.25

    def __init__(self, nc):
        self.nc = nc
        self.ops = []
        self.last_w = {}
        self.readers = {}
        self.schedule = True
        self.tag = ""

    def add(self, eng, fn, reads=(), writes=(), dma=None, cost=0.3, tbl=None, xfer=0.0):
        i = len(self.ops)
        deps = set()
        why = {}
        for r in reads:
            j = self.last_w.get(r)
            if j is not None:
                deps.add(j)
                why.setdefault(j, ("RAW", r))
        for w in writes:
            j = self.last_w.get(w)
            if j is not None:
                deps.add(j)
                why.setdefault(j, ("WAW", w))
            for j in self.readers.get(w, ()):
                deps.add(j)
                why.setdefault(j, ("WAR", w))
        self.why = getattr(self, "why", {})
        self.why[i] = why
        for r in reads:
            self.readers.setdefault(r, []).append(i)
        for w in writes:
            self.last_w[w] = i
            self.readers[w] = []
        deps.discard(i)
        self.ops.append(dict(eng=eng, fn=fn, deps=deps, dma=dma, sig=False, cost=cost, tbl=tbl, xfer=xfer, tag=getattr(self, "tag", "")))
        return i

    def _list_schedule(self):
        import heapq
        ops = self.ops
        n = len(ops)
        succ = [[] for _ in range(n)]
        indeg = [0] * n
        for i, o in enumerate(ops):
            indeg[i] = len(o["deps"])
            for j in o["deps"]:
                succ[j].append(i)
        prio = [0.0] * n
        for i in range(n - 1, -1, -1):
            m = 0.0
            for s in succ[i]:
                if prio[s] > m:
                    m = prio[s]
            prio[i] = m + ops[i]["cost"] + ops[i]["xfer"]
        fin = [0.0] * n
        rt = [0.0] * n
        free = {e: 0.0 for e in self.ENGS}
        ready = {e: [] for e in self.ENGS}
        pending = {e: [] for e in self.ENGS}
        scheduled = [False] * n
        order = {e: [] for e in self.ENGS}
        cur_tbl = [None]
        dma_free = [0.0]
        low = 0
        for i in range(n):
            if indeg[i] == 0:
                heapq.heappush(pending[ops[i]["eng"]], i)
        done = 0
        while done < n:
            while low < n and scheduled[low]:
                low += 1
            lim = low + self.WINDOW
            for e in self.ENGS:
                pe_ = pending[e]
                while pe_ and pe_[0] < lim:
                    ready[e].append(heapq.heappop(pe_))
            best = None
            for e in self.ENGS:
                rl = ready[e]
                if not rl:
                    continue
                tf = free[e]
                bi, bstart, bscore = None, None, None
                for i in rl:
                    stt_ = rt[i] if rt[i] > tf else tf
                    pen = 0.0
                    if e == "act" and ops[i]["tbl"] is not None and ops[i]["tbl"] != cur_tbl[0]:
                        pen = 0.5
                    key = (stt_ + pen, -prio[i], i)
                    if bscore is None or key < bscore:
                        bi, bstart, bscore = i, stt_ + pen, key
                if best is None or (bstart, bi) < (best[0], best[2]):
                    best = (bstart, e, bi)
            if best is None:
                raise RuntimeError("scheduler stuck")
            start, e, i = best
            o = ops[i]
            ready[e].remove(i)
            if e == "act" and o["tbl"] is not None:
                cur_tbl[0] = o["tbl"]
            end = start + o["cost"]
            free[e] = end
            if o["dma"] is not None:
                ds = end if end > dma_free[0] else dma_free[0]
                dma_free[0] = ds + o["xfer"]
                fin[i] = dma_free[0] + 1.5
            else:
                fin[i] = end
            scheduled[i] = True
            order[e].append(i)
            done += 1
            for s in succ[i]:
                lat = 0.0 if (ops[s]["eng"] == "pe" and e == "pe" and o["dma"] is None) else self.SEM_LAT
                t = fin[i] + lat
                if t > rt[s]:
                    rt[s] = t
                indeg[s] -= 1
                if indeg[s] == 0:
                    heapq.heappush(pending[ops[s]["eng"]], s)
        self.sim_end = max(free.values())
        self.fin = fin
        return order

    def emit(self, sems, dma_sems):
        ops = self.ops
        if self.schedule:
            per_eng = self._list_schedule()
        else:
            per_eng = {e: [] for e in self.ENGS}
            for i, o in enumerate(ops):
                per_eng[o["eng"]].append(i)

        def is_pe(o):
            return o["eng"] == "pe" and o["dma"] is None
        for o in ops:
            o["wdeps"] = {j for j in o["deps"] if not (is_pe(o) and is_pe(ops[j]))}
            for j in o["wdeps"]:
                ops[j]["sig"] = True
        cnt = {}
        for e in self.ENGS:
            for i in per_eng[e]:
                o = ops[i]
                if o["dma"] is not None:
                    key = ("dma", o["dma"])
                    cnt[key] = cnt.get(key, 0) + 16
                    o["ticket"] = (key, cnt[key])
                elif o["sig"]:
                    key = ("eng", o["eng"])
                    cnt[key] = cnt.get(key, 0) + 1
                    o["ticket"] = (key, cnt[key])

        def semof(key):
            return dma_sems[key[1]] if key[0] == "dma" else sems[key[1]]

        nc = self.nc
        with nc.Block() as block:
            def body(eng_name):
                def f(eng):
                    known = {}
                    for i in per_eng[eng_name]:
                        o = ops[i]
                        need = {}
                        for j in o["wdeps"]:
                            key, val = ops[j]["ticket"]
                            if val > need.get(key, 0):
                                need[key] = val
                        for key, val in need.items():
                            if known.get(key, 0) >= val:
                                continue
                            eng.wait_ge(semof(key), val)
                            known[key] = val
                        ins = o["fn"](eng)
                        if o["dma"] is not None:
                            ins.then_inc(dma_sems[o["dma"]], 16)
                        elif o["sig"]:
                            ins.then_inc(sems[eng_name], 1)
                    if eng_name == "sp":
                        for key, val in cnt.items():
                            if key[0] == "dma":
                                eng.wait_ge(semof(key), val)
                return f

            block.tensor(body("pe"))
            block.scalar(body("act"))
            block.vector(body("dve"))
            block.gpsimd(body("pool"))
            block.sync(body("sp"))


def build_program():
    nc = bass.Bass("TRN2", target_bir_lowering=False)
    es = ExitStack()

    def din(name, shape, dt=F32):
        return nc.dram_tensor(name, list(shape), dt, kind="ExternalInput").ap()

    def dout(name, shape, dt=F32):
        return nc.dram_tensor(name, list(shape), dt, kind="ExternalOutput").ap()

    xT = din("xT", [D, SEQ])
    xsT = din("xsT", [D, NS])
    cT = din("cT", [D, 1 + NS])
    S_in = din("S_in", [NS, NH, 128, 128])
    sconv_nat = din("sconv_nat", [NS, 3, 4096])
    sconvT = din("sconvT", [128, 32 * 3 * NS])
    spool_nat = din("spool_nat", [NS, 15, D])
    spoolT = din("spoolT", [128, KC * NS * 15])
    ada_w = din("ada_w", [2, D, 6 * D])
    ada_bT = din("ada_bT", [128, 2 * 48])
    ln_gT = din("ln_gT", [128, 32])
    ln_bT = din("ln_bT", [128, 32])
    w_in = din("w_in", [D, IN_DIM])
    convwT = din("convwT", [128, 32 * 4])
    alog_b = din("alog_b", [128, NH])
    dtb_b = din("dtb_b", [128, NH])
    normw_c = din("normw_c", [128, 1])
    w_out = din("w_out", [2048, D])
    pool_w = din("pool_w", [4, 256, 256])
    pscaleT = din("pscaleT", [128, KC])
    w_up = din("w_up", [2, D, 2 * DFF])
    w_down = din("w_down", [2, DFF, D])
    consts_d = din("consts", [128, NCONST])

    yT_o = dout("yT", [D, SEQ])
    ysT_o = dout("ysT", [D, NS])
    pS_o = dout("pS", [NH, 128, 128])
    pconv_o = dout("pconvT", [128, 96])
    ppool_o = dout("ppoolT", [128, KC * 15])
    sS_o = dout("sS", [NS, NH, 128, 128])
    sconv_shift_o = dout("sconv_shift", [NS, 2, 4096])
    sconv_new_o = dout("sconv_newT", [128, 32 * NS])
    spool_shift_o = dout("spool_shift", [NS, 14, D])
    spool_new_o = dout("spool_newT", [128, KC * NS])

    def sb(name, shape, dt=F32):
        return es.enter_context(nc.sbuf_tensor(name, list(shape), dt))

    def psum(name, shape, dt=F32):
        return es.enter_context(nc.psum_tensor(name, list(shape), dt))

    y = sb("y", [128, KC, SEQ])
    ys = sb("ys", [128, KC, NS])
    cst = sb("cst", [128, NCONST])
    id_bf = sb("id_bf", [128, 128], BF16)
    one_bf = sb("one_bf", [128, 128], BF16)
    mod = sb("mod", [128, 2, 6, KC, 1 + NS])
    csil = sb("csil", [128, KC, 1 + NS], BF16)
    lng = sb("lng", [128, 2, 2, KC])
    lnb = sb("lnb", [128, 2, 2, KC])
    lngA = sb("lngA", [128, 2, 2, KC])
    lnbA = sb("lnbA", [128, 2, 2, KC])
    epst = sb("epst", [128, 2])
    convw = sb("convw", [128, 32, 4])
    ccarry = sb("ccarry", [128, 32, 3])
    pcarry = sb("pcarry", [128, KC, 15])
    wba = sb("wba", [128, KC, 32], BF16)
    alog = sb("alog", [128, NH])
    dtb = sb("dtb", [128, NH])
    nexpA = sb("nexpA", [128, NH])
    normw = sb("normw", [128, 1])
    pscale = sb("pscale", [128, KC])
    cgate = sb("cgate", [128, KC, 1 + NS])
    S = sb("S", [128, NH, 128])
    Sb = sb("Sb", [128, NH, 128], BF16)
    ws = [sb(f"ws{i}", [128, 4096], BF16) for i in range(2)]
    u = sb("u", [128, KC, TB], BF16)
    us = sb("us", [128, KC, NS], BF16)
    uf = sb("uf", [128, KC, TB], BF16)
    big = sb("big", [128, FC, TB], BF16)
    bigs = sb("bigs", [128, FC, NS], BF16)
    Qt = sb("Qt", [128, 2, TB], BF16)
    Kt = sb("Kt", [128, 2, TB], BF16)
    Vt = sb("Vt", [128, 4, TB], BF16)
    sz = sb("sz", [128, 4, TB], BF16)
    Ktok = sb("Ktok", [128, 4, 2, 128], BF16)
    Vtok = sb("Vtok", [128, 4, 4, 128], BF16)
    beta = sb("beta", [128, 4, NH])
    nbeta = sb("nbeta", [128, 4, NH])
    gtok = sb("gtok", [128, 4, NH])
    ctmp = sb("ctmp", [128, 8, NH])
    FW = 528
    F_all = sb("F_all", [128, 9, FW])
    H_all = sb("H_all", [128, 23, 512], BF16)
    Fflat = F_all[:].rearrange("p a b -> p (a b)")
    Hf = H_all[:].rearrange("p a b -> p (a b)").bitcast(F32)

    class _T:
        def __init__(self, ap):
            self.ap = ap
        def __getitem__(self, k):
            return self.ap[k]

    Fs = [_T(F_all[:, i, 0:512]) for i in range(9)]
    Fw = [_T(F_all[:, i, :]) for i in range(9)]
    Hs = [_T(H_all[:, i, :]) for i in range(23)]
    pre = Fw[4]
    craw = F_all[:, 7, 0:KC * (1 + NS)].rearrange("p (a b) -> p a b", a=KC)
    adab = F_all[:, 8, 0:96].rearrange("p (a b) -> p a b", a=2)
    u1s = F_all[:, 5, 0:KC * NS].rearrange("p (a b) -> p a b", a=KC)
    pls = F_all[:, 6, 0:KC * NS].rearrange("p (a b) -> p a b", a=KC)
    u1 = Fw[3]
    pres = sb("pres", [128, 48, NS])
    sconv = Fflat[:, 0:1536].rearrange("p (a b c) -> p a b c", a=32, b=3)
    spool = Fflat[:, 0:1920].rearrange("p (a b c) -> p a b c", a=KC, b=NS)
    vtoks = Fflat[0:NS, 0:2048].rearrange("p (a b) -> p a b", a=NH)
    vmask = Fflat[0:NS, 4 * FW:4 * FW + 2048].rearrange("p (a b) -> p a b", a=NH)
    ktoks = Hf[0:NS, 0:1024].rearrange("p (a b) -> p a b", a=8)
    sexp = Hf[0:NS, 1024:1536].rearrange("p (q a b) -> p q a b", q=2, a=NS)
    kqs = Hf[:, 1536:2048].rearrange("p (b h t) -> p b h t", b=NS, h=NH)
    qkvs = Hf[:, 2048:2560].rearrange("p (a b) -> p a b", a=32)
    szs = Hf[:, 2560:2816].rearrange("p (a b) -> p a b", a=16)
    bas = Hf[0:NS, 2816:2848]
    KF03 = tuple(("F", i) for i in range(4))
    KF47 = tuple(("F", i) for i in range(4, 8))
    K_sconv = tuple(("F", i) for i in range(3))
    K_ktoks = tuple(("H", i) for i in range(4))
    K_sexp = (("H", 4), ("H", 5))
    K_kqs = (("H", 6), ("H", 7))
    K_qkvs = (("H", 8), ("H", 9))
    K_szs = (("H", 10),)
    K_bas = (("H", 11),)
    sm = u[:].rearrange("p a b -> p (a b)").bitcast(F32).rearrange("p (a b) -> p a b", a=8)

    PS = [psum(f"ps{i}", [128, 512]) for i in range(8)]
    PSB = [_T(PS[i][:].bitcast(BF16)) for i in range(8)]

    sems = {e: es.enter_context(nc.semaphore(f"s_{e}")) for e in Prog.ENGS}
    dsem = {}

    def stream(name):
        if name not in dsem:
            dsem[name] = es.enter_context(nc.semaphore(f"d_{name}"))
        return name

    P = Prog(nc)
    st = dict(ps=0, pb=0, wsl=0, c=0)

    def _next_bank():
        bs = st.get("bankset")
        if bs == "proj":
            i = BANKS_PJ[st.setdefault("pj", 0) % len(BANKS_PJ)]
            st["pj"] += 1
        elif bs == "sample":
            i = (0, 1, 2)[st.setdefault("sm_", 0) % 3]
            st["sm_"] += 1
        elif bs == "main5":
            i = (3, 4, 5, 6, 7)[st.setdefault("m5", 0) % 5]
            st["m5"] += 1
        else:
            i = st["ps"] % 8
            st["ps"] += 1
        return i

    def bank():
        i = _next_bank()
        return PS[i], ("ps", i)

    def bbank():
        i = _next_bank()
        return PSB[i], ("ps", i)

    def _fs(ap):
        n = 1
        for d in ap.shape[1:]:
            n *= d
        return n

    _TBL = {AF.Silu: "silu", AF.Exp: "explog", AF.Ln: "explog"}

    def act(out, in_, func, r, w, bias=0.0, scale=1.0):
        P.add("act", lambda e: e.activation(out=out, in_=in_, func=func, bias=bias, scale=scale), r, w,
              cost=0.12 + _fs(in_) * 0.00112, tbl=_TBL.get(func))

    def tt(eng, out, in0, in1, op, r, w):
        c = (0.12 + _fs(in0) * 0.00115) if eng == "dve" else (0.2 + _fs(in0) * 0.0021)
        P.add(eng, lambda e: e.tensor_tensor(out=out, in0=in0, in1=in1, op=op), r, w, cost=c)

    def tsc(eng, out, in0, s1, s2, op0, op1, r, w):
        c = (0.12 + _fs(in0) * 0.00115) if eng == "dve" else (0.2 + _fs(in0) * 0.004)
        P.add(eng, lambda e: e.tensor_scalar(out=out, in0=in0, scalar1=s1, scalar2=s2, op0=op0, op1=op1), r, w, cost=c)

    def stt(out, in0, scalar, in1, op0, op1, r, w):
        P.add("dve", lambda e: e.scalar_tensor_tensor(out=out, in0=in0, scalar=scalar, in1=in1, op0=op0, op1=op1), r, w,
              cost=0.12 + _fs(in0) * 0.00125)

    def cp(eng, out, in_, r, w):
        if eng == "act":
            P.add("act", lambda e: e.activation(out=out, in_=in_, func=AF.Copy), r, w, cost=0.12 + _fs(in_) * 0.00112)
        else:
            c = (0.12 + _fs(in_) * 0.00115) if eng == "dve" else (0.2 + _fs(in_) * 0.0021)
            P.add(eng, lambda e: e.tensor_copy(out=out, in_=in_), r, w, cost=c)

    def mm(out, lhsT, rhs, r, w, start=True, stop=True):
        nfree = max(_fs(rhs), 64)
        c = 0.03 + nfree * (4 if rhs.dtype == F32 else 1) / 2400.0
        P.add("pe", lambda e: e.matmul(out, lhsT, rhs, start=start, stop=stop), r, w, cost=c)

    def tr(out, in_, ident, r, w):
        P.add("pe", lambda e: e.transpose(out, in_, ident), r, w, cost=0.1)

    def dma(eng, out, in_, strm, r, w):
        nbytes = 128 * _fs(in_) * 4 if in_.shape[0] == 128 else in_.shape[0] * _fs(in_) * 4
        P.add(eng, lambda e: e.dma_start(out=out, in_=in_), r, w, dma=stream(strm),
              cost=(1.7 if eng == "pool" else 0.15), xfer=nbytes / 330e3)

    def setup_dma(out, in_, w):
        st["c"] += 1
        dma("sp", out, in_, f"c{st['c']}", (), w)

    def memset(eng, ap, val, w):
        P.add(eng, lambda e: e.memset(ap, val), (), w)

    IDf = cst[:, C_ID:C_ID + 128]
    TRI = cst[:, C_TRI:C_TRI + 128]
    SU = cst[:, C_SU:C_SU + 128]
    ONEf = cst[:, C_ONE:C_ONE + 128]
    KCST = ("cst",)

    def b4(ap2d):
        return ap2d.unsqueeze(1).broadcast_to([128, 4, 128])

    def v4(t):
        return t[:].rearrange("p (a b) -> p a b", a=4)

    def hsl_(hh):
        return slice(hh * 128, (hh + 1) * 128)

    def wload(parts):
        s = st["wsl"] % 2
        st["wsl"] += 1
        t = ws[s]
        keys = []
        for pi, (dst_fn, src) in enumerate(parts):
            key = ("ws", s, pi)
            keys.append(key)
            dma("pool", dst_fn(t), src, f"ws{s}_{pi}", (), (key,))
        for pi in range(len(parts), 2):
            keys.append(("ws", s, pi))
        return t, tuple(keys)

    def wv(t, kc, cols, off=0):
        return t[:, off:off + kc * cols].rearrange("p (k c) -> p k c", k=kc)

    setup_dma(cst[:], consts_d, (KCST,))
    setup_dma(craw[:], cT.rearrange("(k p) n -> p k n", p=128), (("F", 7),))
    setup_dma(adab[:].rearrange("p a b -> p (a b)"), ada_bT, (("F", 8),))
    setup_dma(lng[:].rearrange("p a b c -> p (a b c)"), ln_gT, (("lng",),))
    setup_dma(lnb[:].rearrange("p a b c -> p (a b c)"), ln_bT, (("lnb",),))
    setup_dma(convw[:].rearrange("p a b -> p (a b)"), convwT, (("convw",),))
    setup_dma(alog[:], alog_b, (("alog",),))
    setup_dma(dtb[:], dtb_b, (("dtb",),))
    setup_dma(normw[:], normw_c, (("normw",),))
    setup_dma(pscale[:], pscaleT, (("pscale",),))
    def load_x_block(b):
        for k in range(KC):
            dma("sp", y[:, k, b * TB:(b + 1) * TB], xT[k * 128:(k + 1) * 128, b * TB:(b + 1) * TB], f"x{k}", (), (("y", k, b),))
        for k in range(KC):
            tsc("pool", y[:, k, b * TB:(b + 1) * TB], y[:, k, b * TB:(b + 1) * TB], ALPHA, 1.0, ALU.mult, ALU.mult, (("y", k, b),), (("y", k, b),))
    load_x_block(0)
    setup_dma(ys[:], xsT.rearrange("(k p) n -> p k n", p=128), tuple(("ys", k) for k in range(KC)))
    dma("pool", wba[:], w_in.rearrange("(k p) n -> p k n", p=128)[:, :, 6144:6176], "misc", (), (("wba",),))
    dma("sp", sconv_shift_o, sconv_nat[:, 1:3, :], "out", (), ())
    dma("sp", spool_shift_o, spool_nat[:, 1:15, :], "out", (), ())

    tsc("dve", lngA[:], lng[:], ALPHA, None, ALU.mult, ALU.bypass, (("lng",),), (("lngA",),))
    tsc("dve", lnbA[:], lnb[:], ALPHA, None, ALU.mult, ALU.bypass, (("lnb",),), (("lnbA",),))
    memset("dve", epst[:, 0:1], LN_EPS, (("epst",),))
    memset("dve", epst[:, 1:2], NORM_EPS, (("epst",),))
    tsc("pool", ys[:], ys[:], ALPHA, 1.0, ALU.mult, ALU.mult, tuple(("ys", k) for k in range(KC)), tuple(("ys", k) for k in range(KC)))
    for l_ in range(2):
        for kind_ in (1, 4):
            tsc("dve", adab[:, l_, kind_ * 8:(kind_ + 1) * 8], adab[:, l_, kind_ * 8:(kind_ + 1) * 8], 1.0, 1.0 / ALPHA, ALU.add, ALU.mult, (("F", 8),), (("F", 8),))
    cp("dve", id_bf[:], IDf, (KCST,), (("id_bf",),))
    cp("dve", one_bf[:], ONEf, (KCST,), (("one_bf",),))
    memset("dve", ccarry[:], 0.0, tuple(("ccarry", c) for c in range(32)))
    memset("dve", pcarry[:], 0.0, tuple(("pcarry", k) for k in range(KC)))
    memset("dve", S[:], 0.0, tuple(("S", h) for h in range(NH)))
    memset("dve", Sb[:], 0.0, tuple(("Sb", h) for h in range(NH)))
    act(nexpA[:], alog[:], AF.Exp, (("alog",),), (("nexpA",),))
    tsc("dve", nexpA[:], nexpA[:], -1.0, None, ALU.mult, ALU.bypass, (("nexpA",),), (("nexpA",),))
    act(csil[:], craw[:], AF.Silu, (("F", 7),), (("csil",),))

    P.tag = "mod"

    for l_ in range(2):
        for kind_ in range(6):
            cp("dve", mod[:, l_, kind_, :, :], adab[:, l_, kind_ * 8:(kind_ + 1) * 8].unsqueeze(2).broadcast_to([128, 8, 1 + NS]),
               (("F", 8),), tuple(("mod", l_, kind_, ec) for ec in range(8)))

    def mod_piece(l, pc):
        tag0 = P.tag
        P.tag = "mod"
        awl = ada_w[l].rearrange("(k p) n -> p k n", p=128)
        t, wk = wload([(lambda t: wv(t, KC, 512), awl[:, :, pc * 512:(pc + 1) * 512])])
        wvv = wv(t, KC, 512)
        kind = pc // 2
        bs_ = st.get("bankset")
        st["bankset"] = "all"
        pt, pk = bank()
        st["bankset"] = bs_
        for e4 in range(4):
            for k in range(KC):
                mm(pt[:, e4 * 17:(e4 + 1) * 17], wvv[:, k, e4 * 128:(e4 + 1) * 128], csil[:, k, :],
                   wk + (("csil",),), (pk,), start=(k == 0), stop=(k == KC - 1))
        s_ = (1.0 / ALPHA) if kind in (1, 4) else 1.0
        for e4 in range(4):
            ec = (pc % 2) * 4 + e4
            stt(mod[:, l, kind, ec, :], pt[:, e4 * 17:(e4 + 1) * 17], s_, mod[:, l, kind, ec, :], ALU.mult, ALU.add,
                (pk, ("mod", l, kind, ec)), (("mod", l, kind, ec),))
        P.tag = tag0

    mod_queue = [(0, pc) for pc in range(4, 12)] + [(1, pc) for pc in range(12)]
    for pc in range(4):
        mod_piece(0, pc)

    def mod_emit(n):
        for _ in range(n):
            if mod_queue:
                mod_piece(*mod_queue.pop(0))

    def modkeys(l):
        return tuple(("mod", l, kind, ec) for kind in range(6) for ec in range(8))

    def mod_ap(l, kind, k, samp):
        if samp:
            return mod[:, l, kind, k, 1:1 + NS]
        return mod[:, l, kind, k, 0:1]

    def ykey(k, blk):
        return ("ys", k) if blk == "s" else ("y", k, blk)

    def yap(k, blk):
        return ys[:, k, :] if blk == "s" else y[:, k, blk * TB:(blk + 1) * TB]

    def modulate(l, sub, blk, out_fn, out_key_fn):
        ksc, ksh = (1, 0) if sub == 0 else (4, 3)
        samp = blk == "s"
        for k in range(KC):
            mk = (("mod", l, ksc, k), ("mod", l, ksh, k))
            if samp:
                tt("dve", Fs[7][:, 0:NS], yap(k, blk), mod_ap(l, ksc, k, True), ALU.mult, (ykey(k, blk),) + mk, (("F", 7),))
                tt("dve", out_fn(k), Fs[7][:, 0:NS], mod_ap(l, ksh, k, True), ALU.add, (("F", 7),) + mk, (out_key_fn(k),))
            else:
                act(out_fn(k), yap(k, blk), AF.Identity, (ykey(k, blk),) + mk, (out_key_fn(k),),
                    bias=mod_ap(l, ksh, k, False), scale=mod_ap(l, ksc, k, False))

    def residual(l, sub, blk, k, pt, pk, pool=False):
        kg = 2 if sub == 0 else 5
        samp = blk == "s"
        N = NS if samp else TB
        if pool:
            gate = cgate[:, k, 1:1 + NS] if samp else cgate[:, k, 0:1]
            gk = ("cgate",)
        else:
            gate = mod_ap(l, kg, k, samp)
            gk = ("mod", l, kg, k)
        if samp:
            tt("dve", Fs[7][:, 0:NS], pt[:, 0:NS], gate, ALU.mult, (pk, gk), (("F", 7),))
            tt("dve", yap(k, blk), yap(k, blk), Fs[7][:, 0:NS], ALU.add, (ykey(k, blk), ("F", 7)), (ykey(k, blk),))
        else:
            stt(yap(k, blk), pt[:, 0:N], gate, yap(k, blk), ALU.mult, ALU.add, (pk, gk, ykey(k, blk)), (ykey(k, blk),))

    def layer_norm(l, i, blk):
        P.tag = "ln"
        samp = blk == "s"
        N = NS if samp else TB
        psum_t, psk = bank()
        psq_t, pqk = bank()
        for k in range(KC):
            mm(psum_t[:, 0:N], ONEf, yap(k, blk), (KCST, ykey(k, blk)), (psk,), start=(k == 0), stop=(k == KC - 1))
        for k in range(KC):
            h = Hs[6 + (k % 2)]
            hk = ("H", 6 + (k % 2))
            act(h[:, 0:N], yap(k, blk), AF.Square, (ykey(k, blk),), (hk,))
            mm(psq_t[:, 0:N], one_bf[:], h[:, 0:N], (("one_bf",), hk), (pqk,), start=(k == 0), stop=(k == KC - 1))
        m, msq, var, mr = Fs[0], Fs[1], Fs[2], Fs[3]
        act(m[:, 0:N], psum_t[:, 0:N], AF.Copy, (psk,), (("F", 0),), scale=1.0 / D)
        tt("dve", msq[:, 0:N], m[:, 0:N], m[:, 0:N], ALU.mult, (("F", 0),), (("F", 1),))
        stt(var[:, 0:N], psq_t[:, 0:N], 1.0 / D, msq[:, 0:N], ALU.mult, ALU.subtract, (pqk, ("F", 1)), (("F", 2),))
        act(var[:, 0:N], var[:, 0:N], AF.Ln, (("F", 2), ("epst",)), (("F", 2),), bias=epst[:, 0:1])
        act(var[:, 0:N], var[:, 0:N], AF.Exp, (("F", 2),), (("F", 2),), scale=-0.5)
        tt("dve", mr[:, 0:N], m[:, 0:N], var[:, 0:N], ALU.mult, (("F", 0), ("F", 2)), (("F", 3),))
        for k in range(KC):
            f = Fs[4 + (k % 2)]
            fk = ("F", 4 + (k % 2))
            tt("dve", f[:, 0:N], yap(k, blk), var[:, 0:N], ALU.mult, (ykey(k, blk), ("F", 2)), (fk,))
            tt("dve", f[:, 0:N], f[:, 0:N], mr[:, 0:N], ALU.subtract, (fk, ("F", 3)), (fk,))
            fin = (l == 1 and i == 1)
            gt, bt = (lng, lnb) if fin else (lngA, lnbA)
            act(yap(k, blk), f[:, 0:N], AF.Identity, (fk, ("lng",), ("lnb",), ("lngA",), ("lnbA",)), (ykey(k, blk),),
                bias=bt[:, l, i, k:k + 1], scale=gt[:, l, i, k:k + 1])

    def blkinfo(blk, ffn_in=False):
        samp = blk == "s"
        N = NS if samp else TB
        ub = us if samp else (uf if ffn_in else u)
        uk = (lambda k: ("us", k)) if samp else ((lambda k: ("uf", k)) if ffn_in else (lambda k: ("u", k)))
        bg = bigs if samp else big
        bk = (lambda c: ("bigs", c)) if samp else (lambda c: ("big", c))
        return N, ub, uk, bg, bk

    def open_acc(blks, ndc):
        acc = {}
        for blk in blks:
            if blk == "s":
                acc[blk] = [bank()]
            else:
                acc[blk] = [bank() for _ in range(ndc)]
        return acc

    def acc_mm(acc, blk, dl, lhsT, rhs, r, first, last, st_first):
        if blk == "s":
            pt, pk = acc[blk][0]
            out = pt[:, dl * NS:(dl + 1) * NS]
            P.add("pe", lambda e: e.matmul(out, lhsT, rhs, start=st_first, stop=last, skip_group_check=True), r, (pk,), cost=0.06)
        else:
            pt, pk = acc[blk][dl]
            mm(pt[:, 0:TB], lhsT, rhs, r, (pk,), start=first, stop=last)

    def acc_out(acc, blk, dl):
        if blk == "s":
            pt, pk = acc[blk][0]
            return _T(pt[:, dl * NS:(dl + 1) * NS]), pk
        pt, pk = acc[blk][dl]
        return pt, pk

    def ffn(l, blks):
        P.tag = "ffn"
        wup = w_up[l].rearrange("(k p) n -> p k n", p=128)
        wdn = w_down[l].rearrange("(k p) n -> p k n", p=128)
        for blk in blks:
            N, ub, uk, bg, bk = blkinfo(blk, True)
            modulate(l, 1, blk, lambda k, ub=ub, N=N: ub[:, k, 0:N], uk)
        for pc in range(6):
            nf = 4 if pc < 5 else 2
            cols = nf * 128
            for half in range(2):
                t, wk = wload([(lambda t, cols=cols: wv(t, KC, cols), wup[:, :, half * DFF + pc * 512:half * DFF + pc * 512 + cols])])
                wvv = wv(t, KC, cols)
                for blk in blks:
                    N, ub, uk, bg, bk = blkinfo(blk, True)
                    for f in range(nf):
                        fc = pc * 4 + f
                        pg, pgk = bank()
                        for k in range(KC):
                            mm(pg[:, 0:N], wvv[:, k, f * 128:(f + 1) * 128], ub[:, k, 0:N], wk + (uk(k),), (pgk,), start=(k == 0), stop=(k == KC - 1))
                        if blk == "s":
                            tmp, tk = Fs[4][:, f * NS:(f + 1) * NS], ("F", 4)
                        else:
                            tmp, tk = Fs[f][:, 0:N], ("F", f)
                        if half == 0:
                            act(tmp, pg[:, 0:N], AF.Silu, (pgk,), (tk,))
                        else:
                            tt("dve", bg[:, fc, 0:N], tmp, pg[:, 0:N], ALU.mult, (tk, pgk), (bk(fc),))
        fgs = [(0, 8), (8, 16), (16, 22)]
        for ch in range(2):
            acc = open_acc(blks, 4)
            first_s = True
            for gi, (f0, f1) in enumerate(fgs):
                nfc = f1 - f0
                t, wk = wload([(lambda t, nfc=nfc: wv(t, nfc, 512), wdn[:, f0:f1, ch * 512:(ch + 1) * 512])])
                wvv = wv(t, nfc, 512)
                for blk in blks:
                    N, ub, uk, bg, bk = blkinfo(blk)
                    for dl in range(4):
                        for fc in range(f0, f1):
                            acc_mm(acc, blk, dl, wvv[:, fc - f0, dl * 128:(dl + 1) * 128], bg[:, fc, 0:N], wk + (bk(fc),),
                                   first=(fc == 0), last=(fc == FC - 1), st_first=(blk == "s" and first_s))
                            if blk == "s":
                                first_s = False
            for blk in blks:
                for dl in range(4):
                    pt, pk = acc_out(acc, blk, dl)
                    residual(l, 1, blk, ch * 4 + dl, pt, pk)
        for blk in blks:
            layer_norm(l, 1, blk)

    win = w_in.rearrange("(k p) n -> p k n", p=128)

    CONV_SETS = [((Fw[0], ("F", 0)), (Fs[1], ("F", 1)), (Fs[2], ("F", 2))),
                 ((Fw[3], ("F", 3)), (Fs[4], ("F", 4)), (Fs[5], ("F", 5)))]

    CONV_SET_C = ((Fw[6], ("F", 6)), (Fs[8], ("F", 8)), None)

    def conv_set(v=False):
        if v:
            i = st.setdefault("cvv", 0) % 3
            st["cvv"] += 1
            return [CONV_SET_C, CONV_SETS[0], CONV_SETS[1]][i]
        i = st.setdefault("cv", 0) % 2
        st["cv"] += 1
        return CONV_SETS[i]

    def conv_silu(c, out_ap, out_key, pt, pk, cset):
        (pre_, prek), (acc, acck), _ = cset
        cp("act", pre_[:, 3:3 + TB], pt[:, 0:TB], (pk,), (prek,))
        cp("pool", pre_[:, 0:3], ccarry[:, c, :], (("ccarry", c),), (prek,))
        cp("pool", ccarry[:, c, :], pre_[:, TB:TB + 3], (prek,), (("ccarry", c),))
        tsc("dve", acc[:, 0:TB], pre_[:, 0:TB], convw[:, c, 0:1], None, ALU.mult, ALU.bypass, (prek, ("convw",)), (acck,))
        for j in range(1, 4):
            stt(acc[:, 0:TB], pre_[:, j:j + TB], convw[:, c, j:j + 1], acc[:, 0:TB], ALU.mult, ALU.add,
                (prek, ("convw",), acck), (acck,))
        if out_ap is None:
            act(acc[:, 0:TB], acc[:, 0:TB], AF.Silu, (acck,), (acck,))
        else:
            act(out_ap, acc[:, 0:TB], AF.Silu, (acck,), (out_key,))

    def l2norm_set(cset, out_ap, out_key, scale):
        (pre_, prek), (acc, acck), (sq, sqk) = cset
        act(sq[:, 0:TB], acc[:, 0:TB], AF.Square, (acck,), (sqk,))
        pt, pk = bank()
        mm(pt[:, 0:TB], ONEf, sq[:, 0:TB], (KCST, sqk), (pk,))
        act(pre_[:, 0:TB], pt[:, 0:TB], AF.Ln, (pk, ("epst",)), (prek,), bias=epst[:, 1:2])
        act(pre_[:, 0:TB], pre_[:, 0:TB], AF.Exp, (prek,), (prek,), scale=-0.5)
        stt(out_ap, acc[:, 0:TB], scale, pre_[:, 0:TB], ALU.mult, ALU.mult, (acck, prek), (out_key,))

    def gchunk_id(g, ci):
        if ci < 2:
            return g * 2 + ci
        if ci < 4:
            return 8 + g * 2 + (ci - 2)
        return 16 + g * 4 + (ci - 4)

    def gdn_prompt_block(blk):
        P.tag = "gdn_proj"
        st["bankset"] = "proj"
        modulate(0, 0, blk, lambda k: u[:, k, :], lambda k: ("u", k))
        for n in range(4):
            pt, pk = bank()
            for k in range(KC):
                mm(pt[:, 0:32], u[:, k, n * 128:(n + 1) * 128], wba[:, k, :], (("u", k), ("wba",)), (pk,), start=(k == 0), stop=(k == KC - 1))
            act(ctmp[:, 0, :], pt[:, 0:16], AF.Exp, (pk,), (("ctmp", 0),), scale=-1.0)
            tsc("dve", ctmp[:, 0, :], ctmp[:, 0, :], 1.0, None, ALU.add, ALU.bypass, (("ctmp", 0),), (("ctmp", 0),))
            P.add("dve", lambda e, n=n: e.reciprocal(out=beta[:, n, :], in_=ctmp[:, 0, :]), (("ctmp", 0),), (("beta", n),))
            tsc("dve", nbeta[:, n, :], beta[:, n, :], -1.0, None, ALU.mult, ALU.bypass, (("beta", n),), (("nbeta", n),))
            tt("dve", ctmp[:, 1, :], pt[:, 16:32], dtb[:], ALU.add, (pk, ("dtb",), ("ctmp", 0)), (("ctmp", 1),))
            act(ctmp[:, 1, :], ctmp[:, 1, :], AF.Exp, (("ctmp", 1),), (("ctmp", 1),))
            tsc("dve", ctmp[:, 1, :], ctmp[:, 1, :], 1.0, None, ALU.add, ALU.bypass, (("ctmp", 1),), (("ctmp", 1),))
            act(ctmp[:, 1, :], ctmp[:, 1, :], AF.Ln, (("ctmp", 1),), (("ctmp", 1),))
            tt("dve", gtok[:, n, :], ctmp[:, 1, :], nexpA[:], ALU.mult, (("ctmp", 1), ("nexpA",)), (("gtok", n),))

        for g in range(4):
            tA, wkA = wload([
                (lambda t: wv(t, KC, 512)[:, :, 0:256], win[:, :, g * 256:(g + 1) * 256]),
                (lambda t: wv(t, KC, 512)[:, :, 256:512], win[:, :, 1024 + g * 256:1024 + (g + 1) * 256]),
            ])
            wA = wv(tA, KC, 512)

            def proj(wvv, wk, cl, rhs_fn, rkey_fn, N):
                pt, pk = bank()
                for k in range(KC):
                    mm(pt[:, 0:N], wvv[:, k, cl * 128:(cl + 1) * 128], rhs_fn(k), wk + (rkey_fn(k),), (pk,), start=(k == 0), stop=(k == KC - 1))
                return pt, pk

            pr_u = (lambda k: u[:, k, :], lambda k: ("u", k), TB)
            pr_s = (lambda k: us[:, k, :], lambda k: ("us", k), NS)
            for ci in range(4):
                pt, pk = proj(wA, wkA, ci, *pr_u)
                cset = conv_set()
                conv_silu(gchunk_id(g, ci), None, None, pt, pk, cset)
                if ci < 2:
                    l2norm_set(cset, Qt[:, ci, :], ("Qt", ci), 128.0 ** -0.5)
                else:
                    l2norm_set(cset, Kt[:, ci - 2, :], ("Kt", ci - 2), 1.0)
            if blk == 0:
                for ci in range(4):
                    gc = gchunk_id(g, ci)
                    pt, pk = proj(wA, wkA, ci, *pr_s)
                    cp("act", pres[:, gc, :], pt[:, 0:NS], (pk,), (("pres", gc),))
            tB, wkB = wload([(lambda t: wv(t, KC, 512), win[:, :, 2048 + g * 512:2048 + (g + 1) * 512])])
            wB = wv(tB, KC, 512)
            for ci in range(4):
                pt, pk = proj(wB, wkB, ci, *pr_u)
                conv_silu(gchunk_id(g, 4 + ci), Vt[:, ci, :], ("Vt", ci), pt, pk, conv_set(True))
            if blk == 0:
                for ci in range(4):
                    gc = gchunk_id(g, 4 + ci)
                    pt, pk = proj(wB, wkB, ci, *pr_s)
                    cp("act", pres[:, gc, :], pt[:, 0:NS], (pk,), (("pres", gc),))
            tC, wkC = wload([(lambda t: wv(t, KC, 512), win[:, :, 4096 + g * 512:4096 + (g + 1) * 512])])
            wC = wv(tC, KC, 512)
            for ci in range(4):
                pt, pk = proj(wC, wkC, ci, *pr_u)
                act(sz[:, ci, :], pt[:, 0:TB], AF.Silu, (pk,), (("sz", ci),))
            if blk == 0:
                for ci in range(4):
                    gc = 32 + g * 4 + ci
                    pt, pk = proj(wC, wkC, ci, *pr_s)
                    cp("act", pres[:, gc, :], pt[:, 0:NS], (pk,), (("pres", gc),))
            for n in range(4):
                pb, pbk = bbank()
                for kh in range(2):
                    tr(pb[:, kh * 128:(kh + 1) * 128], Kt[:, kh, n * 128:(n + 1) * 128], id_bf[:], (("Kt", kh), ("id_bf",)), (pbk,))
                for vh in range(4):
                    tr(pb[:, 256 + vh * 128:256 + (vh + 1) * 128], Vt[:, vh, n * 128:(n + 1) * 128], id_bf[:], (("Vt", vh), ("id_bf",)), (pbk,))
                cp("act", Ktok[:, n, :, :].rearrange("p a b -> p (a b)"), pb[:, 0:256], (pbk,), (("Ktok", n),))
                cp("act", Vtok[:, n, :, :].rearrange("p a b -> p (a b)"), pb[:, 256:768], (pbk,), (("Vtok", n),))
            gdn_group_chunks(g, blk)
            P.tag = "gdn_proj"
        st["bankset"] = "all"
        if blk < NBLK - 1:
            load_x_block(blk + 1)
        if blk == 0:
            mod_emit(2)
            P.tag = "sample_gdn"
            st["bankset"] = "sample"
            sample_gdn()
            st["bankset"] = "main5"
        P.tag = "wout"
        wo = w_out.rearrange("(k p) n -> p k n", p=128)
        blks2 = [blk, "s"] if blk == 1 else [blk]
        for ch in range(2):
            acc = open_acc(blks2, 4)
            first_s = True
            for vh in range(2):
                t, wk = wload([(lambda t: wv(t, 8, 512), wo[:, vh * 8:(vh + 1) * 8, ch * 512:(ch + 1) * 512])])
                wvv = wv(t, 8, 512)
                for blk2 in blks2:
                    N, ub, uk, bg, bk = blkinfo(blk2)
                    for dl in range(4):
                        for vc in range(vh * 8, vh * 8 + 8):
                            acc_mm(acc, blk2, dl, wvv[:, vc - vh * 8, dl * 128:(dl + 1) * 128], bg[:, vc, 0:N], wk + (bk(vc),),
                                   first=(vc == 0), last=(vc == 15), st_first=(blk2 == "s" and first_s))
                            if blk2 == "s":
                                first_s = False
            for blk2 in blks2:
                for dl in range(4):
                    pt, pk = acc_out(acc, blk2, dl)
                    residual(0, 0, blk2, ch * 4 + dl, pt, pk)
        if blk == 0:
            mod_emit(6)
        layer_norm(0, 0, blk)
        if blk == 1:
            layer_norm(0, 0, "s")

    def bank_pd():
        i = BANKS_PD[st.setdefault("pd", 0) % len(BANKS_PD)]
        st["pd"] += 1
        return PS[i], ("ps", i)

    def bank_rc():
        i = BANKS_RC[st.setdefault("rc", 0) % len(BANKS_RC)]
        st["rc"] += 1
        return PS[i], ("ps", i)

    def gdn_tiles(blk):
        if blk < NBLK - 1:
            return [(_T(y[:, k, (blk + 1) * TB:(blk + 2) * TB]), ("y", k, blk + 1)) for k in range(8)]
        return [(Fs[i], ("F", i)) for i in range(6)] + [(Fs[6], ("F", 6)), (Fs[8], ("F", 8))]

    def sethk(s):
        b = 9 + 4 * s
        return [(Hs[b + j], ("H", b + j)) for j in range(4)]

    def pd_stages(g, n, s, blk):
        cs = slice(n * 128, (n + 1) * 128)
        hs = slice(4 * g, 4 * g + 4)
        gcols = gtok[:, n, hs]
        gk = ("gtok", n)
        GT = gdn_tiles(blk)
        (gSL, k0), (gTRI, k1), (G1, k2), (DTm, k3), (Dm, k4), (eGr, k5) = GT[0:6]
        Vb, Vbk = GT[6 + s]
        (Tfin, Tfk), (attnT, atk), (Qd, Qdk), (kdec, kdk) = sethk(s)
        hb = 0 if s == 0 else 17
        A0, U0, P1, Q1, Ta, Tb_ = (Hs[hb + j] for j in range(6))
        hk_ = lambda j: ("H", hb + j)
        cN = ("ctmp", 3, s)
        cG = ("ctmp", 4, s)
        stages = []

        def prep():
            gb = gcols.unsqueeze(2).broadcast_to([128, 4, 128])
            tt("pool", v4(gSL), b4(SU), gb, ALU.mult, (KCST, gk), (k0,))
            tt("pool", v4(gTRI), b4(TRI), gb, ALU.mult, (KCST, gk), (k1,))
            tt("pool", v4(G1), b4(ONEf), gb, ALU.mult, (KCST, gk), (k2,))
            pA, pAk = bank_pd()
            pB, pBk = bank_pd()
            pC, pCk = bank_pd()
            for hh in range(4):
                hsl = hsl_(hh)
                mm(pA[:, hsl], gSL[:, hsl], TRI, (k0, KCST), (pAk,))
                mm(pB[:, hsl], gTRI[:, hsl], SU, (k1, KCST), (pBk,))
                mm(pC[:, hsl], G1[:, hsl], TRI, (k2, KCST), (pCk,))
            act(DTm[:], pA[:], AF.Exp, (pAk,), (k3,))
            tt("pool", v4(DTm), v4(DTm), b4(TRI), ALU.mult, (k3, KCST), (k3,))
            act(Dm[:], pB[:], AF.Exp, (pBk,), (k4,))
            tt("pool", v4(Dm), v4(Dm), b4(SU), ALU.mult, (k4, KCST), (k4,))
            act(eGr[:], pC[:], AF.Exp, (pCk,), (k5,))
            cp("pool", ctmp[:, 4, 4 * s:4 * s + 4], v4(eGr)[:, :, 127], (k5,), (cG,))
            pD, pDk = bank_pd()
            mm(pD[:, 0:4], TRI, gcols, (KCST, gk), (pDk,))
            mm(pD[:, 4:8], SU, gcols, (KCST, gk), (pDk,))
            act(ctmp[:, 2, 0:8], pD[:, 0:8], AF.Exp, (pDk,), (("ctmp", 2),))
            tt("dve", ctmp[:, 3, 4 * s:4 * s + 4], ctmp[:, 2, 0:4], nbeta[:, n, hs], ALU.mult, (("ctmp", 2), ("nbeta", n)), (cN,))
            pK, pKk = bank_pd()
            for kh in range(2):
                mm(pK[:, kh * 128:(kh + 1) * 128], Kt[:, kh, cs], Kt[:, kh, cs], (("Kt", kh),), (pKk,))
                mm(pK[:, 256 + kh * 128:256 + (kh + 1) * 128], Kt[:, kh, cs], Qt[:, kh, cs], (("Kt", kh), ("Qt", kh)), (pKk,))
            for hh in range(4):
                kh = hh // 2
                hsl = hsl_(hh)
                stt(A0[:, hsl], pK[:, kh * 128:(kh + 1) * 128], nbeta[:, n, 4 * g + hh:4 * g + hh + 1], Dm[:, hsl], ALU.mult, ALU.mult,
                    (pKk, ("nbeta", n), k4), (hk_(0),))
                tt("dve", attnT[:, hsl], pK[:, 256 + kh * 128:256 + (kh + 1) * 128], DTm[:, hsl], ALU.mult, (pKk, k3), (atk,))
                tt("pool", Qd[:, hsl], Qt[:, kh, cs], eGr[:, hsl], ALU.mult, (("Qt", kh), k5), (Qdk,))
                act(Vb[:, hsl], Vtok[:, n, hh, :], AF.Identity, (("Vtok", n), ("beta", n)), (Vbk,), scale=beta[:, n, 4 * g + hh:4 * g + hh + 1])
                act(kdec[:, hsl], Ktok[:, n, kh, :], AF.Identity, (("Ktok", n), ("ctmp", 2)), (kdk,), scale=ctmp[:, 2, 4 + hh:5 + hh])
            pb, pbk = bbank()
            for hh in range(4):
                tr(pb[:, hsl_(hh)], A0[:, hsl_(hh)], id_bf[:], (hk_(0), ("id_bf",)), (pbk,))
            cp("act", U0[:], pb[:, 0:512], (pbk,), (hk_(1),))
            tt("dve", v4(Ta), v4(U0), b4(id_bf[:]), ALU.add, (hk_(1), ("id_bf",)), (hk_(4),))
        stages.append(prep)

        state = dict(Pc=U0, Qc=A0, Pk=hk_(1), Qk=hk_(0), Pn=P1, Qn=Q1, Pnk=hk_(2), Qnk=hk_(3),
                     Tc=Ta, Tck=hk_(4), Tn=Tb_, Tnk=hk_(5))

        def level(lev):
            def f():
                z = state
                last = lev == 5
                if last:
                    z["Tn"], z["Tnk"] = Tfin, Tfk
                pq, pqk = bank_pd()
                for hh in range(4):
                    mm(pq[:, hsl_(hh)], z["Pc"][:, hsl_(hh)], z["Qc"][:, hsl_(hh)], (z["Pk"], z["Qk"]), (pqk,))
                cp("act", z["Qn"][:], pq[:], (pqk,), (z["Qnk"],))
                if not last:
                    pp, ppk = bank_pd()
                    for hh in range(4):
                        mm(pp[:, hsl_(hh)], z["Qc"][:, hsl_(hh)], z["Pc"][:, hsl_(hh)], (z["Pk"], z["Qk"]), (ppk,))
                    cp("dve", z["Pn"][:], pp[:], (ppk,), (z["Pnk"],))
                ptt, ptk = bank_pd()
                for hh in range(4):
                    mm(ptt[:, hsl_(hh)], z["Qn"][:, hsl_(hh)], z["Tc"][:, hsl_(hh)], (z["Qnk"], z["Tck"]), (ptk,))
                tt("dve", z["Tn"][:], z["Tc"][:], ptt[:], ALU.add, (z["Tck"], ptk), (z["Tnk"],))
                (z["Pc"], z["Qc"], z["Pk"], z["Qk"], z["Pn"], z["Qn"], z["Pnk"], z["Qnk"]) = (
                    z["Pn"], z["Qn"], z["Pnk"], z["Qnk"], z["Pc"], z["Qc"], z["Pk"], z["Qk"])
                z["Tc"], z["Tck"], z["Tn"], z["Tnk"] = z["Tn"], z["Tnk"], z["Tc"], z["Tck"]
            return f
        for lev in range(6):
            stages.append(level(lev))
        return stages

    def rc_stages(g, n, s, blk):
        cs = slice(n * 128, (n + 1) * 128)
        Vb, Vbk = gdn_tiles(blk)[6 + s]
        (Tfin, Tfk), (attnT, atk), (Qd, Qdk), (kdec, kdk) = sethk(s)
        W, vnew, osq, Ft = Hs[6], Hs[7], Hs[8], Fs[7]
        cN = ("ctmp", 3, s)
        cG = ("ctmp", 4, s)
        hold = {}

        def r1():
            pks, pksk = bank_rc()
            for hh in range(4):
                h = 4 * g + hh
                mm(pks[:, hsl_(hh)], Kt[:, hh // 2, cs], Sb[:, h, :], (("Kt", hh // 2), ("Sb", h)), (pksk,))
            for hh in range(4):
                stt(W[:, hsl_(hh)], pks[:, hsl_(hh)], ctmp[:, 3, 4 * s + hh:4 * s + hh + 1], Vb[:, hsl_(hh)], ALU.mult, ALU.add,
                    (pksk, cN, Vbk), (("H", 6),))

        def r2():
            pvn, pvnk = bank_rc()
            for hh in range(4):
                mm(pvn[:, hsl_(hh)], Tfin[:, hsl_(hh)], W[:, hsl_(hh)], (Tfk, ("H", 6)), (pvnk,))
            cp("act", vnew[:], pvn[:], (pvnk,), (("H", 7),))

        def r3():
            po, pok = bank_rc()
            pds, pdsk = bank_rc()
            hold["po"] = (po, pok)
            for hh in range(4):
                h = 4 * g + hh
                mm(po[:, hsl_(hh)], Sb[:, h, :], Qd[:, hsl_(hh)], (("Sb", h), Qdk), (pok,), start=True, stop=False)
                mm(po[:, hsl_(hh)], vnew[:, hsl_(hh)], attnT[:, hsl_(hh)], (("H", 7), atk), (pok,), start=False, stop=True)
                mm(pds[:, hsl_(hh)], kdec[:, hsl_(hh)], vnew[:, hsl_(hh)], (kdk, ("H", 7)), (pdsk,))
            for hh in range(4):
                h = 4 * g + hh
                stt(S[:, h, :], S[:, h, :], ctmp[:, 4, 4 * s + hh:4 * s + hh + 1], pds[:, hsl_(hh)], ALU.mult, ALU.add,
                    (("S", h), cG, pdsk), (("S", h),))
                cp("pool", Sb[:, h, :], S[:, h, :], (("S", h),), (("Sb", h),))

        def r4a():
            po, pok = hold["po"]
            act(osq[:], po[:], AF.Square, (pok,), (("H", 8),))
            pss, pssk = bank_rc()
            hold["pss"] = (pss, pssk)
            mm(pss[:], one_bf[:], osq[:], (("one_bf",), ("H", 8)), (pssk,))
            tsc("dve", Ft[:], pss[:], 1.0 / 128.0, NORM_EPS, ALU.mult, ALU.add, (pssk,), (("F", 7),))
            act(Ft[:], Ft[:], AF.Ln, (("F", 7),), (("F", 7),))
            act(Ft[:], Ft[:], AF.Exp, (("F", 7),), (("F", 7),), scale=-0.5)

        def r4b():
            po, pok = hold["po"]
            tt("dve", Ft[:], po[:], Ft[:], ALU.mult, (pok, ("F", 7)), (("F", 7),))
            stt(big[:, 4 * g:4 * g + 4, cs], v4(Ft), normw[:, 0:1], sz[:, :, cs], ALU.mult, ALU.mult,
                (("F", 7), ("normw",)) + tuple(("sz", i) for i in range(4)), tuple(("big", 4 * g + i) for i in range(4)))

        return [r1, r2, r3, r4a, r4b]

    def gdn_group_chunks(g, blk):
        P.tag = "gdn_chunk"
        for f in pd_stages(g, 0, 0, blk):
            f()
        for n in range(4):
            rc = rc_stages(g, n, n % 2, blk)
            pd = pd_stages(g, n + 1, (n + 1) % 2, blk) if n < 3 else []
            order = []
            ri, pi = 0, 0
            while ri < len(rc) or pi < len(pd):
                if pi < len(pd):
                    order.append(pd[pi]); pi += 1
                if ri < len(rc):
                    order.append(rc[ri]); ri += 1
            for f in order:
                f()

    def sample_gdn():
        NSL = 8
        Sst = [_T(y[:, i, 3 * TB:4 * TB].rearrange("p (a b) -> p a b", a=4)) for i in range(NSL)]
        sstk = lambda sl: ("y", sl, 3)
        preskeys = tuple(("pres", c) for c in range(32))
        dma("sp", sconv[:].rearrange("p a b c -> p (a b c)"), sconvT, "sc", (), K_sconv)
        pt, pk = bank()
        for k in range(KC):
            mm(pt[0:NS, 0:32], us[:, k, :], wba[:, k, :], (("us", k), ("wba",)), (pk,), start=(k == 0), stop=(k == KC - 1))
        act(bas[:, 0:16], pt[0:NS, 0:16], AF.Exp, (pk,), K_bas, scale=-1.0)
        tsc("dve", bas[:, 0:16], bas[:, 0:16], 1.0, None, ALU.add, ALU.bypass, K_bas, K_bas)
        P.add("dve", lambda e: e.reciprocal(out=bas[:, 0:16], in_=bas[:, 0:16]), K_bas, K_bas)
        tt("dve", bas[:, 16:32], pt[0:NS, 16:32], dtb[0:NS, :], ALU.add, (pk, ("dtb",)) + K_bas, K_bas)
        act(bas[:, 16:32], bas[:, 16:32], AF.Exp, K_bas, K_bas)
        tsc("dve", bas[:, 16:32], bas[:, 16:32], 1.0, None, ALU.add, ALU.bypass, K_bas, K_bas)
        act(bas[:, 16:32], bas[:, 16:32], AF.Ln, K_bas, K_bas)
        tt("dve", bas[:, 16:32], bas[:, 16:32], nexpA[0:NS, :], ALU.mult, K_bas + (("nexpA",),), K_bas)
        act(bas[:, 16:32], bas[:, 16:32], AF.Exp, K_bas, K_bas)
        I16 = cst[0:NS, C_ID:C_ID + NS].unsqueeze(2).broadcast_to([NS, NS, NH])
        for qi in range(2):
            tt("dve", sexp[:, qi, :, :], bas[:, qi * 16:qi * 16 + 16].unsqueeze(1).broadcast_to([NS, NS, NH]), I16, ALU.mult, K_bas + (KCST,), K_sexp)
        prow, prk = bank()
        for qi in range(2):
            mm(prow[:, qi * 256:(qi + 1) * 256], cst[0:NS, C_ONE:C_ONE + 128], sexp[:, qi, :, :].rearrange("p a b -> p (a b)"),
               (KCST,) + K_sexp, (prk,))
        cp("act", sm[:, 0, :], prow[:, 0:256], (prk,), (("u", 0),))
        cp("act", sm[:, 1, :], prow[:, 256:512], (prk,), (("u", 1),))

        def cw(j):
            return convw[:, :, j].unsqueeze(2).broadcast_to([128, 32, NS])
        ctmp2 = sm[:, 2:4, :].rearrange("p a b -> p (a b)").rearrange("p (c n) -> p c n", c=32)
        ck = (("u", 2), ("u", 3))
        tt("dve", qkvs[:], pres[:, 0:32, :], cw(3), ALU.mult, preskeys + (("convw",),), K_qkvs)
        for j in range(3):
            tt("dve", ctmp2, sconv[:, :, j, :], cw(j), ALU.mult, K_sconv + (("convw",),), ck)
            tt("dve", qkvs[:], qkvs[:], ctmp2, ALU.add, K_qkvs + ck, K_qkvs)
        act(qkvs[:], qkvs[:], AF.Silu, K_qkvs, K_qkvs)
        act(szs[:], pres[:, 32:48, :], AF.Silu, tuple(("pres", c) for c in range(32, 48)), K_szs)
        dma("sp", sconv_new_o, pres[:, 0:32, :].rearrange("p a b -> p (a b)"), "out", preskeys, ())
        qk2 = qkvs[:, 0:16, :].rearrange("p a b -> p (a b)")
        act(sm[:, 2, :], qk2, AF.Square, K_qkvs, (("u", 2),))
        pt2, pk2 = bank()
        mm(pt2[:, 0:256], ONEf, sm[:, 2, :], (KCST, ("u", 2)), (pk2,))
        tsc("dve", sm[:, 3, :], pt2[:, 0:256], NORM_EPS, None, ALU.add, ALU.bypass, (pk2,), (("u", 3),))
        act(sm[:, 3, :], sm[:, 3, :], AF.Ln, (("u", 3),), (("u", 3),))
        act(sm[:, 3, :], sm[:, 3, :], AF.Exp, (("u", 3),), (("u", 3),), scale=-0.5)
        tt("dve", qk2, qk2, sm[:, 3, :], ALU.mult, K_qkvs + (("u", 3),), K_qkvs)
        tsc("dve", qkvs[:, 0:8, :], qkvs[:, 0:8, :], 128.0 ** -0.5, None, ALU.mult, ALU.bypass, K_qkvs, K_qkvs)
        for r in range(2):
            src_k = qkvs[:, 8:16, :].rearrange("p k b -> p b k")
            src_q = qkvs[:, 0:8, :].rearrange("p k b -> p b k")
            dstk = kqs[:, :, :, 0].rearrange("p b (k r) -> p b k r", r=2)[:, :, :, r]
            dstq = kqs[:, :, :, 1].rearrange("p b (k r) -> p b k r", r=2)[:, :, :, r]
            cp("dve", dstk, src_k, K_qkvs, K_kqs)
            cp("dve", dstq, src_q, K_qkvs, K_kqs)
        tt("dve", sm[:, 4, 0:128], qkvs[:, 0:8, :].rearrange("p a b -> p (a b)"), qkvs[:, 8:16, :].rearrange("p a b -> p (a b)"), ALU.mult,
           K_qkvs, (("u", 4),))
        pqk_t, pqk_k = bank()
        mm(pqk_t[:, 0:128], ONEf, sm[:, 4, 0:128], (KCST, ("u", 4)), (pqk_k,))
        cp("act", sm[:, 5, 0:128], pqk_t[:, 0:128], (pqk_k,), (("u", 5),))
        ptk, ptkk = bank()
        ptk2, ptkk2 = bank()
        for kh in range(8):
            dst = (ptk if kh < 4 else ptk2)
            dk_ = (ptkk if kh < 4 else ptkk2)
            P.add("pe", lambda e, dst=dst, kh=kh: e.transpose(dst[0:NS, (kh % 4) * 128:(kh % 4 + 1) * 128], qkvs[:, 8 + kh, :], IDf),
                  K_qkvs + (KCST,), (dk_,))
        cp("act", ktoks[:, 0:4, :].rearrange("p a b -> p (a b)"), ptk[0:NS, :], (ptkk,), K_ktoks)
        cp("act", ktoks[:, 4:8, :].rearrange("p a b -> p (a b)"), ptk2[0:NS, :], (ptkk2,), K_ktoks)
        pS_t, pS_k = bank()
        it = 0
        for b in range(NS):
            for q4 in range(4):
                sl = it % NSL
                it += 1
                dma("sp", Sst[sl][:], S_in[b, q4 * 4:(q4 + 1) * 4].rearrange("h k v -> k h v"), f"sin{sl}", (), (sstk(sl),))
                for hh in range(4):
                    h = q4 * 4 + hh
                    mm(pS_t[:, (b * NH + h) * 2:(b * NH + h) * 2 + 2], Sst[sl][:, hh, :], kqs[:, b, h, :], (sstk(sl),) + K_kqs, (pS_k,))
        pS4 = pS_t[:].rearrange("p (b h t) -> p b h t", b=NS, h=NH)
        Sk = pS4[:, :, :, 0]
        Sq = pS4[:, :, :, 1]

        def bh(t2d):
            return t2d.rearrange("p (b h) -> p b h", b=NS)
        vS = qkvs[:, 16:32, :].rearrange("p h b -> p b h")
        tt("dve", bh(sm[:, 6, :]), Sk, bh(sm[:, 1, :]), ALU.mult, (pS_k, ("u", 1)), (("u", 6),))
        tt("dve", bh(sm[:, 6, :]), vS, bh(sm[:, 6, :]), ALU.subtract, K_qkvs + (("u", 6),), (("u", 6),))
        tt("dve", bh(sm[:, 6, :]), bh(sm[:, 6, :]), bh(sm[:, 0, :]), ALU.mult, (("u", 6), ("u", 0)), (("u", 6),))
        tt("dve", bh(sm[:, 7, :]), Sq, bh(sm[:, 1, :]), ALU.mult, (pS_k, ("u", 1)), (("u", 7),))
        qkb = sm[:, 5, 0:128].rearrange("p (k b) -> p b k", k=8)
        for r in range(2):
            vn_r = bh(sm[:, 6, :]).rearrange("p b (k r) -> p b k r", r=2)[:, :, :, r]
            o8_r = bh(sm[:, 2, :]).rearrange("p b (k r) -> p b k r", r=2)[:, :, :, r]
            tt("dve", o8_r, vn_r, qkb, ALU.mult, (("u", 6), ("u", 5)), (("u", 2),))
        tt("dve", sm[:, 7, :], sm[:, 7, :], sm[:, 2, :], ALU.add, (("u", 7), ("u", 2)), (("u", 7),))
        act(sm[:, 3, :], sm[:, 7, :], AF.Square, (("u", 7),), (("u", 3),))
        pn_t, pn_k = bank()
        mm(pn_t[:, 0:256], ONEf, sm[:, 3, :], (KCST, ("u", 3)), (pn_k,))
        tsc("dve", sm[:, 4, :], pn_t[:, 0:256], 1.0 / 128.0, NORM_EPS, ALU.mult, ALU.add, (pn_k,), (("u", 4),))
        act(sm[:, 4, :], sm[:, 4, :], AF.Ln, (("u", 4),), (("u", 4),))
        act(sm[:, 4, :], sm[:, 4, :], AF.Exp, (("u", 4),), (("u", 4),), scale=-0.5)
        tt("dve", sm[:, 7, :], sm[:, 7, :], sm[:, 4, :], ALU.mult, (("u", 7), ("u", 4)), (("u", 7),))
        stt(bigs[:, 0:16, :].rearrange("p h b -> p b h"), bh(sm[:, 7, :]), normw[:, 0:1], szs[:].rearrange("p h b -> p b h"), ALU.mult, ALU.mult,
            (("u", 7), ("normw",)) + K_szs, tuple(("bigs", c) for c in range(16)))
        vt_t = [_T(y[0:NS, i, 2 * TB:3 * TB].rearrange("p (a b) -> p a b", a=4)) for i in range(4)]
        vm_t = [_T(y[0:NS, 4 + i, 2 * TB:3 * TB].rearrange("p (a b) -> p a b", a=4)) for i in range(4)]
        vtk = lambda i: ("y", i, 2)
        vmk = lambda i: ("y", 4 + i, 2)
        for q4 in range(4):
            pv, pvk = bank()
            for hh in range(4):
                h = q4 * 4 + hh
                P.add("pe", lambda e, pv=pv, hh=hh, h=h: e.transpose(pv[0:NS, hh * 128:(hh + 1) * 128], bh(sm[:, 6, :])[:, :, h], IDf),
                      (("u", 6), KCST), (pvk,))
            cp("act", vt_t[q4][:].rearrange("p a b -> p (a b)"), pv[0:NS, :], (pvk,), (vtk(q4),))
        for b in range(NS):
            for q4_ in range(4):
                tsc("dve", vm_t[q4_][:].rearrange("p a b -> p (a b)"), vt_t[q4_][:].rearrange("p a b -> p (a b)"), cst[0:NS, C_ID + b:C_ID + b + 1], None,
                    ALU.mult, ALU.bypass, (vtk(q4_), KCST), (vmk(q4_),))
            for q4 in range(4):
                sl = it % NSL
                it += 1
                dma("sp", Sst[sl][:], S_in[b, q4 * 4:(q4 + 1) * 4].rearrange("h k v -> k h v"), f"sin{sl}", (), (sstk(sl),))
                pu_t, pu_k = bank()
                for hh in range(4):
                    h = q4 * 4 + hh
                    mm(pu_t[:, hsl_(hh)], ktoks[:, h // 2, :], vm_t[q4][:, hh, :], K_ktoks + (vmk(q4),), (pu_k,))
                for hh in range(4):
                    h = q4 * 4 + hh
                    stt(Sst[sl][:, hh, :], Sst[sl][:, hh, :], sm[:, 1, b * NH + h:b * NH + h + 1], pu_t[:, hsl_(hh)], ALU.mult, ALU.add,
                        (sstk(sl), ("u", 1), pu_k), (sstk(sl),))
                dma("sp", sS_o[b, q4 * 4:(q4 + 1) * 4].rearrange("h k v -> k h v"), Sst[sl][:], f"sout{sl}", (sstk(sl),), ())

    def pool_weights():
        t, wk = wload([(lambda t: t[:, 0:2048].rearrange("p (g k c) -> p g k c", g=4, k=2), pool_w.rearrange("g (k p) c -> p g k c", p=128))])
        return t[:, 0:2048].rearrange("p (g k c) -> p g k c", g=4, k=2), wk

    def pool_prompt_block(blk, pw, pwk):
        P.tag = "pool"
        L = 15 + TB
        PSETS = [((Fw[3 * i], ("F", 3 * i)), [(Fw[3 * i + 1], ("F", 3 * i + 1)), (Fw[3 * i + 2], ("F", 3 * i + 2))]) for i in range(3)]
        for k in range(KC):
            (u1_, u1k), bufs = PSETS[k % 3]
            gi = k // 2
            w = 2 << gi
            mk = (("mod", 1, 1, k), ("mod", 1, 0, k))
            cp("pool", u1_[:, 0:15], pcarry[:, k, :], (("pcarry", k),), (u1k,))
            act(u1_[:, 15:15 + TB], yap(k, blk), AF.Identity, (ykey(k, blk),) + mk, (u1k,),
                bias=mod_ap(1, 0, k, False), scale=mod_ap(1, 1, k, False))
            cp("pool", pcarry[:, k, :], u1_[:, TB:TB + 15], (u1k,), (("pcarry", k),))
            src_, srck, off, m, bi = u1_, u1k, 0, 1, 0
            while m < w:
                dst, dstk = bufs[bi % 2]
                n_el = L - off - m
                tt("dve", dst[:, 0:n_el], src_[:, m:m + n_el], src_[:, 0:n_el], ALU.add, (srck,), (dstk,))
                src_, srck = dst, dstk
                off += m
                m *= 2
                bi += 1
            s0 = 15 - off
            mean, meank = bufs[bi % 2]
            tsc("dve", mean[:, 0:TB], src_[:, s0:s0 + TB], 1.0 / w, None, ALU.mult, ALU.bypass, (srck,), (meank,))
            if blk == 0:
                rc = cst[:, C_RC + gi * 16:C_RC + gi * 16 + 16]
                tt("dve", mean[:, 0:16], src_[:, s0:s0 + 16], rc, ALU.mult, (srck, KCST, meank), (meank,))
            tt("dve", u[:, k, :], mean[:, 0:TB], u1_[:, 15:15 + TB], ALU.subtract, (meank, u1k), (("u", k),))
        for dc in range(KC):
            gi = dc // 2
            el = dc % 2
            pt, pk = bank()
            for k2 in range(2):
                mm(pt[:, 0:TB], pw[:, gi, k2, el * 128:(el + 1) * 128], u[:, gi * 2 + k2, :], pwk + (("u", gi * 2 + k2),), (pk,), start=(k2 == 0), stop=(k2 == 1))
            residual(1, 0, blk, dc, pt, pk, pool=True)
        layer_norm(1, 0, blk)

    def pool_sample(pw, pwk):
        dma("sp", spool[:].rearrange("p a b c -> p (a b c)"), spoolT, "sp_", (), KF03)
        modulate(1, 0, "s", lambda k: u1s[:, k, :], lambda k: ("F", 5))
        u1keys = (("F", 5),)
        dma("sp", spool_new_o, u1s[:].rearrange("p a b -> p (a b)"), "out", u1keys, ())
        for gi in range(4):
            w = 2 << gi
            ks = slice(gi * 2, gi * 2 + 2)
            P.add("dve", lambda e, ks=ks, w=w: e.reduce_sum(out=pls[:, ks, :], in_=spool[:, ks, :, 15 - (w - 1):15], axis=mybir.AxisListType.X),
                  KF03, (("F", 6),))
            tt("dve", pls[:, ks, :], pls[:, ks, :], u1s[:, ks, :], ALU.add, (("F", 6),) + u1keys, (("F", 6),))
            stt(us[:, ks, :], pls[:, ks, :], 1.0 / w, u1s[:, ks, :], ALU.mult, ALU.subtract, (("F", 6),) + u1keys, (("us", gi * 2), ("us", gi * 2 + 1)))
        for dc in range(KC):
            gi = dc // 2
            el = dc % 2
            pt, pk = bank()
            for k2 in range(2):
                mm(pt[:, 0:NS], pw[:, gi, k2, el * 128:(el + 1) * 128], us[:, gi * 2 + k2, :], pwk + (("us", gi * 2 + k2),), (pk,), start=(k2 == 0), stop=(k2 == 1))
            residual(1, 0, "s", dc, pt, pk, pool=True)
        layer_norm(1, 0, "s")

    modulate(0, 0, "s", lambda k: us[:, k, :], lambda k: ("us", k))
    for blk in range(NBLK):
        gdn_prompt_block(blk)
        ffn(0, [blk, "s"] if blk == 2 else [blk])
        if blk == 0:
            mod_emit(12)
    for k in range(KC):
        tsc("dve", cgate[:, k, :], mod[:, 1, 2, k, :], pscale[:, k:k + 1], None, ALU.mult, ALU.bypass, (("mod", 1, 2, k), ("pscale",)), (("cgate",),))
    for blk in range(NBLK):
        pw, pwk = pool_weights()
        pool_prompt_block(blk, pw, pwk)
        if blk == 0:
            pool_sample(pw, pwk)
        ffn(1, [blk, "s"] if blk == 1 else [blk])
    for k in range(KC):
        dma("sp", yT_o[k * 128:(k + 1) * 128, :], y[:, k, :], "out", tuple(("y", k, b) for b in range(NBLK)), ())
    dma("sp", ysT_o.rearrange("(k p) n -> p k n", p=128), ys[:], "out", tuple(("ys", k) for k in range(KC)), ())
    dma("sp", pS_o.rearrange("h k v -> k h v"), S[:], "out", tuple(("S", h) for h in range(NH)), ())
    dma("sp", pconv_o, ccarry[:].rearrange("p a b -> p (a b)"), "out", tuple(("ccarry", c) for c in range(32)), ())
    dma("sp", ppool_o, pcarry[:].rearrange("p a b -> p (a b)"), "out", tuple(("pcarry", k) for k in range(KC)), ())

    P.emit(sems, dsem)
    return nc, es, P


_CACHE = {}


def _consts():
    c = np.zeros((128, NCONST), np.float32)
    i = np.arange(128)
    c[:, C_ID:C_ID + 128] = np.eye(128, dtype=np.float32)
    c[:, C_TRI:C_TRI + 128] = (i[:, None] <= i[None, :]).astype(np.float32)
    c[:, C_SU:C_SU + 128] = (i[:, None] > i[None, :]).astype(np.float32)
    c[:, C_ONE:C_ONE + 128] = 1.0
    for gi in range(4):
        w = 2 << gi
        c[:, C_RC + gi * 16:C_RC + gi * 16 + 16] = (1.0 / np.minimum(np.arange(16) + 1, w)).astype(np.float32)[None, :]
    return c


def _pm(a):
    n = a.shape[0] // 128
    b = a.reshape((n, 128) + a.shape[1:])
    b = np.moveaxis(b, 1, 0)
    return np.ascontiguousarray(b.reshape(128, -1))


def kernel(x_prompt, x_sample, c_prompt, c_sample, state_gdn_S, state_gdn_conv, state_pool,
           ada_w, ada_b, ln_g, ln_b, gdn_w_in, gdn_conv_w, gdn_A_log, gdn_dt_bias, gdn_norm_w,
           gdn_w_out, pool_w, pool_scale, ffn_w_up, ffn_w_down):
    f = lambda a: np.ascontiguousarray(np.asarray(a, dtype=np.float32))
    x_prompt, x_sample, c_prompt, c_sample = f(x_prompt), f(x_sample), f(c_prompt), f(c_sample)
    state_gdn_S, state_gdn_conv, state_pool = f(state_gdn_S), f(state_gdn_conv), f(state_pool)
    if "nc" not in _CACHE:
        _CACHE["nc"] = build_program()
    nc = _CACHE["nc"][0]
    shared = {
        "ada_w": f(ada_w),
        "ada_bT": np.ascontiguousarray(f(ada_b).reshape(2, 48, 128).transpose(2, 0, 1).reshape(128, 96)),
        "ln_gT": np.ascontiguousarray(f(ln_g).reshape(2, 2, 8, 128).transpose(3, 0, 1, 2).reshape(128, 32)),
        "ln_bT": np.ascontiguousarray(f(ln_b).reshape(2, 2, 8, 128).transpose(3, 0, 1, 2).reshape(128, 32)),
        "w_in": f(gdn_w_in)[0],
        "convwT": np.ascontiguousarray(f(gdn_conv_w)[0].reshape(4, 32, 128).transpose(2, 1, 0).reshape(128, 128)),
        "alog_b": np.ascontiguousarray(np.broadcast_to(f(gdn_A_log)[0][None, :], (128, NH))),
        "dtb_b": np.ascontiguousarray(np.broadcast_to(f(gdn_dt_bias)[0][None, :], (128, NH))),
        "normw_c": np.ascontiguousarray(f(gdn_norm_w)[0].reshape(128, 1)),
        "w_out": f(gdn_w_out)[0],
        "pool_w": f(pool_w)[0],
        "pscaleT": np.ascontiguousarray(f(pool_scale)[0].reshape(8, 128).T),
        "w_up": f(ffn_w_up),
        "w_down": f(ffn_w_down),
        "consts": _consts(),
    }
    in_maps = []
    for i in range(NCORES):
        sl = slice(i * NS, (i + 1) * NS)
        cT = np.concatenate([c_prompt[i][:, None], c_sample[sl].T], axis=1)
        sc = state_gdn_conv[0, sl]
        spl = state_pool[0, sl]
        m = dict(shared)
        m.update({
            "xT": np.ascontiguousarray(x_prompt[i].T),
            "xsT": np.ascontiguousarray(x_sample[sl, 0, :].T),
            "cT": np.ascontiguousarray(cT),
            "S_in": np.ascontiguousarray(state_gdn_S[0, sl]),
            "sconv_nat": np.ascontiguousarray(sc),
            "sconvT": np.ascontiguousarray(sc.reshape(NS, 3, 32, 128).transpose(3, 2, 1, 0).reshape(128, 32 * 3 * NS)),
            "spool_nat": np.ascontiguousarray(spl),
            "spoolT": np.ascontiguousarray(spl.reshape(NS, 15, 8, 128).transpose(3, 2, 0, 1).reshape(128, 8 * NS * 15)),
        })
        in_maps.append(m)
    res = run_bass_kernel_spmd(nc, in_maps, core_ids=list(range(NCORES)))
    R = res.results
    B, DEC = NCORES, NCORES * NS
    y_prompt = np.zeros((B, SEQ, D), np.float32)
    y_sample = np.zeros((DEC, 1, D), np.float32)
    p_S = np.zeros((1, B, NH, 128, 128), np.float32)
    p_conv = np.zeros((1, B, 3, 4096), np.float32)
    p_pool = np.zeros((1, B, 15, D), np.float32)
    s_S = np.zeros((1, DEC, NH, 128, 128), np.float32)
    s_conv = np.zeros((1, DEC, 3, 4096), np.float32)
    s_pool = np.zeros((1, DEC, 15, D), np.float32)
    for i in range(NCORES):
        r = R[i]
        sl = slice(i * NS, (i + 1) * NS)
        g = lambda k: np.asarray(r[k], dtype=np.float32)
        y_prompt[i] = g("yT").T
        y_sample[sl, 0, :] = g("ysT").T
        p_S[0, i] = g("pS")
        p_conv[0, i] = g("pconvT").reshape(128, 32, 3).transpose(2, 1, 0).reshape(3, 4096)
        p_pool[0, i] = g("ppoolT").reshape(128, 8, 15).transpose(2, 1, 0).reshape(15, D)
        s_S[0, sl] = g("sS")
        s_conv[0, sl, 0:2] = g("sconv_shift")
        s_conv[0, sl, 2] = g("sconv_newT").reshape(128, 32, NS).transpose(2, 1, 0).reshape(NS, 4096)
        s_pool[0, sl, 0:14] = g("spool_shift")
        s_pool[0, sl, 14] = g("spool_newT").reshape(128, 8, NS).transpose(2, 1, 0).reshape(NS, D)
    return (y_prompt, y_sample, p_S, p_conv, p_pool, s_S, s_conv, s_pool)
```

```python
import numpy as np
from contextlib import ExitStack
import concourse.bass as bass
import concourse.mybir as mybir
from concourse.bass_utils import run_bass_kernel_spmd

F32 = mybir.dt.float32
BF16 = mybir.dt.bfloat16
AF = mybir.ActivationFunctionType
ALU = mybir.AluOpType

D = 1024
SEQ = 2048
TB = 512
NBLK = SEQ // TB
NS = 16
KC = 8
DFF = 2816
FC = 22
NH = 16
IN_DIM = 6176
ALPHA = float(4.0 ** 0.25)
LN_EPS = 1e-5
NORM_EPS = 1e-6
NCORES = 8
C_ID, C_TRI, C_SU, C_ONE, C_RC = 0, 128, 256, 384, 512
NCONST = 576
BANKS_RC = (0, 1, 2)
BANKS_PD = (3, 4, 5)
BANKS_PJ = (6, 7)


class Prog:
    ENGS = ("pe", "act", "dve", "pool", "sp")
    WINDOW = 700
    SEM_LAT = 0.25

    def __init__(self, nc):
        self.nc = nc
        self.ops = []
        self.last_w = {}
        self.readers = {}
        self.schedule = True
        self.tag = ""

    def add(self, eng, fn, reads=(), writes=(), dma=None, cost=0.3, tbl=None, xfer=0.0):
        i = len(self.ops)
        deps = set()
        why = {}
        for r in reads:
            j = self.last_w.get(r)
            if j is not None:
                deps.add(j)
                why.setdefault(j, ("RAW", r))
        for w in writes:
            j = self.last_w.get(w)
            if j is not None:
                deps.add(j)
                why.setdefault(j, ("WAW", w))
            for j in self.readers.get(w, ()):
                deps.add(j)
                why.setdefault(j, ("WAR", w))
        self.why = getattr(self, "why", {})
        self.why[i] = why
        for r in reads:
            self.readers.setdefault(r, []).append(i)
        for w in writes:
            self.last_w[w] = i
            self.readers[w] = []
        deps.discard(i)
        self.ops.append(dict(eng=eng, fn=fn, deps=deps, dma=dma, sig=False, cost=cost, tbl=tbl, xfer=xfer, tag=getattr(self, "tag", "")))
        return i

    def _list_schedule(self):
        import heapq
        ops = self.ops
        n = len(ops)
        succ = [[] for _ in range(n)]
        indeg = [0] * n
        for i, o in enumerate(ops):
            indeg[i] = len(o["deps"])
            for j in o["deps"]:
                succ[j].append(i)
        prio = [0.0] * n
        for i in range(n - 1, -1, -1):
            m = 0.0
            for s in succ[i]:
                if prio[s] > m:
                    m = prio[s]
            prio[i] = m + ops[i]["cost"] + ops[i]["xfer"]
        fin = [0.0] * n
        rt = [0.0] * n
        free = {e: 0.0 for e in self.ENGS}
        ready = {e: [] for e in self.ENGS}
        pending = {e: [] for e in self.ENGS}
        scheduled = [False] * n
        order = {e: [] for e in self.ENGS}
        cur_tbl = [None]
        dma_free = [0.0]
        low = 0
        for i in range(n):
            if indeg[i] == 0:
                heapq.heappush(pending[ops[i]["eng"]], i)
        done = 0
        while done < n:
            while low < n and scheduled[low]:
                low += 1
            lim = low + self.WINDOW
            for e in self.ENGS:
                pe_ = pending[e]
                while pe_ and pe_[0] < lim:
                    ready[e].append(heapq.heappop(pe_))
            best = None
            for e in self.ENGS:
                rl = ready[e]
                if not rl:
                    continue
                tf = free[e]
                bi, bstart, bscore = None, None, None
                for i in rl:
                    stt_ = rt[i] if rt[i] > tf else tf
                    pen = 0.0
                    if e == "act" and ops[i]["tbl"] is not None and ops[i]["tbl"] != cur_tbl[0]:
                        pen = 0.3
                    key = (stt_ + pen, -prio[i], i)
                    if bscore is None or key < bscore:
                        bi, bstart, bscore = i, stt_ + pen, key
                if best is None or (bstart, bi) < (best[0], best[2]):
                    best = (bstart, e, bi)
            if best is None:
                raise RuntimeError("scheduler stuck")
            start, e, i = best
            o = ops[i]
            ready[e].remove(i)
            if e == "act" and o["tbl"] is not None:
                cur_tbl[0] = o["tbl"]
            end = start + o["cost"]
            free[e] = end
            if o["dma"] is not None:
                ds = end if end > dma_free[0] else dma_free[0]
                dma_free[0] = ds + o["xfer"]
                fin[i] = dma_free[0] + 1.5
            else:
                fin[i] = end
            scheduled[i] = True
            order[e].append(i)
            done += 1
            for s in succ[i]:
                lat = 0.0 if (ops[s]["eng"] == "pe" and e == "pe" and o["dma"] is None) else self.SEM_LAT
                t = fin[i] + lat
                if t > rt[s]:
                    rt[s] = t
                indeg[s] -= 1
                if indeg[s] == 0:
                    heapq.heappush(pending[ops[s]["eng"]], s)
        self.sim_end = max(free.values())
        self.fin = fin
        return order

    def emit(self, sems, dma_sems):
        ops = self.ops
        if self.schedule:
            per_eng = self._list_schedule()
        else:
            per_eng = {e: [] for e in self.ENGS}
            for i, o in enumerate(ops):
                per_eng[o["eng"]].append(i)

        def is_pe(o):
            return o["eng"] == "pe" and o["dma"] is None
        for o in ops:
            o["wdeps"] = {j for j in o["deps"] if not (is_pe(o) and is_pe(ops[j]))}
            for j in o["wdeps"]:
                ops[j]["sig"] = True
        cnt = {}
        for e in self.ENGS:
            for i in per_eng[e]:
                o = ops[i]
                if o["dma"] is not None:
                    key = ("dma", o["dma"])
                    cnt[key] = cnt.get(key, 0) + 16
                    o["ticket"] = (key, cnt[key])
                elif o["sig"]:
                    key = ("eng", o["eng"])
                    cnt[key] = cnt.get(key, 0) + 1
                    o["ticket"] = (key, cnt[key])

        def semof(key):
            return dma_sems[key[1]] if key[0] == "dma" else sems[key[1]]

        nc = self.nc
        with nc.Block() as block:
            def body(eng_name):
                def f(eng):
                    known = {}
                    for i in per_eng[eng_name]:
                        o = ops[i]
                        need = {}
                        for j in o["wdeps"]:
                            key, val = ops[j]["ticket"]
                            if val > need.get(key, 0):
                                need[key] = val
                        for key, val in need.items():
                            if known.get(key, 0) >= val:
                                continue
                            eng.wait_ge(semof(key), val)
                            known[key] = val
                        ins = o["fn"](eng)
                        if o["dma"] is not None:
                            ins.then_inc(dma_sems[o["dma"]], 16)
                        elif o["sig"]:
                            ins.then_inc(sems[eng_name], 1)
                    if eng_name == "sp":
                        for key, val in cnt.items():
                            if key[0] == "dma":
                                eng.wait_ge(semof(key), val)
                return f

            block.tensor(body("pe"))
            block.scalar(body("act"))
            block.vector(body("dve"))
            block.gpsimd(body("pool"))
            block.sync(body("sp"))


def build_program():
    nc = bass.Bass("TRN2", target_bir_lowering=False)
    es = ExitStack()

    def din(name, shape, dt=F32):
        return nc.dram_tensor(name, list(shape), dt, kind="ExternalInput").ap()

    def dout(name, shape, dt=F32):
        return nc.dram_tensor(name, list(shape), dt, kind="ExternalOutput").ap()

    xT = din("xT", [D, SEQ])
    xsT = din("xsT", [D, NS])
    cT = din("cT", [D, 1 + NS])
    S_in = din("S_in", [NS, NH, 128, 128])
    sconv_nat = din("sconv_nat", [NS, 3, 4096])
    sconvT = din("sconvT", [128, 32 * 3 * NS])
    spool_nat = din("spool_nat", [NS, 15, D])
    spoolT = din("spoolT", [128, KC * NS * 15])
    ada_w = din("ada_w", [2, D, 6 * D])
    ada_bT = din("ada_bT", [128, 2 * 48])
    ln_gT = din("ln_gT", [128, 32])
    ln_bT = din("ln_bT", [128, 32])
    w_in = din("w_in", [D, IN_DIM])
    convwT = din("convwT", [128, 32 * 4])
    alog_b = din("alog_b", [128, NH])
    dtb_b = din("dtb_b", [128, NH])
    normw_c = din("normw_c", [128, 1])
    w_out = din("w_out", [2048, D])
    pool_w = din("pool_w", [4, 256, 256])
    pscaleT = din("pscaleT", [128, KC])
    w_up = din("w_up", [2, D, 2 * DFF])
    w_down = din("w_down", [2, DFF, D])
    consts_d = din("consts", [128, NCONST])

    yT_o = dout("yT", [D, SEQ])
    ysT_o = dout("ysT", [D, NS])
    pS_o = dout("pS", [NH, 128, 128])
    pconv_o = dout("pconvT", [128, 96])
    ppool_o = dout("ppoolT", [128, KC * 15])
    sS_o = dout("sS", [NS, NH, 128, 128])
    sconv_shift_o = dout("sconv_shift", [NS, 2, 4096])
    sconv_new_o = dout("sconv_newT", [128, 32 * NS])
    spool_shift_o = dout("spool_shift", [NS, 14, D])
    spool_new_o = dout("spool_newT", [128, KC * NS])

    def sb(name, shape, dt=F32):
        return es.enter_context(nc.sbuf_tensor(name, list(shape), dt))

    def psum(name, shape, dt=F32):
        return es.enter_context(nc.psum_tensor(name, list(shape), dt))

    y = sb("y", [128, KC, SEQ])
    ys = sb("ys", [128, KC, NS])
    cst = sb("cst", [128, NCONST])
    id_bf = sb("id_bf", [128, 128], BF16)
    one_bf = sb("one_bf", [128, 128], BF16)
    mod = sb("mod", [128, 2, 6, KC, 1 + NS])
    csil = sb("csil", [128, KC, 1 + NS], BF16)
    lng = sb("lng", [128, 2, 2, KC])
    lnb = sb("lnb", [128, 2, 2, KC])
    lngA = sb("lngA", [128, 2, 2, KC])
    lnbA = sb("lnbA", [128, 2, 2, KC])
    epst = sb("epst", [128, 2])
    convw = sb("convw", [128, 32, 4])
    ccarry = sb("ccarry", [128, 32, 3])
    pcarry = sb("pcarry", [128, KC, 15])
    wba = sb("wba", [128, KC, 32], BF16)
    alog = sb("alog", [128, NH])
    dtb = sb("dtb", [128, NH])
    nexpA = sb("nexpA", [128, NH])
    normw = sb("normw", [128, 1])
    pscale = sb("pscale", [128, KC])
    cgate = sb("cgate", [128, KC, 1 + NS])
    S = sb("S", [128, NH, 128])
    Sb = sb("Sb", [128, NH, 128], BF16)
    ws = [sb(f"ws{i}", [128, 4096], BF16) for i in range(2)]
    u = sb("u", [128, KC, TB], BF16)
    us = sb("us", [128, KC, NS], BF16)
    uf = sb("uf", [128, KC, TB], BF16)
    big = sb("big", [128, FC, TB], BF16)
    bigs = sb("bigs", [128, FC, NS], BF16)
    Qt = sb("Qt", [128, 2, TB], BF16)
    Kt = sb("Kt", [128, 2, TB], BF16)
    Vt = sb("Vt", [128, 4, TB], BF16)
    sz = sb("sz", [128, 4, TB], BF16)
    Ktok = sb("Ktok", [128, 4, 2, 128], BF16)
    Vtok = sb("Vtok", [128, 4, 4, 128], BF16)
    beta = sb("beta", [128, 4, NH])
    nbeta = sb("nbeta", [128, 4, NH])
    gtok = sb("gtok", [128, 4, NH])
    ctmp = sb("ctmp", [128, 8, NH])
    FW = 528
    F_all = sb("F_all", [128, 9, FW])
    H_all = sb("H_all", [128, 23, 512], BF16)
    Fflat = F_all[:].rearrange("p a b -> p (a b)")
    Hf = H_all[:].rearrange("p a b -> p (a b)").bitcast(F32)

    class _T:
        def __init__(self, ap):
            self.ap = ap
        def __getitem__(self, k):
            return self.ap[k]

    Fs = [_T(F_all[:, i, 0:512]) for i in range(9)]
    Fw = [_T(F_all[:, i, :]) for i in range(9)]
    Hs = [_T(H_all[:, i, :]) for i in range(23)]
    pre = Fw[4]
    craw = F_all[:, 7, 0:KC * (1 + NS)].rearrange("p (a b) -> p a b", a=KC)
    adab = F_all[:, 8, 0:96].rearrange("p (a b) -> p a b", a=2)
    u1s = F_all[:, 5, 0:KC * NS].rearrange("p (a b) -> p a b", a=KC)
    pls = F_all[:, 6, 0:KC * NS].rearrange("p (a b) -> p a b", a=KC)
    u1 = Fw[3]
    pres = sb("pres", [128, 48, NS])
    sconv = Fflat[:, 0:1536].rearrange("p (a b c) -> p a b c", a=32, b=3)
    spool = Fflat[:, 0:1920].rearrange("p (a b c) -> p a b c", a=KC, b=NS)
    vtoks = Fflat[0:NS, 0:2048].rearrange("p (a b) -> p a b", a=NH)
    vmask = Fflat[0:NS, 4 * FW:4 * FW + 2048].rearrange("p (a b) -> p a b", a=NH)
    ktoks = Hf[0:NS, 0:1024].rearrange("p (a b) -> p a b", a=8)
    sexp = Hf[0:NS, 1024:1536].rearrange("p (q a b) -> p q a b", q=2, a=NS)
    kqs = Hf[:, 1536:2048].rearrange("p (b h t) -> p b h t", b=NS, h=NH)
    qkvs = Hf[:, 2048:2560].rearrange("p (a b) -> p a b", a=32)
    szs = Hf[:, 2560:2816].rearrange("p (a b) -> p a b", a=16)
    bas = Hf[0:NS, 2816:2848]
    KF03 = tuple(("F", i) for i in range(4))
    KF47 = tuple(("F", i) for i in range(4, 8))
    K_sconv = tuple(("F", i) for i in range(3))
    K_ktoks = tuple(("H", i) for i in range(4))
    K_sexp = (("H", 4), ("H", 5))
    K_kqs = (("H", 6), ("H", 7))
    K_qkvs = (("H", 8), ("H", 9))
    K_szs = (("H", 10),)
    K_bas = (("H", 11),)
    sm = u[:].rearrange("p a b -> p (a b)").bitcast(F32).rearrange("p (a b) -> p a b", a=8)

    PS = [psum(f"ps{i}", [128, 512]) for i in range(8)]
    PSB = [_T(PS[i][:].bitcast(BF16)) for i in range(8)]

    sems = {e: es.enter_context(nc.semaphore(f"s_{e}")) for e in Prog.ENGS}
    dsem = {}

    def stream(name):
        if name not in dsem:
            dsem[name] = es.enter_context(nc.semaphore(f"d_{name}"))
        return name

    P = Prog(nc)
    st = dict(ps=0, pb=0, wsl=0, c=0)

    def _next_bank():
        bs = st.get("bankset")
        if bs == "proj":
            i = BANKS_PJ[st.setdefault("pj", 0) % len(BANKS_PJ)]
            st["pj"] += 1
        elif bs == "sample":
            i = (0, 1, 2)[st.setdefault("sm_", 0) % 3]
            st["sm_"] += 1
        elif bs == "main5":
            i = (3, 4, 5, 6, 7)[st.setdefault("m5", 0) % 5]
            st["m5"] += 1
        else:
            i = st["ps"] % 8
            st["ps"] += 1
        return i

    def bank():
        i = _next_bank()
        return PS[i], ("ps", i)

    def bbank():
        i = _next_bank()
        return PSB[i], ("ps", i)

    def _fs(ap):
        n = 1
        for d in ap.shape[1:]:
            n *= d
        return n

    _TBL = {AF.Silu: "silu", AF.Exp: "explog", AF.Ln: "explog"}

    def act(out, in_, func, r, w, bias=0.0, scale=1.0):
        P.add("act", lambda e: e.activation(out=out, in_=in_, func=func, bias=bias, scale=scale), r, w,
              cost=0.12 + _fs(in_) * 0.00112, tbl=_TBL.get(func))

    def tt(eng, out, in0, in1, op, r, w):
        c = (0.12 + _fs(in0) * 0.00115) if eng == "dve" else (0.2 + _fs(in0) * 0.0021)
        P.add(eng, lambda e: e.tensor_tensor(out=out, in0=in0, in1=in1, op=op), r, w, cost=c)

    def tsc(eng, out, in0, s1, s2, op0, op1, r, w):
        c = (0.12 + _fs(in0) * 0.00115) if eng == "dve" else (0.2 + _fs(in0) * 0.004)
        P.add(eng, lambda e: e.tensor_scalar(out=out, in0=in0, scalar1=s1, scalar2=s2, op0=op0, op1=op1), r, w, cost=c)

    def stt(out, in0, scalar, in1, op0, op1, r, w):
        P.add("dve", lambda e: e.scalar_tensor_tensor(out=out, in0=in0, scalar=scalar, in1=in1, op0=op0, op1=op1), r, w,
              cost=0.12 + _fs(in0) * 0.00125)

    def cp(eng, out, in_, r, w):
        if eng == "act":
            P.add("act", lambda e: e.activation(out=out, in_=in_, func=AF.Copy), r, w, cost=0.12 + _fs(in_) * 0.00112)
        else:
            c = (0.12 + _fs(in_) * 0.00115) if eng == "dve" else (0.2 + _fs(in_) * 0.0021)
            P.add(eng, lambda e: e.tensor_copy(out=out, in_=in_), r, w, cost=c)

    def mm(out, lhsT, rhs, r, w, start=True, stop=True):
        nfree = max(_fs(rhs), 64)
        c = 0.03 + nfree * (4 if rhs.dtype == F32 else 1) / 2400.0
        P.add("pe", lambda e: e.matmul(out, lhsT, rhs, start=start, stop=stop), r, w, cost=c)

    def tr(out, in_, ident, r, w):
        P.add("pe", lambda e: e.transpose(out, in_, ident), r, w, cost=0.1)

    def dma(eng, out, in_, strm, r, w):
        nbytes = 128 * _fs(in_) * 4 if in_.shape[0] == 128 else in_.shape[0] * _fs(in_) * 4
        P.add(eng, lambda e: e.dma_start(out=out, in_=in_), r, w, dma=stream(strm),
              cost=(1.7 if eng == "pool" else 0.15), xfer=nbytes / 330e3)

    def setup_dma(out, in_, w):
        st["c"] += 1
        dma("sp", out, in_, f"c{st['c']}", (), w)

    def memset(eng, ap, val, w):
        P.add(eng, lambda e: e.memset(ap, val), (), w)

    IDf = cst[:, C_ID:C_ID + 128]
    TRI = cst[:, C_TRI:C_TRI + 128]
    SU = cst[:, C_SU:C_SU + 128]
    ONEf = cst[:, C_ONE:C_ONE + 128]
    KCST = ("cst",)

    def b4(ap2d):
        return ap2d.unsqueeze(1).broadcast_to([128, 4, 128])

    def v4(t):
        return t[:].rearrange("p (a b) -> p a b", a=4)

    def hsl_(hh):
        return slice(hh * 128, (hh + 1) * 128)

    def wload(parts):
        s = st["wsl"] % 2
        st["wsl"] += 1
        t = ws[s]
        keys = []
        for pi, (dst_fn, src) in enumerate(parts):
            key = ("ws", s, pi)
            keys.append(key)
            dma("pool", dst_fn(t), src, f"ws{s}_{pi}", (), (key,))
        for pi in range(len(parts), 2):
            keys.append(("ws", s, pi))
        return t, tuple(keys)

    def wv(t, kc, cols, off=0):
        return t[:, off:off + kc * cols].rearrange("p (k c) -> p k c", k=kc)

    setup_dma(cst[:], consts_d, (KCST,))
    setup_dma(craw[:], cT.rearrange("(k p) n -> p k n", p=128), (("F", 7),))
    setup_dma(adab[:].rearrange("p a b -> p (a b)"), ada_bT, (("F", 8),))
    setup_dma(lng[:].rearrange("p a b c -> p (a b c)"), ln_gT, (("lng",),))
    setup_dma(lnb[:].rearrange("p a b c -> p (a b c)"), ln_bT, (("lnb",),))
    setup_dma(convw[:].rearrange("p a b -> p (a b)"), convwT, (("convw",),))
    setup_dma(alog[:], alog_b, (("alog",),))
    setup_dma(dtb[:], dtb_b, (("dtb",),))
    setup_dma(normw[:], normw_c, (("normw",),))
    setup_dma(pscale[:], pscaleT, (("pscale",),))
    def load_x_block(b):
        for k in range(KC):
            dma("sp", y[:, k, b * TB:(b + 1) * TB], xT[k * 128:(k + 1) * 128, b * TB:(b + 1) * TB], f"x{k}", (), (("y", k, b),))
        for k in range(KC):
            tsc("pool", y[:, k, b * TB:(b + 1) * TB], y[:, k, b * TB:(b + 1) * TB], ALPHA, 1.0, ALU.mult, ALU.mult, (("y", k, b),), (("y", k, b),))
    load_x_block(0)
    setup_dma(ys[:], xsT.rearrange("(k p) n -> p k n", p=128), tuple(("ys", k) for k in range(KC)))
    dma("pool", wba[:], w_in.rearrange("(k p) n -> p k n", p=128)[:, :, 6144:6176], "misc", (), (("wba",),))
    dma("sp", sconv_shift_o, sconv_nat[:, 1:3, :], "out", (), ())
    dma("sp", spool_shift_o, spool_nat[:, 1:15, :], "out", (), ())

    tsc("dve", lngA[:], lng[:], ALPHA, None, ALU.mult, ALU.bypass, (("lng",),), (("lngA",),))
    tsc("dve", lnbA[:], lnb[:], ALPHA, None, ALU.mult, ALU.bypass, (("lnb",),), (("lnbA",),))
    memset("dve", epst[:, 0:1], LN_EPS, (("epst",),))
    memset("dve", epst[:, 1:2], NORM_EPS, (("epst",),))
    tsc("pool", ys[:], ys[:], ALPHA, 1.0, ALU.mult, ALU.mult, tuple(("ys", k) for k in range(KC)), tuple(("ys", k) for k in range(KC)))
    for l_ in range(2):
        for kind_ in (1, 4):
            tsc("dve", adab[:, l_, kind_ * 8:(kind_ + 1) * 8], adab[:, l_, kind_ * 8:(kind_ + 1) * 8], 1.0, 1.0 / ALPHA, ALU.add, ALU.mult, (("F", 8),), (("F", 8),))
    cp("dve", id_bf[:], IDf, (KCST,), (("id_bf",),))
    cp("dve", one_bf[:], ONEf, (KCST,), (("one_bf",),))
    memset("dve", ccarry[:], 0.0, tuple(("ccarry", c) for c in range(32)))
    memset("dve", pcarry[:], 0.0, tuple(("pcarry", k) for k in range(KC)))
    memset("dve", S[:], 0.0, tuple(("S", h) for h in range(NH)))
    memset("dve", Sb[:], 0.0, tuple(("Sb", h) for h in range(NH)))
    act(nexpA[:], alog[:], AF.Exp, (("alog",),), (("nexpA",),))
    tsc("dve", nexpA[:], nexpA[:], -1.0, None, ALU.mult, ALU.bypass, (("nexpA",),), (("nexpA",),))
    act(csil[:], craw[:], AF.Silu, (("F", 7),), (("csil",),))

    P.tag = "mod"

    for l_ in range(2):
        for kind_ in range(6):
            cp("dve", mod[:, l_, kind_, :, :], adab[:, l_, kind_ * 8:(kind_ + 1) * 8].unsqueeze(2).broadcast_to([128, 8, 1 + NS]),
               (("F", 8),), tuple(("mod", l_, kind_, ec) for ec in range(8)))

    def mod_piece(l, pc):
        tag0 = P.tag
        P.tag = "mod"
        awl = ada_w[l].rearrange("(k p) n -> p k n", p=128)
        t, wk = wload([(lambda t: wv(t, KC, 512), awl[:, :, pc * 512:(pc + 1) * 512])])
        wvv = wv(t, KC, 512)
        kind = pc // 2
        bs_ = st.get("bankset")
        st["bankset"] = "all"
        pt, pk = bank()
        st["bankset"] = bs_
        for e4 in range(4):
            for k in range(KC):
                mm(pt[:, e4 * 17:(e4 + 1) * 17], wvv[:, k, e4 * 128:(e4 + 1) * 128], csil[:, k, :],
                   wk + (("csil",),), (pk,), start=(k == 0), stop=(k == KC - 1))
        s_ = (1.0 / ALPHA) if kind in (1, 4) else 1.0
        for e4 in range(4):
            ec = (pc % 2) * 4 + e4
            stt(mod[:, l, kind, ec, :], pt[:, e4 * 17:(e4 + 1) * 17], s_, mod[:, l, kind, ec, :], ALU.mult, ALU.add,
                (pk, ("mod", l, kind, ec)), (("mod", l, kind, ec),))
        P.tag = tag0

    mod_queue = [(0, pc) for pc in range(4, 12)] + [(1, pc) for pc in range(12)]
    for pc in range(4):
        mod_piece(0, pc)

    def mod_emit(n):
        for _ in range(n):
            if mod_queue:
                mod_piece(*mod_queue.pop(0))

    def modkeys(l):
        return tuple(("mod", l, kind, ec) for kind in range(6) for ec in range(8))

    def mod_ap(l, kind, k, samp):
        if samp:
            return mod[:, l, kind, k, 1:1 + NS]
        return mod[:, l, kind, k, 0:1]

    def ykey(k, blk):
        return ("ys", k) if blk == "s" else ("y", k, blk)

    def yap(k, blk):
        return ys[:, k, :] if blk == "s" else y[:, k, blk * TB:(blk + 1) * TB]

    def modulate(l, sub, blk, out_fn, out_key_fn):
        ksc, ksh = (1, 0) if sub == 0 else (4, 3)
        samp = blk == "s"
        for k in range(KC):
            mk = (("mod", l, ksc, k), ("mod", l, ksh, k))
            if samp:
                tt("dve", Fs[7][:, 0:NS], yap(k, blk), mod_ap(l, ksc, k, True), ALU.mult, (ykey(k, blk),) + mk, (("F", 7),))
                tt("dve", out_fn(k), Fs[7][:, 0:NS], mod_ap(l, ksh, k, True), ALU.add, (("F", 7),) + mk, (out_key_fn(k),))
            else:
                act(out_fn(k), yap(k, blk), AF.Identity, (ykey(k, blk),) + mk, (out_key_fn(k),),
                    bias=mod_ap(l, ksh, k, False), scale=mod_ap(l, ksc, k, False))

    def residual(l, sub, blk, k, pt, pk, pool=False):
        kg = 2 if sub == 0 else 5
        samp = blk == "s"
        N = NS if samp else TB
        if pool:
            gate = cgate[:, k, 1:1 + NS] if samp else cgate[:, k, 0:1]
            gk = ("cgate",)
        else:
            gate = mod_ap(l, kg, k, samp)
            gk = ("mod", l, kg, k)
        if samp:
            tt("dve", Fs[7][:, 0:NS], pt[:, 0:NS], gate, ALU.mult, (pk, gk), (("F", 7),))
            tt("dve", yap(k, blk), yap(k, blk), Fs[7][:, 0:NS], ALU.add, (ykey(k, blk), ("F", 7)), (ykey(k, blk),))
        else:
            stt(yap(k, blk), pt[:, 0:N], gate, yap(k, blk), ALU.mult, ALU.add, (pk, gk, ykey(k, blk)), (ykey(k, blk),))

    def layer_norm(l, i, blk):
        P.tag = "ln"
        samp = blk == "s"
        N = NS if samp else TB
        psum_t, psk = bank()
        psq_t, pqk = bank()
        for k in range(KC):
            mm(psum_t[:, 0:N], ONEf, yap(k, blk), (KCST, ykey(k, blk)), (psk,), start=(k == 0), stop=(k == KC - 1))
        for k in range(KC):
            h = Hs[6 + (k % 2)]
            hk = ("H", 6 + (k % 2))
            act(h[:, 0:N], yap(k, blk), AF.Square, (ykey(k, blk),), (hk,))
            mm(psq_t[:, 0:N], one_bf[:], h[:, 0:N], (("one_bf",), hk), (pqk,), start=(k == 0), stop=(k == KC - 1))
        m, msq, var, mr = Fs[0], Fs[1], Fs[2], Fs[3]
        act(m[:, 0:N], psum_t[:, 0:N], AF.Copy, (psk,), (("F", 0),), scale=1.0 / D)
        tt("dve", msq[:, 0:N], m[:, 0:N], m[:, 0:N], ALU.mult, (("F", 0),), (("F", 1),))
        stt(var[:, 0:N], psq_t[:, 0:N], 1.0 / D, msq[:, 0:N], ALU.mult, ALU.subtract, (pqk, ("F", 1)), (("F", 2),))
        act(var[:, 0:N], var[:, 0:N], AF.Ln, (("F", 2), ("epst",)), (("F", 2),), bias=epst[:, 0:1])
        act(var[:, 0:N], var[:, 0:N], AF.Exp, (("F", 2),), (("F", 2),), scale=-0.5)
        tt("dve", mr[:, 0:N], m[:, 0:N], var[:, 0:N], ALU.mult, (("F", 0), ("F", 2)), (("F", 3),))
        for k in range(KC):
            f = Fs[4 + (k % 2)]
            fk = ("F", 4 + (k % 2))
            tt("dve", f[:, 0:N], yap(k, blk), var[:, 0:N], ALU.mult, (ykey(k, blk), ("F", 2)), (fk,))
            tt("dve", f[:, 0:N], f[:, 0:N], mr[:, 0:N], ALU.subtract, (fk, ("F", 3)), (fk,))
            fin = (l == 1 and i == 1)
            gt, bt = (lng, lnb) if fin else (lngA, lnbA)
            act(yap(k, blk), f[:, 0:N], AF.Identity, (fk, ("lng",), ("lnb",), ("lngA",), ("lnbA",)), (ykey(k, blk),),
                bias=bt[:, l, i, k:k + 1], scale=gt[:, l, i, k:k + 1])

    def blkinfo(blk, ffn_in=False):
        samp = blk == "s"
        N = NS if samp else TB
        ub = us if samp else (uf if ffn_in else u)
        uk = (lambda k: ("us", k)) if samp else ((lambda k: ("uf", k)) if ffn_in else (lambda k: ("u", k)))
        bg = bigs if samp else big
        bk = (lambda c: ("bigs", c)) if samp else (lambda c: ("big", c))
        return N, ub, uk, bg, bk

    def open_acc(blks, ndc):
        acc = {}
        for blk in blks:
            if blk == "s":
                acc[blk] = [bank()]
            else:
                acc[blk] = [bank() for _ in range(ndc)]
        return acc

    def acc_mm(acc, blk, dl, lhsT, rhs, r, first, last, st_first):
        if blk == "s":
            pt, pk = acc[blk][0]
            out = pt[:, dl * NS:(dl + 1) * NS]
            P.add("pe", lambda e: e.matmul(out, lhsT, rhs, start=st_first, stop=last, skip_group_check=True), r, (pk,), cost=0.06)
        else:
            pt, pk = acc[blk][dl]
            mm(pt[:, 0:TB], lhsT, rhs, r, (pk,), start=first, stop=last)

    def acc_out(acc, blk, dl):
        if blk == "s":
            pt, pk = acc[blk][0]
            return _T(pt[:, dl * NS:(dl + 1) * NS]), pk
        pt, pk = acc[blk][dl]
        return pt, pk

    def ffn(l, blks):
        P.tag = "ffn"
        wup = w_up[l].rearrange("(k p) n -> p k n", p=128)
        wdn = w_down[l].rearrange("(k p) n -> p k n", p=128)
        for blk in blks:
            N, ub, uk, bg, bk = blkinfo(blk, True)
            modulate(l, 1, blk, lambda k, ub=ub, N=N: ub[:, k, 0:N], uk)
        for pc in range(6):
            nf = 4 if pc < 5 else 2
            cols = nf * 128
            for half in range(2):
                t, wk = wload([(lambda t, cols=cols: wv(t, KC, cols), wup[:, :, half * DFF + pc * 512:half * DFF + pc * 512 + cols])])
                wvv = wv(t, KC, cols)
                for blk in blks:
                    N, ub, uk, bg, bk = blkinfo(blk, True)
                    for f in range(nf):
                        fc = pc * 4 + f
                        pg, pgk = bank()
                        for k in range(KC):
                            mm(pg[:, 0:N], wvv[:, k, f * 128:(f + 1) * 128], ub[:, k, 0:N], wk + (uk(k),), (pgk,), start=(k == 0), stop=(k == KC - 1))
                        if blk == "s":
                            tmp, tk = Fs[4][:, f * NS:(f + 1) * NS], ("F", 4)
                        else:
                            tmp, tk = Fs[f][:, 0:N], ("F", f)
                        if half == 0:
                            act(tmp, pg[:, 0:N], AF.Silu, (pgk,), (tk,))
                        else:
                            tt("dve", bg[:, fc, 0:N], tmp, pg[:, 0:N], ALU.mult, (tk, pgk), (bk(fc),))
        fgs = [(0, 8), (8, 16), (16, 22)]
        for ch in range(2):
            acc = open_acc(blks, 4)
            first_s = True
            for gi, (f0, f1) in enumerate(fgs):
                nfc = f1 - f0
                t, wk = wload([(lambda t, nfc=nfc: wv(t, nfc, 512), wdn[:, f0:f1, ch * 512:(ch + 1) * 512])])
                wvv = wv(t, nfc, 512)
                for blk in blks:
                    N, ub, uk, bg, bk = blkinfo(blk)
                    for dl in range(4):
                        for fc in range(f0, f1):
                            acc_mm(acc, blk, dl, wvv[:, fc - f0, dl * 128:(dl + 1) * 128], bg[:, fc, 0:N], wk + (bk(fc),),
                                   first=(fc == 0), last=(fc == FC - 1), st_first=(blk == "s" and first_s))
                            if blk == "s":
                                first_s = False
            for blk in blks:
                for dl in range(4):
                    pt, pk = acc_out(acc, blk, dl)
                    residual(l, 1, blk, ch * 4 + dl, pt, pk)
        for blk in blks:
            layer_norm(l, 1, blk)

    win = w_in.rearrange("(k p) n -> p k n", p=128)

    CONV_SETS = [((Fw[0], ("F", 0)), (Fs[1], ("F", 1)), (Fs[2], ("F", 2))),
                 ((Fw[3], ("F", 3)), (Fs[4], ("F", 4)), (Fs[5], ("F", 5)))]

    CONV_SET_C = ((Fw[6], ("F", 6)), (Fs[8], ("F", 8)), None)

    def conv_set(v=False):
        if v:
            i = st.setdefault("cvv", 0) % 3
            st["cvv"] += 1
            return [CONV_SET_C, CONV_SETS[0], CONV_SETS[1]][i]
        i = st.setdefault("cv", 0) % 2
        st["cv"] += 1
        return CONV_SETS[i]

    def conv_silu(c, out_ap, out_key, pt, pk, cset):
        (pre_, prek), (acc, acck), _ = cset
        cp("act", pre_[:, 3:3 + TB], pt[:, 0:TB], (pk,), (prek,))
        cp("pool", pre_[:, 0:3], ccarry[:, c, :], (("ccarry", c),), (prek,))
        cp("pool", ccarry[:, c, :], pre_[:, TB:TB + 3], (prek,), (("ccarry", c),))
        tsc("dve", acc[:, 0:TB], pre_[:, 0:TB], convw[:, c, 0:1], None, ALU.mult, ALU.bypass, (prek, ("convw",)), (acck,))
        for j in range(1, 4):
            stt(acc[:, 0:TB], pre_[:, j:j + TB], convw[:, c, j:j + 1], acc[:, 0:TB], ALU.mult, ALU.add,
                (prek, ("convw",), acck), (acck,))
        if out_ap is None:
            act(acc[:, 0:TB], acc[:, 0:TB], AF.Silu, (acck,), (acck,))
        else:
            act(out_ap, acc[:, 0:TB], AF.Silu, (acck,), (out_key,))

    def l2norm_set(cset, out_ap, out_key, scale):
        (pre_, prek), (acc, acck), (sq, sqk) = cset
        act(sq[:, 0:TB], acc[:, 0:TB], AF.Square, (acck,), (sqk,))
        pt, pk = bank()
        mm(pt[:, 0:TB], ONEf, sq[:, 0:TB], (KCST, sqk), (pk,))
        act(pre_[:, 0:TB], pt[:, 0:TB], AF.Ln, (pk, ("epst",)), (prek,), bias=epst[:, 1:2])
        act(pre_[:, 0:TB], pre_[:, 0:TB], AF.Exp, (prek,), (prek,), scale=-0.5)
        stt(out_ap, acc[:, 0:TB], scale, pre_[:, 0:TB], ALU.mult, ALU.mult, (acck, prek), (out_key,))

    def gchunk_id(g, ci):
        if ci < 2:
            return g * 2 + ci
        if ci < 4:
            return 8 + g * 2 + (ci - 2)
        return 16 + g * 4 + (ci - 4)

    def gdn_prompt_block(blk):
        P.tag = "gdn_proj"
        st["bankset"] = "proj"
        modulate(0, 0, blk, lambda k: u[:, k, :], lambda k: ("u", k))
        for n in range(4):
            pt, pk = bank()
            for k in range(KC):
                mm(pt[:, 0:32], u[:, k, n * 128:(n + 1) * 128], wba[:, k, :], (("u", k), ("wba",)), (pk,), start=(k == 0), stop=(k == KC - 1))
            act(ctmp[:, 0, :], pt[:, 0:16], AF.Exp, (pk,), (("ctmp", 0),), scale=-1.0)
            tsc("dve", ctmp[:, 0, :], ctmp[:, 0, :], 1.0, None, ALU.add, ALU.bypass, (("ctmp", 0),), (("ctmp", 0),))
            P.add("dve", lambda e, n=n: e.reciprocal(out=beta[:, n, :], in_=ctmp[:, 0, :]), (("ctmp", 0),), (("beta", n),))
            tsc("dve", nbeta[:, n, :], beta[:, n, :], -1.0, None, ALU.mult, ALU.bypass, (("beta", n),), (("nbeta", n),))
            tt("dve", ctmp[:, 1, :], pt[:, 16:32], dtb[:], ALU.add, (pk, ("dtb",), ("ctmp", 0)), (("ctmp", 1),))
            act(ctmp[:, 1, :], ctmp[:, 1, :], AF.Exp, (("ctmp", 1),), (("ctmp", 1),))
            tsc("dve", ctmp[:, 1, :], ctmp[:, 1, :], 1.0, None, ALU.add, ALU.bypass, (("ctmp", 1),), (("ctmp", 1),))
            act(ctmp[:, 1, :], ctmp[:, 1, :], AF.Ln, (("ctmp", 1),), (("ctmp", 1),))
            tt("dve", gtok[:, n, :], ctmp[:, 1, :], nexpA[:], ALU.mult, (("ctmp", 1), ("nexpA",)), (("gtok", n),))

        for g in range(4):
            tA, wkA = wload([
                (lambda t: wv(t, KC, 512)[:, :, 0:256], win[:, :, g * 256:(g + 1) * 256]),
                (lambda t: wv(t, KC, 512)[:, :, 256:512], win[:, :, 1024 + g * 256:1024 + (g + 1) * 256]),
            ])
            wA = wv(tA, KC, 512)

            def proj(wvv, wk, cl, rhs_fn, rkey_fn, N):
                pt, pk = bank()
                for k in range(KC):
                    mm(pt[:, 0:N], wvv[:, k, cl * 128:(cl + 1) * 128], rhs_fn(k), wk + (rkey_fn(k),), (pk,), start=(k == 0), stop=(k == KC - 1))
                return pt, pk

            pr_u = (lambda k: u[:, k, :], lambda k: ("u", k), TB)
            pr_s = (lambda k: us[:, k, :], lambda k: ("us", k), NS)
            for ci in range(4):
                pt, pk = proj(wA, wkA, ci, *pr_u)
                cset = conv_set()
                conv_silu(gchunk_id(g, ci), None, None, pt, pk, cset)
                if ci < 2:
                    l2norm_set(cset, Qt[:, ci, :], ("Qt", ci), 128.0 ** -0.5)
                else:
                    l2norm_set(cset, Kt[:, ci - 2, :], ("Kt", ci - 2), 1.0)
            if blk == 0:
                for ci in range(4):
                    gc = gchunk_id(g, ci)
                    pt, pk = proj(wA, wkA, ci, *pr_s)
                    cp("act", pres[:, gc, :], pt[:, 0:NS], (pk,), (("pres", gc),))
            tB, wkB = wload([(lambda t: wv(t, KC, 512), win[:, :, 2048 + g * 512:2048 + (g + 1) * 512])])
            wB = wv(tB, KC, 512)
            for ci in range(4):
                pt, pk = proj(wB, wkB, ci, *pr_u)
                conv_silu(gchunk_id(g, 4 + ci), Vt[:, ci, :], ("Vt", ci), pt, pk, conv_set(True))
            if blk == 0:
                for ci in range(4):
                    gc = gchunk_id(g, 4 + ci)
                    pt, pk = proj(wB, wkB, ci, *pr_s)
                    cp("act", pres[:, gc, :], pt[:, 0:NS], (pk,), (("pres", gc),))
            tC, wkC = wload([(lambda t: wv(t, KC, 512), win[:, :, 4096 + g * 512:4096 + (g + 1) * 512])])
            wC = wv(tC, KC, 512)
            for ci in range(4):
                pt, pk = proj(wC, wkC, ci, *pr_u)
                act(sz[:, ci, :], pt[:, 0:TB], AF.Silu, (pk,), (("sz", ci),))
            if blk == 0:
                for ci in range(4):
                    gc = 32 + g * 4 + ci
                    pt, pk = proj(wC, wkC, ci, *pr_s)
                    cp("act", pres[:, gc, :], pt[:, 0:NS], (pk,), (("pres", gc),))
            for n in range(4):
                pb, pbk = bbank()
                for kh in range(2):
                    tr(pb[:, kh * 128:(kh + 1) * 128], Kt[:, kh, n * 128:(n + 1) * 128], id_bf[:], (("Kt", kh), ("id_bf",)), (pbk,))
                for vh in range(4):
                    tr(pb[:, 256 + vh * 128:256 + (vh + 1) * 128], Vt[:, vh, n * 128:(n + 1) * 128], id_bf[:], (("Vt", vh), ("id_bf",)), (pbk,))
                cp("act", Ktok[:, n, :, :].rearrange("p a b -> p (a b)"), pb[:, 0:256], (pbk,), (("Ktok", n),))
                cp("act", Vtok[:, n, :, :].rearrange("p a b -> p (a b)"), pb[:, 256:768], (pbk,), (("Vtok", n),))
            gdn_group_chunks(g, blk)
            P.tag = "gdn_proj"
        st["bankset"] = "all"
        if blk < NBLK - 1:
            load_x_block(blk + 1)
        if blk == 0:
            mod_emit(2)
            P.tag = "sample_gdn"
            st["bankset"] = "sample"
            sample_gdn()
            st["bankset"] = "main5"
        P.tag = "wout"
        wo = w_out.rearrange("(k p) n -> p k n", p=128)
        blks2 = [blk, "s"] if blk == 1 else [blk]
        for ch in range(2):
            acc = open_acc(blks2, 4)
            first_s = True
            for vh in range(2):
                t, wk = wload([(lambda t: wv(t, 8, 512), wo[:, vh * 8:(vh + 1) * 8, ch * 512:(ch + 1) * 512])])
                wvv = wv(t, 8, 512)
                for blk2 in blks2:
                    N, ub, uk, bg, bk = blkinfo(blk2)
                    for dl in range(4):
                        for vc in range(vh * 8, vh * 8 + 8):
                            acc_mm(acc, blk2, dl, wvv[:, vc - vh * 8, dl * 128:(dl + 1) * 128], bg[:, vc, 0:N], wk + (bk(vc),),
                                   first=(vc == 0), last=(vc == 15), st_first=(blk2 == "s" and first_s))
                            if blk2 == "s":
                                first_s = False
            for blk2 in blks2:
                for dl in range(4):
                    pt, pk = acc_out(acc, blk2, dl)
                    residual(0, 0, blk2, ch * 4 + dl, pt, pk)
        if blk == 0:
            mod_emit(6)
        layer_norm(0, 0, blk)
        if blk == 1:
            layer_norm(0, 0, "s")

    def bank_pd():
        i = BANKS_PD[st.setdefault("pd", 0) % len(BANKS_PD)]
        st["pd"] += 1
        return PS[i], ("ps", i)

    def bank_rc():
        i = BANKS_RC[st.setdefault("rc", 0) % len(BANKS_RC)]
        st["rc"] += 1
        return PS[i], ("ps", i)

    def gdn_tiles(blk):
        if blk < NBLK - 1:
            return [(_T(y[:, k, (blk + 1) * TB:(blk + 2) * TB]), ("y", k, blk + 1)) for k in range(8)]
        return [(Fs[i], ("F", i)) for i in range(6)] + [(Fs[6], ("F", 6)), (Fs[8], ("F", 8))]

    def sethk(s):
        b = 9 + 4 * s
        return [(Hs[b + j], ("H", b + j)) for j in range(4)]

    def pd_stages(g, n, s, blk):
        cs = slice(n * 128, (n + 1) * 128)
        hs = slice(4 * g, 4 * g + 4)
        gcols = gtok[:, n, hs]
        gk = ("gtok", n)
        GT = gdn_tiles(blk)
        (gSL, k0), (gTRI, k1), (G1, k2), (DTm, k3), (Dm, k4), (eGr, k5) = GT[0:6]
        Vb, Vbk = GT[6 + s]
        (Tfin, Tfk), (attnT, atk), (Qd, Qdk), (kdec, kdk) = sethk(s)
        hb = 0 if s == 0 else 17
        A0, U0, P1, Q1, Ta, Tb_ = (Hs[hb + j] for j in range(6))
        hk_ = lambda j: ("H", hb + j)
        cN = ("ctmp", 3, s)
        cG = ("ctmp", 4, s)
        stages = []

        def prep():
            gb = gcols.unsqueeze(2).broadcast_to([128, 4, 128])
            tt("pool", v4(gSL), b4(SU), gb, ALU.mult, (KCST, gk), (k0,))
            tt("pool", v4(gTRI), b4(TRI), gb, ALU.mult, (KCST, gk), (k1,))
            tt("pool", v4(G1), b4(ONEf), gb, ALU.mult, (KCST, gk), (k2,))
            pA, pAk = bank_pd()
            pB, pBk = bank_pd()
            pC, pCk = bank_pd()
            for hh in range(4):
                hsl = hsl_(hh)
                mm(pA[:, hsl], gSL[:, hsl], TRI, (k0, KCST), (pAk,))
                mm(pB[:, hsl], gTRI[:, hsl], SU, (k1, KCST), (pBk,))
                mm(pC[:, hsl], G1[:, hsl], TRI, (k2, KCST), (pCk,))
            act(DTm[:], pA[:], AF.Exp, (pAk,), (k3,))
            tt("pool", v4(DTm), v4(DTm), b4(TRI), ALU.mult, (k3, KCST), (k3,))
            act(Dm[:], pB[:], AF.Exp, (pBk,), (k4,))
            tt("pool", v4(Dm), v4(Dm), b4(SU), ALU.mult, (k4, KCST), (k4,))
            act(eGr[:], pC[:], AF.Exp, (pCk,), (k5,))
            cp("pool", ctmp[:, 4, 4 * s:4 * s + 4], v4(eGr)[:, :, 127], (k5,), (cG,))
            pD, pDk = bank_pd()
            mm(pD[:, 0:4], TRI, gcols, (KCST, gk), (pDk,))
            mm(pD[:, 4:8], SU, gcols, (KCST, gk), (pDk,))
            act(ctmp[:, 2, 0:8], pD[:, 0:8], AF.Exp, (pDk,), (("ctmp", 2),))
            tt("dve", ctmp[:, 3, 4 * s:4 * s + 4], ctmp[:, 2, 0:4], nbeta[:, n, hs], ALU.mult, (("ctmp", 2), ("nbeta", n)), (cN,))
            pK, pKk = bank_pd()
            for kh in range(2):
                mm(pK[:, kh * 128:(kh + 1) * 128], Kt[:, kh, cs], Kt[:, kh, cs], (("Kt", kh),), (pKk,))
                mm(pK[:, 256 + kh * 128:256 + (kh + 1) * 128], Kt[:, kh, cs], Qt[:, kh, cs], (("Kt", kh), ("Qt", kh)), (pKk,))
            for hh in range(4):
                kh = hh // 2
                hsl = hsl_(hh)
                stt(A0[:, hsl], pK[:, kh * 128:(kh + 1) * 128], nbeta[:, n, 4 * g + hh:4 * g + hh + 1], Dm[:, hsl], ALU.mult, ALU.mult,
                    (pKk, ("nbeta", n), k4), (hk_(0),))
                tt("dve", attnT[:, hsl], pK[:, 256 + kh * 128:256 + (kh + 1) * 128], DTm[:, hsl], ALU.mult, (pKk, k3), (atk,))
                tt("pool", Qd[:, hsl], Qt[:, kh, cs], eGr[:, hsl], ALU.mult, (("Qt", kh), k5), (Qdk,))
                act(Vb[:, hsl], Vtok[:, n, hh, :], AF.Identity, (("Vtok", n), ("beta", n)), (Vbk,), scale=beta[:, n, 4 * g + hh:4 * g + hh + 1])
                act(kdec[:, hsl], Ktok[:, n, kh, :], AF.Identity, (("Ktok", n), ("ctmp", 2)), (kdk,), scale=ctmp[:, 2, 4 + hh:5 + hh])
            pb, pbk = bbank()
            for hh in range(4):
                tr(pb[:, hsl_(hh)], A0[:, hsl_(hh)], id_bf[:], (hk_(0), ("id_bf",)), (pbk,))
            cp("act", U0[:], pb[:, 0:512], (pbk,), (hk_(1),))
            tt("dve", v4(Ta), v4(U0), b4(id_bf[:]), ALU.add, (hk_(1), ("id_bf",)), (hk_(4),))
        stages.append(prep)

        state = dict(Pc=U0, Qc=A0, Pk=hk_(1), Qk=hk_(0), Pn=P1, Qn=Q1, Pnk=hk_(2), Qnk=hk_(3),
                     Tc=Ta, Tck=hk_(4), Tn=Tb_, Tnk=hk_(5))

        def level(lev):
            def f():
                z = state
                last = lev == 5
                if last:
                    z["Tn"], z["Tnk"] = Tfin, Tfk
                pq, pqk = bank_pd()
                for hh in range(4):
                    mm(pq[:, hsl_(hh)], z["Pc"][:, hsl_(hh)], z["Qc"][:, hsl_(hh)], (z["Pk"], z["Qk"]), (pqk,))
                cp("act", z["Qn"][:], pq[:], (pqk,), (z["Qnk"],))
                if not last:
                    pp, ppk = bank_pd()
                    for hh in range(4):
                        mm(pp[:, hsl_(hh)], z["Qc"][:, hsl_(hh)], z["Pc"][:, hsl_(hh)], (z["Pk"], z["Qk"]), (ppk,))
                    cp("dve", z["Pn"][:], pp[:], (ppk,), (z["Pnk"],))
                ptt, ptk = bank_pd()
                for hh in range(4):
                    mm(ptt[:, hsl_(hh)], z["Qn"][:, hsl_(hh)], z["Tc"][:, hsl_(hh)], (z["Qnk"], z["Tck"]), (ptk,))
                tt("dve", z["Tn"][:], z["Tc"][:], ptt[:], ALU.add, (z["Tck"], ptk), (z["Tnk"],))
                (z["Pc"], z["Qc"], z["Pk"], z["Qk"], z["Pn"], z["Qn"], z["Pnk"], z["Qnk"]) = (
                    z["Pn"], z["Qn"], z["Pnk"], z["Qnk"], z["Pc"], z["Qc"], z["Pk"], z["Qk"])
                z["Tc"], z["Tck"], z["Tn"], z["Tnk"] = z["Tn"], z["Tnk"], z["Tc"], z["Tck"]
            return f
        for lev in range(6):
            stages.append(level(lev))
        return stages

    def rc_stages(g, n, s, blk):
        cs = slice(n * 128, (n + 1) * 128)
        Vb, Vbk = gdn_tiles(blk)[6 + s]
        (Tfin, Tfk), (attnT, atk), (Qd, Qdk), (kdec, kdk) = sethk(s)
        W, vnew, osq, Ft = Hs[6], Hs[7], Hs[8], Fs[7]
        cN = ("ctmp", 3, s)
        cG = ("ctmp", 4, s)
        hold = {}

        def r1():
            pks, pksk = bank_rc()
            for hh in range(4):
                h = 4 * g + hh
                mm(pks[:, hsl_(hh)], Kt[:, hh // 2, cs], Sb[:, h, :], (("Kt", hh // 2), ("Sb", h)), (pksk,))
            for hh in range(4):
                stt(W[:, hsl_(hh)], pks[:, hsl_(hh)], ctmp[:, 3, 4 * s + hh:4 * s + hh + 1], Vb[:, hsl_(hh)], ALU.mult, ALU.add,
                    (pksk, cN, Vbk), (("H", 6),))

        def r2():
            pvn, pvnk = bank_rc()
            for hh in range(4):
                mm(pvn[:, hsl_(hh)], Tfin[:, hsl_(hh)], W[:, hsl_(hh)], (Tfk, ("H", 6)), (pvnk,))
            cp("act", vnew[:], pvn[:], (pvnk,), (("H", 7),))

        def r3():
            po, pok = bank_rc()
            pds, pdsk = bank_rc()
            hold["po"] = (po, pok)
            for hh in range(4):
                h = 4 * g + hh
                mm(po[:, hsl_(hh)], Sb[:, h, :], Qd[:, hsl_(hh)], (("Sb", h), Qdk), (pok,), start=True, stop=False)
                mm(po[:, hsl_(hh)], vnew[:, hsl_(hh)], attnT[:, hsl_(hh)], (("H", 7), atk), (pok,), start=False, stop=True)
                mm(pds[:, hsl_(hh)], kdec[:, hsl_(hh)], vnew[:, hsl_(hh)], (kdk, ("H", 7)), (pdsk,))
            for hh in range(4):
                h = 4 * g + hh
                stt(S[:, h, :], S[:, h, :], ctmp[:, 4, 4 * s + hh:4 * s + hh + 1], pds[:, hsl_(hh)], ALU.mult, ALU.add,
                    (("S", h), cG, pdsk), (("S", h),))
                cp("pool", Sb[:, h, :], S[:, h, :], (("S", h),), (("Sb", h),))

        def r4a():
            po, pok = hold["po"]
            act(osq[:], po[:], AF.Square, (pok,), (("H", 8),))
            pss, pssk = bank_rc()
            hold["pss"] = (pss, pssk)
            mm(pss[:], one_bf[:], osq[:], (("one_bf",), ("H", 8)), (pssk,))
            tsc("dve", Ft[:], pss[:], 1.0 / 128.0, NORM_EPS, ALU.mult, ALU.add, (pssk,), (("F", 7),))
            act(Ft[:], Ft[:], AF.Ln, (("F", 7),), (("F", 7),))
            act(Ft[:], Ft[:], AF.Exp, (("F", 7),), (("F", 7),), scale=-0.5)

        def r4b():
            po, pok = hold["po"]
            tt("dve", Ft[:], po[:], Ft[:], ALU.mult, (pok, ("F", 7)), (("F", 7),))
            stt(big[:, 4 * g:4 * g + 4, cs], v4(Ft), normw[:, 0:1], sz[:, :, cs], ALU.mult, ALU.mult,
                (("F", 7), ("normw",)) + tuple(("sz", i) for i in range(4)), tuple(("big", 4 * g + i) for i in range(4)))

        return [r1, r2, r3, r4a, r4b]

    def gdn_group_chunks(g, blk):
        P.tag = "gdn_chunk"
        for f in pd_stages(g, 0, 0, blk):
            f()
        for n in range(4):
            rc = rc_stages(g, n, n % 2, blk)
            pd = pd_stages(g, n + 1, (n + 1) % 2, blk) if n < 3 else []
            order = []
            ri, pi = 0, 0
            while ri < len(rc) or pi < len(pd):
                if pi < len(pd):
                    order.append(pd[pi]); pi += 1
                if ri < len(rc):
                    order.append(rc[ri]); ri += 1
            for f in order:
                f()

    def sample_gdn():
        NSL = 8
        Sst = [_T(y[:, i, 3 * TB:4 * TB].rearrange("p (a b) -> p a b", a=4)) for i in range(NSL)]
        sstk = lambda sl: ("y", sl, 3)
        preskeys = tuple(("pres", c) for c in range(32))
        dma("sp", sconv[:].rearrange("p a b c -> p (a b c)"), sconvT, "sc", (), K_sconv)
        pt, pk = bank()
        for k in range(KC):
            mm(pt[0:NS, 0:32], us[:, k, :], wba[:, k, :], (("us", k), ("wba",)), (pk,), start=(k == 0), stop=(k == KC - 1))
        act(bas[:, 0:16], pt[0:NS, 0:16], AF.Exp, (pk,), K_bas, scale=-1.0)
        tsc("dve", bas[:, 0:16], bas[:, 0:16], 1.0, None, ALU.add, ALU.bypass, K_bas, K_bas)
        P.add("dve", lambda e: e.reciprocal(out=bas[:, 0:16], in_=bas[:, 0:16]), K_bas, K_bas)
        tt("dve", bas[:, 16:32], pt[0:NS, 16:32], dtb[0:NS, :], ALU.add, (pk, ("dtb",)) + K_bas, K_bas)
        act(bas[:, 16:32], bas[:, 16:32], AF.Exp, K_bas, K_bas)
        tsc("dve", bas[:, 16:32], bas[:, 16:32], 1.0, None, ALU.add, ALU.bypass, K_bas, K_bas)
        act(bas[:, 16:32], bas[:, 16:32], AF.Ln, K_bas, K_bas)
        tt("dve", bas[:, 16:32], bas[:, 16:32], nexpA[0:NS, :], ALU.mult, K_bas + (("nexpA",),), K_bas)
        act(bas[:, 16:32], bas[:, 16:32], AF.Exp, K_bas, K_bas)
        I16 = cst[0:NS, C_ID:C_ID + NS].unsqueeze(2).broadcast_to([NS, NS, NH])
        for qi in range(2):
            tt("dve", sexp[:, qi, :, :], bas[:, qi * 16:qi * 16 + 16].unsqueeze(1).broadcast_to([NS, NS, NH]), I16, ALU.mult, K_bas + (KCST,), K_sexp)
        prow, prk = bank()
        for qi in range(2):
            mm(prow[:, qi * 256:(qi + 1) * 256], cst[0:NS, C_ONE:C_ONE + 128], sexp[:, qi, :, :].rearrange("p a b -> p (a b)"),
               (KCST,) + K_sexp, (prk,))
        cp("act", sm[:, 0, :], prow[:, 0:256], (prk,), (("u", 0),))
        cp("act", sm[:, 1, :], prow[:, 256:512], (prk,), (("u", 1),))

        def cw(j):
            return convw[:, :, j].unsqueeze(2).broadcast_to([128, 32, NS])
        ctmp2 = sm[:, 2:4, :].rearrange("p a b -> p (a b)").rearrange("p (c n) -> p c n", c=32)
        ck = (("u", 2), ("u", 3))
        tt("dve", qkvs[:], pres[:, 0:32, :], cw(3), ALU.mult, preskeys + (("convw",),), K_qkvs)
        for j in range(3):
            tt("dve", ctmp2, sconv[:, :, j, :], cw(j), ALU.mult, K_sconv + (("convw",),), ck)
            tt("dve", qkvs[:], qkvs[:], ctmp2, ALU.add, K_qkvs + ck, K_qkvs)
        act(qkvs[:], qkvs[:], AF.Silu, K_qkvs, K_qkvs)
        act(szs[:], pres[:, 32:48, :], AF.Silu, tuple(("pres", c) for c in range(32, 48)), K_szs)
        dma("sp", sconv_new_o, pres[:, 0:32, :].rearrange("p a b -> p (a b)"), "out", preskeys, ())
        qk2 = qkvs[:, 0:16, :].rearrange("p a b -> p (a b)")
        act(sm[:, 2, :], qk2, AF.Square, K_qkvs, (("u", 2),))
        pt2, pk2 = bank()
        mm(pt2[:, 0:256], ONEf, sm[:, 2, :], (KCST, ("u", 2)), (pk2,))
        tsc("dve", sm[:, 3, :], pt2[:, 0:256], NORM_EPS, None, ALU.add, ALU.bypass, (pk2,), (("u", 3),))
        act(sm[:, 3, :], sm[:, 3, :], AF.Ln, (("u", 3),), (("u", 3),))
        act(sm[:, 3, :], sm[:, 3, :], AF.Exp, (("u", 3),), (("u", 3),), scale=-0.5)
        tt("dve", qk2, qk2, sm[:, 3, :], ALU.mult, K_qkvs + (("u", 3),), K_qkvs)
        tsc("dve", qkvs[:, 0:8, :], qkvs[:, 0:8, :], 128.0 ** -0.5, None, ALU.mult, ALU.bypass, K_qkvs, K_qkvs)
        for r in range(2):
            src_k = qkvs[:, 8:16, :].rearrange("p k b -> p b k")
            src_q = qkvs[:, 0:8, :].rearrange("p k b -> p b k")
            dstk = kqs[:, :, :, 0].rearrange("p b (k r) -> p b k r", r=2)[:, :, :, r]
            dstq = kqs[:, :, :, 1].rearrange("p b (k r) -> p b k r", r=2)[:, :, :, r]
            cp("dve", dstk, src_k, K_qkvs, K_kqs)
            cp("dve", dstq, src_q, K_qkvs, K_kqs)
        tt("dve", sm[:, 4, 0:128], qkvs[:, 0:8, :].rearrange("p a b -> p (a b)"), qkvs[:, 8:16, :].rearrange("p a b -> p (a b)"), ALU.mult,
           K_qkvs, (("u", 4),))
        pqk_t, pqk_k = bank()
        mm(pqk_t[:, 0:128], ONEf, sm[:, 4, 0:128], (KCST, ("u", 4)), (pqk_k,))
        cp("act", sm[:, 5, 0:128], pqk_t[:, 0:128], (pqk_k,), (("u", 5),))
        ptk, ptkk = bank()
        ptk2, ptkk2 = bank()
        for kh in range(8):
            dst = (ptk if kh < 4 else ptk2)
            dk_ = (ptkk if kh < 4 else ptkk2)
            P.add("pe", lambda e, dst=dst, kh=kh: e.transpose(dst[0:NS, (kh % 4) * 128:(kh % 4 + 1) * 128], qkvs[:, 8 + kh, :], IDf),
                  K_qkvs + (KCST,), (dk_,))
        cp("act", ktoks[:, 0:4, :].rearrange("p a b -> p (a b)"), ptk[0:NS, :], (ptkk,), K_ktoks)
        cp("act", ktoks[:, 4:8, :].rearrange("p a b -> p (a b)"), ptk2[0:NS, :], (ptkk2,), K_ktoks)
        pS_t, pS_k = bank()
        it = 0
        for b in range(NS):
            for q4 in range(4):
                sl = it % NSL
                it += 1
                dma("sp", Sst[sl][:], S_in[b, q4 * 4:(q4 + 1) * 4].rearrange("h k v -> k h v"), f"sin{sl}", (), (sstk(sl),))
                for hh in range(4):
                    h = q4 * 4 + hh
                    mm(pS_t[:, (b * NH + h) * 2:(b * NH + h) * 2 + 2], Sst[sl][:, hh, :], kqs[:, b, h, :], (sstk(sl),) + K_kqs, (pS_k,))
        pS4 = pS_t[:].rearrange("p (b h t) -> p b h t", b=NS, h=NH)
        Sk = pS4[:, :, :, 0]
        Sq = pS4[:, :, :, 1]

        def bh(t2d):
            return t2d.rearrange("p (b h) -> p b h", b=NS)
        vS = qkvs[:, 16:32, :].rearrange("p h b -> p b h")
        tt("dve", bh(sm[:, 6, :]), Sk, bh(sm[:, 1, :]), ALU.mult, (pS_k, ("u", 1)), (("u", 6),))
        tt("dve", bh(sm[:, 6, :]), vS, bh(sm[:, 6, :]), ALU.subtract, K_qkvs + (("u", 6),), (("u", 6),))
        tt("dve", bh(sm[:, 6, :]), bh(sm[:, 6, :]), bh(sm[:, 0, :]), ALU.mult, (("u", 6), ("u", 0)), (("u", 6),))
        tt("dve", bh(sm[:, 7, :]), Sq, bh(sm[:, 1, :]), ALU.mult, (pS_k, ("u", 1)), (("u", 7),))
        qkb = sm[:, 5, 0:128].rearrange("p (k b) -> p b k", k=8)
        for r in range(2):
            vn_r = bh(sm[:, 6, :]).rearrange("p b (k r) -> p b k r", r=2)[:, :, :, r]
            o8_r = bh(sm[:, 2, :]).rearrange("p b (k r) -> p b k r", r=2)[:, :, :, r]
            tt("dve", o8_r, vn_r, qkb, ALU.mult, (("u", 6), ("u", 5)), (("u", 2),))
        tt("dve", sm[:, 7, :], sm[:, 7, :], sm[:, 2, :], ALU.add, (("u", 7), ("u", 2)), (("u", 7),))
        act(sm[:, 3, :], sm[:, 7, :], AF.Square, (("u", 7),), (("u", 3),))
        pn_t, pn_k = bank()
        mm(pn_t[:, 0:256], ONEf, sm[:, 3, :], (KCST, ("u", 3)), (pn_k,))
        tsc("dve", sm[:, 4, :], pn_t[:, 0:256], 1.0 / 128.0, NORM_EPS, ALU.mult, ALU.add, (pn_k,), (("u", 4),))
        act(sm[:, 4, :], sm[:, 4, :], AF.Ln, (("u", 4),), (("u", 4),))
        act(sm[:, 4, :], sm[:, 4, :], AF.Exp, (("u", 4),), (("u", 4),), scale=-0.5)
        tt("dve", sm[:, 7, :], sm[:, 7, :], sm[:, 4, :], ALU.mult, (("u", 7), ("u", 4)), (("u", 7),))
        stt(bigs[:, 0:16, :].rearrange("p h b -> p b h"), bh(sm[:, 7, :]), normw[:, 0:1], szs[:].rearrange("p h b -> p b h"), ALU.mult, ALU.mult,
            (("u", 7), ("normw",)) + K_szs, tuple(("bigs", c) for c in range(16)))
        vt_t = [_T(y[0:NS, i, 2 * TB:3 * TB].rearrange("p (a b) -> p a b", a=4)) for i in range(4)]
        vm_t = [_T(y[0:NS, 4 + i, 2 * TB:3 * TB].rearrange("p (a b) -> p a b", a=4)) for i in range(4)]
        vtk = lambda i: ("y", i, 2)
        vmk = lambda i: ("y", 4 + i, 2)
        for q4 in range(4):
            pv, pvk = bank()
            for hh in range(4):
                h = q4 * 4 + hh
                P.add("pe", lambda e, pv=pv, hh=hh, h=h: e.transpose(pv[0:NS, hh * 128:(hh + 1) * 128], bh(sm[:, 6, :])[:, :, h], IDf),
                      (("u", 6), KCST), (pvk,))
            cp("act", vt_t[q4][:].rearrange("p a b -> p (a b)"), pv[0:NS, :], (pvk,), (vtk(q4),))
        for b in range(NS):
            for q4_ in range(4):
                tsc("dve", vm_t[q4_][:].rearrange("p a b -> p (a b)"), vt_t[q4_][:].rearrange("p a b -> p (a b)"), cst[0:NS, C_ID + b:C_ID + b + 1], None,
                    ALU.mult, ALU.bypass, (vtk(q4_), KCST), (vmk(q4_),))
            for q4 in range(4):
                sl = it % NSL
                it += 1
                dma("sp", Sst[sl][:], S_in[b, q4 * 4:(q4 + 1) * 4].rearrange("h k v -> k h v"), f"sin{sl}", (), (sstk(sl),))
                pu_t, pu_k = bank()
                for hh in range(4):
                    h = q4 * 4 + hh
                    mm(pu_t[:, hsl_(hh)], ktoks[:, h // 2, :], vm_t[q4][:, hh, :], K_ktoks + (vmk(q4),), (pu_k,))
                for hh in range(4):
                    h = q4 * 4 + hh
                    stt(Sst[sl][:, hh, :], Sst[sl][:, hh, :], sm[:, 1, b * NH + h:b * NH + h + 1], pu_t[:, hsl_(hh)], ALU.mult, ALU.add,
                        (sstk(sl), ("u", 1), pu_k), (sstk(sl),))
                dma("sp", sS_o[b, q4 * 4:(q4 + 1) * 4].rearrange("h k v -> k h v"), Sst[sl][:], f"sout{sl}", (sstk(sl),), ())

    def pool_weights():
        t, wk = wload([(lambda t: t[:, 0:2048].rearrange("p (g k c) -> p g k c", g=4, k=2), pool_w.rearrange("g (k p) c -> p g k c", p=128))])
        return t[:, 0:2048].rearrange("p (g k c) -> p g k c", g=4, k=2), wk

    def pool_prompt_block(blk, pw, pwk):
        P.tag = "pool"
        L = 15 + TB
        PSETS = [((Fw[3 * i], ("F", 3 * i)), [(Fw[3 * i + 1], ("F", 3 * i + 1)), (Fw[3 * i + 2], ("F", 3 * i + 2))]) for i in range(3)]
        for k in range(KC):
            (u1_, u1k), bufs = PSETS[k % 3]
            gi = k // 2
            w = 2 << gi
            mk = (("mod", 1, 1, k), ("mod", 1, 0, k))
            cp("pool", u1_[:, 0:15], pcarry[:, k, :], (("pcarry", k),), (u1k,))
            act(u1_[:, 15:15 + TB], yap(k, blk), AF.Identity, (ykey(k, blk),) + mk, (u1k,),
                bias=mod_ap(1, 0, k, False), scale=mod_ap(1, 1, k, False))
            cp("pool", pcarry[:, k, :], u1_[:, TB:TB + 15], (u1k,), (("pcarry", k),))
            src_, srck, off, m, bi = u1_, u1k, 0, 1, 0
            while m < w:
                dst, dstk = bufs[bi % 2]
                n_el = L - off - m
                tt("dve", dst[:, 0:n_el], src_[:, m:m + n_el], src_[:, 0:n_el], ALU.add, (srck,), (dstk,))
                src_, srck = dst, dstk
                off += m
                m *= 2
                bi += 1
            s0 = 15 - off
            mean, meank = bufs[bi % 2]
            tsc("dve", mean[:, 0:TB], src_[:, s0:s0 + TB], 1.0 / w, None, ALU.mult, ALU.bypass, (srck,), (meank,))
            if blk == 0:
                rc = cst[:, C_RC + gi * 16:C_RC + gi * 16 + 16]
                tt("dve", mean[:, 0:16], src_[:, s0:s0 + 16], rc, ALU.mult, (srck, KCST, meank), (meank,))
            tt("dve", u[:, k, :], mean[:, 0:TB], u1_[:, 15:15 + TB], ALU.subtract, (meank, u1k), (("u", k),))
        for dc in range(KC):
            gi = dc // 2
            el = dc % 2
            pt, pk = bank()
            for k2 in range(2):
                mm(pt[:, 0:TB], pw[:, gi, k2, el * 128:(el + 1) * 128], u[:, gi * 2 + k2, :], pwk + (("u", gi * 2 + k2),), (pk,), start=(k2 == 0), stop=(k2 == 1))
            residual(1, 0, blk, dc, pt, pk, pool=True)
        layer_norm(1, 0, blk)

    def pool_sample(pw, pwk):
        dma("sp", spool[:].rearrange("p a b c -> p (a b c)"), spoolT, "sp_", (), KF03)
        modulate(1, 0, "s", lambda k: u1s[:, k, :], lambda k: ("F", 5))
        u1keys = (("F", 5),)
        dma("sp", spool_new_o, u1s[:].rearrange("p a b -> p (a b)"), "out", u1keys, ())
        for gi in range(4):
            w = 2 << gi
            ks = slice(gi * 2, gi * 2 + 2)
            P.add("dve", lambda e, ks=ks, w=w: e.reduce_sum(out=pls[:, ks, :], in_=spool[:, ks, :, 15 - (w - 1):15], axis=mybir.AxisListType.X),
                  KF03, (("F", 6),))
            tt("dve", pls[:, ks, :], pls[:, ks, :], u1s[:, ks, :], ALU.add, (("F", 6),) + u1keys, (("F", 6),))
            stt(us[:, ks, :], pls[:, ks, :], 1.0 / w, u1s[:, ks, :], ALU.mult, ALU.subtract, (("F", 6),) + u1keys, (("us", gi * 2), ("us", gi * 2 + 1)))
        for dc in range(KC):
            gi = dc // 2
            el = dc % 2
            pt, pk = bank()
            for k2 in range(2):
                mm(pt[:, 0:NS], pw[:, gi, k2, el * 128:(el + 1) * 128], us[:, gi * 2 + k2, :], pwk + (("us", gi * 2 + k2),), (pk,), start=(k2 == 0), stop=(k2 == 1))
            residual(1, 0, "s", dc, pt, pk, pool=True)
        layer_norm(1, 0, "s")

    modulate(0, 0, "s", lambda k: us[:, k, :], lambda k: ("us", k))
    for blk in range(NBLK):
        gdn_prompt_block(blk)
        ffn(0, [blk, "s"] if blk == 2 else [blk])
        if blk == 0:
            mod_emit(12)
    for k in range(KC):
        tsc("dve", cgate[:, k, :], mod[:, 1, 2, k, :], pscale[:, k:k + 1], None, ALU.mult, ALU.bypass, (("mod", 1, 2, k), ("pscale",)), (("cgate",),))
    for blk in range(NBLK):
        pw, pwk = pool_weights()
        pool_prompt_block(blk, pw, pwk)
        if blk == 0:
            pool_sample(pw, pwk)
        ffn(1, [blk, "s"] if blk == 1 else [blk])
    for k in range(KC):
        dma("sp", yT_o[k * 128:(k + 1) * 128, :], y[:, k, :], "out", tuple(("y", k, b) for b in range(NBLK)), ())
    dma("sp", ysT_o.rearrange("(k p) n -> p k n", p=128), ys[:], "out", tuple(("ys", k) for k in range(KC)), ())
    dma("sp", pS_o.rearrange("h k v -> k h v"), S[:], "out", tuple(("S", h) for h in range(NH)), ())
    dma("sp", pconv_o, ccarry[:].rearrange("p a b -> p (a b)"), "out", tuple(("ccarry", c) for c in range(32)), ())
    dma("sp", ppool_o, pcarry[:].rearrange("p a b -> p (a b)"), "out", tuple(("pcarry", k) for k in range(KC)), ())

    P.emit(sems, dsem)
    return nc, es, P


_CACHE = {}


def _consts():
    c = np.zeros((128, NCONST), np.float32)
    i = np.arange(128)
    c[:, C_ID:C_ID + 128] = np.eye(128, dtype=np.float32)
    c[:, C_TRI:C_TRI + 128] = (i[:, None] <= i[None, :]).astype(np.float32)
    c[:, C_SU:C_SU + 128] = (i[:, None] > i[None, :]).astype(np.float32)
    c[:, C_ONE:C_ONE + 128] = 1.0
    for gi in range(4):
        w = 2 << gi
        c[:, C_RC + gi * 16:C_RC + gi * 16 + 16] = (1.0 / np.minimum(np.arange(16) + 1, w)).astype(np.float32)[None, :]
    return c


def _pm(a):
    n = a.shape[0] // 128
    b = a.reshape((n, 128) + a.shape[1:])
    b = np.moveaxis(b, 1, 0)
    return np.ascontiguousarray(b.reshape(128, -1))


def kernel(x_prompt, x_sample, c_prompt, c_sample, state_gdn_S, state_gdn_conv, state_pool,
           ada_w, ada_b, ln_g, ln_b, gdn_w_in, gdn_conv_w, gdn_A_log, gdn_dt_bias, gdn_norm_w,
           gdn_w_out, pool_w, pool_scale, ffn_w_up, ffn_w_down):
    f = lambda a: np.ascontiguousarray(np.asarray(a, dtype=np.float32))
    x_prompt, x_sample, c_prompt, c_sample = f(x_prompt), f(x_sample), f(c_prompt), f(c_sample)
    state_gdn_S, state_gdn_conv, state_pool = f(state_gdn_S), f(state_gdn_conv), f(state_pool)
    if "nc" not in _CACHE:
        _CACHE["nc"] = build_program()
    nc = _CACHE["nc"][0]
    shared = {
        "ada_w": f(ada_w),
        "ada_bT": np.ascontiguousarray(f(ada_b).reshape(2, 48, 128).transpose(2, 0, 1).reshape(128, 96)),
        "ln_gT": np.ascontiguousarray(f(ln_g).reshape(2, 2, 8, 128).transpose(3, 0, 1, 2).reshape(128, 32)),
        "ln_bT": np.ascontiguousarray(f(ln_b).reshape(2, 2, 8, 128).transpose(3, 0, 1, 2).reshape(128, 32)),
        "w_in": f(gdn_w_in)[0],
        "convwT": np.ascontiguousarray(f(gdn_conv_w)[0].reshape(4, 32, 128).transpose(2, 1, 0).reshape(128, 128)),
        "alog_b": np.ascontiguousarray(np.broadcast_to(f(gdn_A_log)[0][None, :], (128, NH))),
        "dtb_b": np.ascontiguousarray(np.broadcast_to(f(gdn_dt_bias)[0][None, :], (128, NH))),
        "normw_c": np.ascontiguousarray(f(gdn_norm_w)[0].reshape(128, 1)),
        "w_out": f(gdn_w_out)[0],
        "pool_w": f(pool_w)[0],
        "pscaleT": np.ascontiguousarray(f(pool_scale)[0].reshape(8, 128).T),
        "w_up": f(ffn_w_up),
        "w_down": f(ffn_w_down),
        "consts": _consts(),
    }
    in_maps = []
    for i in range(NCORES):
        sl = slice(i * NS, (i + 1) * NS)
        cT = np.concatenate([c_prompt[i][:, None], c_sample[sl].T], axis=1)
        sc = state_gdn_conv[0, sl]
        spl = state_pool[0, sl]
        m = dict(shared)
        m.update({
            "xT": np.ascontiguousarray(x_prompt[i].T),
            "xsT": np.ascontiguousarray(x_sample[sl, 0, :].T),
            "cT": np.ascontiguousarray(cT),
            "S_in": np.ascontiguousarray(state_gdn_S[0, sl]),
            "sconv_nat": np.ascontiguousarray(sc),
            "sconvT": np.ascontiguousarray(sc.reshape(NS, 3, 32, 128).transpose(3, 2, 1, 0).reshape(128, 32 * 3 * NS)),
            "spool_nat": np.ascontiguousarray(spl),
            "spoolT": np.ascontiguousarray(spl.reshape(NS, 15, 8, 128).transpose(3, 2, 0, 1).reshape(128, 8 * NS * 15)),
        })
        in_maps.append(m)
    res = run_bass_kernel_spmd(nc, in_maps, core_ids=list(range(NCORES)))
    R = res.results
    B, DEC = NCORES, NCORES * NS
    y_prompt = np.zeros((B, SEQ, D), np.float32)
    y_sample = np.zeros((DEC, 1, D), np.float32)
    p_S = np.zeros((1, B, NH, 128, 128), np.float32)
    p_conv = np.zeros((1, B, 3, 4096), np.float32)
    p_pool = np.zeros((1, B, 15, D), np.float32)
    s_S = np.zeros((1, DEC, NH, 128, 128), np.float32)
    s_conv = np.zeros((1, DEC, 3, 4096), np.float32)
    s_pool = np.zeros((1, DEC, 15, D), np.float32)
    for i in range(NCORES):
        r = R[i]
        sl = slice(i * NS, (i + 1) * NS)
        g = lambda k: np.asarray(r[k], dtype=np.float32)
        y_prompt[i] = g("yT").T
        y_sample[sl, 0, :] = g("ysT").T
        p_S[0, i] = g("pS")
        p_conv[0, i] = g("pconvT").reshape(128, 32, 3).transpose(2, 1, 0).reshape(3, 4096)
        p_pool[0, i] = g("ppoolT").reshape(128, 8, 15).transpose(2, 1, 0).reshape(15, D)
        s_S[0, sl] = g("sS")
        s_conv[0, sl, 0:2] = g("sconv_shift")
        s_conv[0, sl, 2] = g("sconv_newT").reshape(128, 32, NS).transpose(2, 1, 0).reshape(NS, 4096)
        s_pool[0, sl, 0:14] = g("spool_shift")
        s_pool[0, sl, 14] = g("spool_newT").reshape(128, 8, NS).transpose(2, 1, 0).reshape(NS, D)
    return (y_prompt, y_sample, p_S, p_conv, p_pool, s_S, s_conv, s_pool)
```
